# Optimizing a Trainium2 kernel written in Bass

```python
import jax, jax.numpy as jnp
from jax import lax
import numpy as np

D_MODEL = 1024
BATCH = 8
SEQ = 2048
DEPTH = 4

CTX_LEN = 256
GRID_W = 64
Q_BLOCK = 128
ROPE_THETA = 10000.0
EPS = 1e-6

F_GROUPS = 4
F_GROUP_DIM = 128
F_WIDTH = F_GROUPS * F_GROUP_DIM
MLA_HEADS = 8
MLA_Q_RANK = 256
MLA_KV_RANK = 256
MLA_NOPE_DIM = 64
MLA_ROPE_DIM = 32
MLA_QK_DIM = MLA_NOPE_DIM + MLA_ROPE_DIM
MLA_V_DIM = 64
MLA_WIDTH = MLA_HEADS * MLA_V_DIM
GQA_HEADS = 8
GQA_KV_HEADS = 2
GQA_GROUP = GQA_HEADS // GQA_KV_HEADS
GQA_HEAD_DIM = 64
GQA_WIDTH = GQA_HEADS * GQA_HEAD_DIM
GQA_KV_WIDTH = GQA_KV_HEADS * GQA_HEAD_DIM

N_BRANCHES = 3
D_FF = 4 * D_MODEL
N_MOD = 6

KV_COLS = MLA_KV_RANK + MLA_ROPE_DIM + 2 * GQA_KV_WIDTH
KV_SPLITS = (MLA_KV_RANK, MLA_KV_RANK + MLA_ROPE_DIM, MLA_KV_RANK + MLA_ROPE_DIM + GQA_KV_WIDTH)
REST_SPLITS = (F_WIDTH, F_WIDTH + MLA_Q_RANK, F_WIDTH + MLA_Q_RANK + GQA_WIDTH)
IN_COLS = KV_COLS + F_WIDTH + MLA_Q_RANK + GQA_WIDTH + N_BRANCHES * D_MODEL

kernel_name = "hybrid_fourier_mla_gqa_dit_prefix"


def layer_norm(x, g=None, b=None):
    xf = x.astype(jnp.float32)
    mu = xf.mean(-1, keepdims=True)
    var = jnp.square(xf - mu).mean(-1, keepdims=True)
    y = (xf - mu) * lax.rsqrt(var + EPS)
    if g is not None:
        y = y * g.astype(jnp.float32) + b.astype(jnp.float32)
    return y.astype(x.dtype)


def rms_norm(x, g):
    xf = x.astype(jnp.float32)
    y = xf * lax.rsqrt(jnp.square(xf).mean(-1, keepdims=True) + EPS)
    return (y * g.astype(jnp.float32)).astype(x.dtype)


def modulate(x, shift, scale):
    return layer_norm(x) * (1 + scale) + shift


def rope_angles(rows, cols, d_rot):
    n = d_rot // 4
    freqs = ROPE_THETA ** (-jnp.arange(n, dtype=jnp.float32) / n)
    ang = jnp.concatenate([rows[:, None] * freqs, cols[:, None] * freqs], axis=-1)
    return jnp.cos(ang), jnp.sin(ang)


def apply_rope(x, cos, sin):
    x1, x2 = jnp.split(x, 2, axis=-1)
    cos = cos[:, None, :].astype(x.dtype)
    sin = sin[:, None, :].astype(x.dtype)
    return jnp.concatenate([x1 * cos - x2 * sin, x1 * sin + x2 * cos], axis=-1)


def attend(q, k, v):
    B, S, Hk, G, dk = q.shape
    nb = S // Q_BLOCK
    qb = jnp.moveaxis(q.reshape(B, nb, Q_BLOCK, Hk, G, dk), 1, 0)

    def block(qi):
        s = jnp.einsum("bqkgd,btkd->bkgqt", qi, k).astype(jnp.float32)
        w = jax.nn.softmax(s, axis=-1).astype(v.dtype)
        return jnp.einsum("bkgqt,btkd->bqkgd", w, v)

    o = lax.map(block, qb)
    return jnp.moveaxis(o, 0, 1).reshape(B, S, Hk * G * v.shape[-1])


def fourier_mix(u):
    B, S, _ = u.shape
    ug = u.reshape(B, S, F_GROUPS, F_GROUP_DIM).astype(jnp.float32)
    f = jnp.fft.fft2(ug, axes=(1, 3), norm="ortho").real
    return f.reshape(B, S, F_WIDTH).astype(u.dtype)


def kv_parts(p, lw, rope):
    B, T, _ = p.shape
    c_kv, k_r, k_g, v_g = jnp.split(p[..., :KV_COLS], KV_SPLITS, axis=-1)
    c_kv = rms_norm(c_kv, lw["mla_kv_g"])
    k_nope = (c_kv @ lw["w_uk"]).reshape(B, T, MLA_HEADS, MLA_NOPE_DIM)
    v_m = (c_kv @ lw["w_uv"]).reshape(B, T, MLA_HEADS, MLA_V_DIM)
    k_r = k_r[:, :, None, :]
    k_g = rms_norm(k_g.reshape(B, T, GQA_KV_HEADS, GQA_HEAD_DIM), lw["gqa_k_g"])
    v_g = v_g.reshape(B, T, GQA_KV_HEADS, GQA_HEAD_DIM)
    if rope is not None:
        k_r = apply_rope(k_r, *rope[0])
        k_g = apply_rope(k_g, *rope[1])
    k_m = jnp.concatenate([k_nope, jnp.broadcast_to(k_r, (B, T, MLA_HEADS, MLA_ROPE_DIM))], axis=-1)
    return (k_m, v_m, k_g, v_g)


def mixer(p, kv, lw, rope):
    B, S, _ = p.shape
    k_m, v_m, k_g, v_g = kv
    f_in, c_q, q_g, gate_logits = jnp.split(p[..., KV_COLS:], REST_SPLITS, axis=-1)
    y_f = fourier_mix(f_in) @ lw["w_fo"]
    q_m = (rms_norm(c_q, lw["mla_q_g"]) @ lw["w_uq"]).reshape(B, S, MLA_HEADS, MLA_QK_DIM)
    q_nope, q_rope = jnp.split(q_m, [MLA_NOPE_DIM], axis=-1)
    q_g = rms_norm(q_g.reshape(B, S, GQA_HEADS, GQA_HEAD_DIM), lw["gqa_q_g"])
    if rope is not None:
        q_rope = apply_rope(q_rope, *rope[0])
        q_g = apply_rope(q_g, *rope[1])
    q_m = jnp.concatenate([q_nope, q_rope], axis=-1)[:, :, :, None, :] * (MLA_QK_DIM ** -0.5)
    y_m = attend(q_m, k_m, v_m) @ lw["w_mo"]
    q_g = q_g.reshape(B, S, GQA_KV_HEADS, GQA_GROUP, GQA_HEAD_DIM) * (GQA_HEAD_DIM ** -0.5)
    y_g = attend(q_g, k_g, v_g) @ lw["w_go"]
    g_f, g_m, g_g = jnp.split(jax.nn.sigmoid(gate_logits + lw["b_gate"]), N_BRANCHES, axis=-1)
    return (g_f * y_f + g_m * y_m + g_g * y_g) @ lw["w_o"]


def sq_relu_mlp(h, lw):
    return jnp.square(jax.nn.relu(h @ lw["w1"])) @ lw["w2"]


def setup_inputs(seed: int = 0) -> dict:
    key = jax.random.key(seed)
    ks = iter(jax.random.split(key, 32))

    def nrm(shape, scale):
        return jax.random.normal(next(ks), shape, jnp.float32) * scale

    def gain(shape):
        return 1.0 + nrm(shape, 0.02)

    L, D = DEPTH, D_MODEL
    beta = (8.0 * DEPTH) ** -0.25
    return {
        "x": nrm((BATCH, SEQ, D), 1.0),
        "c": nrm((BATCH, D), 1.0),
        "ctx": nrm((BATCH, CTX_LEN, D), 1.0),
        "c_ctx": nrm((D,), 1.0),
        "w_ada": nrm((L, D, N_MOD * D), 0.5 * D ** -0.5),
        "b_ada": nrm((L, N_MOD * D), 0.01),
        "w_in": nrm((L, D, IN_COLS), D ** -0.5),
        "b_gate": nrm((L, N_BRANCHES * D), 0.01),
        "mla_q_g": gain((L, MLA_Q_RANK)),
        "mla_kv_g": gain((L, MLA_KV_RANK)),
        "w_uq": nrm((L, MLA_Q_RANK, MLA_HEADS * MLA_QK_DIM), MLA_Q_RANK ** -0.5),
        "w_uk": nrm((L, MLA_KV_RANK, MLA_HEADS * MLA_NOPE_DIM), MLA_KV_RANK ** -0.5),
        "w_uv": nrm((L, MLA_KV_RANK, MLA_HEADS * MLA_V_DIM), MLA_KV_RANK ** -0.5),
        "gqa_q_g": gain((L, GQA_HEAD_DIM)),
        "gqa_k_g": gain((L, GQA_HEAD_DIM)),
        "w_fo": nrm((L, F_WIDTH, D), F_WIDTH ** -0.5),
        "w_mo": nrm((L, MLA_WIDTH, D), MLA_WIDTH ** -0.5),
        "w_go": nrm((L, GQA_WIDTH, D), GQA_WIDTH ** -0.5),
        "w_o": nrm((L, D, D), beta * D ** -0.5),
        "ln1_g": gain((L, D)),
        "ln1_b": nrm((L, D), 0.02),
        "w1": nrm((L, D, D_FF), D ** -0.5),
        "w2": nrm((L, D_FF, D), beta * D_FF ** -0.5),
        "ln2_g": gain((L, D)),
        "ln2_b": nrm((L, D), 0.02),
    }


def reference(x, c, ctx, c_ctx, w_ada, b_ada, w_in, b_gate, mla_q_g, mla_kv_g, w_uq, w_uk, w_uv,
              gqa_q_g, gqa_k_g, w_fo, w_mo, w_go, w_o, ln1_g, ln1_b, w1, w2, ln2_g, ln2_b):
    B, S, D = x.shape
    ROWS = S // GRID_W
    rows = jnp.repeat(jnp.arange(ROWS), GRID_W).astype(jnp.float32)
    cols = jnp.tile(jnp.arange(GRID_W), ROWS).astype(jnp.float32)
    rope = (rope_angles(rows, cols, MLA_ROPE_DIM), rope_angles(rows, cols, GQA_HEAD_DIM))
    alpha = (2.0 * DEPTH) ** 0.25
    xc = ctx
    for l in range(DEPTH):
        last = l == DEPTH - 1
        lw = {
            "w_in": w_in[l], "b_gate": b_gate[l], "mla_q_g": mla_q_g[l], "mla_kv_g": mla_kv_g[l],
            "w_uq": w_uq[l], "w_uk": w_uk[l], "w_uv": w_uv[l], "gqa_q_g": gqa_q_g[l],
            "gqa_k_g": gqa_k_g[l], "w_fo": w_fo[l], "w_mo": w_mo[l], "w_go": w_go[l], "w_o": w_o[l],
            "w1": w1[l], "w2": w2[l],
        }
        mod_x = (jax.nn.silu(c) @ w_ada[l] + b_ada[l])[:, None, :]
        mod_c = (jax.nn.silu(c_ctx) @ w_ada[l] + b_ada[l])[None, None, :]
        sh1, sc1, g1, sh2, sc2, g2 = jnp.split(mod_x, N_MOD, axis=-1)
        csh1, csc1, cg1, csh2, csc2, cg2 = jnp.split(mod_c, N_MOD, axis=-1)

        h_x = modulate(x, sh1, sc1)
        h_c = modulate(xc, csh1, csc1)
        p_x = h_x @ lw["w_in"]
        p_c = h_c @ (lw["w_in"][:, :KV_COLS] if last else lw["w_in"])
        kv_c = kv_parts(p_c, lw, None)
        kv_x = kv_parts(p_x, lw, rope)
        kv_all = (
            jnp.concatenate([kv_c[0], kv_x[0]], axis=1),
            jnp.concatenate([kv_c[1], kv_x[1]], axis=1),
            jnp.concatenate([kv_c[2], kv_x[2]], axis=1),
            jnp.concatenate([kv_c[3], kv_x[3]], axis=1),
        )
        mix_x = mixer(p_x, kv_all, lw, rope)
        x = layer_norm(alpha * x + g1 * mix_x, ln1_g[l], ln1_b[l])
        x = layer_norm(alpha * x + g2 * sq_relu_mlp(modulate(x, sh2, sc2), lw), ln2_g[l], ln2_b[l])

        if not last:
            mix_c = mixer(p_c, kv_c, lw, None)
            xc = layer_norm(alpha * xc + cg1 * mix_c, ln1_g[l], ln1_b[l])
            xc = layer_norm(alpha * xc + cg2 * sq_relu_mlp(modulate(xc, csh2, csc2), lw), ln2_g[l], ln2_b[l])
    return x
```

```python
import numpy as np
import ml_dtypes
from contextlib import ExitStack
import concourse.bass as bass
import concourse.mybir as mybir
from concourse.bass_utils import run_bass_kernel_spmd

F32 = mybir.dt.float32
BF16 = mybir.dt.bfloat16
AF = mybir.ActivationFunctionType
ALU = mybir.AluOpType
AX = mybir.AxisListType

D = 1024
T = 2304
TC = 256
TX = 2048
NTT = 18
TB = [(0, 256), (256, 512), (768, 512), (1280, 512), (1792, 512)]
EPS = 1e-6
ALPHA = float((2.0 * 4) ** 0.25)
NL = 4


class Buf:
    def __init__(self, name, dram=False):
        self.name = name
        self.w = None
        self.r = {}
        self.dram = dram
        self.sem = None
        self.phase = -1


class Rot:
    def __init__(self, items):
        self.items = list(items)
        self.i = 0

    def next(self):
        it = self.items[self.i % len(self.items)]
        self.i += 1
        return it


class KB:
    def __init__(self, nc, es, npool=56):
        self.nc = nc
        self.eng = {"pe": nc.tensor, "act": nc.scalar, "dve": nc.vector, "pool": nc.gpsimd, "sp": nc.sync}
        self.sem = {k: es.enter_context(nc.semaphore("s_" + k)) for k in ["pe", "act", "dve", "pool"]}
        self.cnt = {k: 0 for k in self.sem}
        self.waited = {}
        self.semcnt = {}
        self.pool = [es.enter_context(nc.semaphore("dq%d" % i)) for i in range(npool)]
        self.pool_i = 0
        self.phase = 0
        self.dram_sems = []
        self.es = es
        self.uid = 0

    def name(self, p):
        self.uid += 1
        return "%s_%d" % (p, self.uid)

    def dram_buf(self, name):
        return Buf(name, dram=True)

    def _wait(self, e, tok):
        if tok is None:
            return
        sem, val = tok
        key = (e, id(sem))
        if self.waited.get(key, 0) >= val:
            return
        self.eng[e].wait_ge(sem, val)
        self.waited[key] = val

    def _deps(self, e, reads, writes):
        for b in reads:
            self._wait(e, b.w)
        for b in writes:
            if not b.dram:
                self._wait(e, b.w)
            for t in list(b.r.values()):
                self._wait(e, t)

    def _post(self, tok, reads, writes):
        for b in reads:
            b.r[id(tok[0])] = tok
        for b in writes:
            b.w = tok
            if not b.dram:
                b.r = {}

    def op(self, e, fn, reads=(), writes=()):
        self._deps(e, reads, writes)
        ins = fn(self.eng[e])
        self.cnt[e] += 1
        ins.then_inc(self.sem[e], 1)
        self._post((self.sem[e], self.cnt[e]), reads, writes)

    def dma(self, q, out, in_, dst, reads=(), src=None, **kw):
        holder = dst
        if dst.dram:
            assert src is not None
            holder = src
        if holder.sem is None or holder.phase != self.phase:
            assert self.pool_i < len(self.pool), "dma sem pool exhausted"
            holder.sem = self.pool[self.pool_i]
            self.pool_i += 1
            holder.phase = self.phase
            self.semcnt.setdefault(id(holder.sem), 0)
        reads = list(reads)
        if src is not None and src not in reads:
            reads.append(src)
        self._deps(q, reads, [dst])
        ins = self.eng[q].dma_start(out=out, in_=in_, **kw)
        self.semcnt[id(holder.sem)] += 16
        ins.then_inc(holder.sem, 16)
        self._post((holder.sem, self.semcnt[id(holder.sem)]), reads, [dst])

    def barrier(self):
        toks = [(self.sem[k], self.cnt[k]) for k in self.sem if self.cnt[k] > 0]
        for s in self.pool[: self.pool_i]:
            c = self.semcnt.get(id(s), 0)
            if c > 0:
                toks.append((s, c))
        for e in self.eng:
            for t in toks:
                self._wait(e, t)
        self.pool_i = 0
        self.phase += 1


def build(n_layers=NL, debug=False):
    nc = bass.Bass("TRN2", target_bir_lowering=False)
    es = ExitStack()
    with es:
        _build(nc, es, n_layers, debug)
    return nc


def _build(nc, es, n_layers, debug):
    def din(name, shape, dt=F32):
        return nc.dram_tensor(name, list(shape), dt, kind="ExternalInput").ap()

    skind = "ExternalOutput" if debug else "Internal"

    def dscr(name, shape, dt=BF16):
        return nc.dram_tensor(name, list(shape), dt, kind=skind).ap()

    xin = din("xin", [T, D])
    ccT = din("ccT", [128, 16])
    w_ada = din("w_ada", [NL, D, 6144])
    b_ada = din("b_ada", [NL, 6144])
    b_adaT = din("b_adaT", [NL, 128, 48])
    w_tm = din("w_tm", [NL, D, 1280])
    w_fm = din("w_fm", [NL, D, 3616])
    b_gateT = din("b_gateT", [NL, 128, 24])
    g_mq = din("g_mq", [NL, 256])
    g_mkv = din("g_mkv", [NL, 256])
    g_gq = din("g_gq", [NL, 512])
    g_gk = din("g_gk", [NL, 128])
    w_uq = din("w_uq", [NL, 256, 768])
    w_uk = din("w_uk", [NL, 256, 512])
    w_uv = din("w_uv", [NL, 256, 512])
    w_fo = din("w_fo", [NL, 512, D])
    w_mo = din("w_mo", [NL, 512, D])
    w_go = din("w_go", [NL, 512, D])
    w_o = din("w_o", [NL, D, D])
    ln1_g = din("ln1_g", [NL, D])
    ln1_b = din("ln1_b", [NL, D])
    ln2_g = din("ln2_g", [NL, D])
    ln2_b = din("ln2_b", [NL, D])
    w1 = din("w1", [NL, D, 4096])
    w2 = din("w2", [NL, 4096, D])
    ident_d = din("ident", [128, 128])
    cs128_d = din("cs128", [128, 256], BF16)
    dftc_d = din("dftc", [4, 128, 16, 512], BF16)
    dfts_d = din("dfts", [4, 128, 16, 512], BF16)
    dftc256_d = din("dftc256", [TC, TC], BF16)
    dfts256_d = din("dfts256", [TC, TC], BF16)
    ropeM_d = din("ropeM", [128, 2, T])
    ropeG_d = din("ropeG", [TX, 2, 32])
    y_d = nc.dram_tensor("y", [TX, D], F32, kind="ExternalOutput").ap()

    mod_d = dscr("mod_d", [NL, 2, 6144], F32)
    kmT_d = dscr("kmT_d", [8, 96, T])
    qmT_d = dscr("qmT_d", [8, 96, T])
    vm_d = dscr("vm_d", [T, 512])
    kgT_d = dscr("kgT_d", [2, 64, T])
    qgT_d = dscr("qgT_d", [8, 64, T])
    vg_d = dscr("vg_d", [T, 128])
    ckvT_d = dscr("ckvT_d", [256, T])
    cqT_d = dscr("cqT_d", [256, T])
    z_d = dscr("z_d", [T, 4, 256])
    fmT_d = dscr("fmT_d", [512, T])
    amT_d = dscr("amT_d", [512, T])
    agT_d = dscr("agT_d", [512, T])
    gT_d = dscr("gT_d", [3072, T])
    aT_d = dscr("aT_d", [4096, T])

    K = KB(nc, es)
    B_mod = K.dram_buf("mod")
    B_kmT = K.dram_buf("kmT")
    B_qmT = K.dram_buf("qmT")
    B_vm = K.dram_buf("vm")
    B_kgT = K.dram_buf("kgT")
    B_qgT = K.dram_buf("qgT")
    B_vg = K.dram_buf("vg")
    B_ckvT = K.dram_buf("ckvT")
    B_cqT = K.dram_buf("cqT")
    B_z = K.dram_buf("z")
    B_fmT = K.dram_buf("fmT")
    B_amT = K.dram_buf("amT")
    B_agT = K.dram_buf("agT")
    B_gT = K.dram_buf("gT")
    B_aT = K.dram_buf("aT")
    B_y = K.dram_buf("y")

    def sb(stack, shape, dt, pfx="t"):
        t = stack.enter_context(nc.sbuf_tensor(K.name(pfx), list(shape), dt))
        return t, Buf(pfx)

    X, _ = sb(es, [128, NTT, D], F32, "X")
    XB = [Buf("X%d" % i) for i in range(NTT)]
    ident, B_ident = sb(es, [128, 128], BF16, "ident")
    modT, B_modT = sb(es, [128, NL, 48, 2], F32, "modT")
    epsc, B_eps = sb(es, [128, 1], F32, "epsc")
    K.op("pool", lambda e: e.memset(epsc[:], EPS), [], [B_eps])
    PFall = es.enter_context(nc.psum_tensor(K.name("pf"), [128, 6 * 512], F32))
    PF = [(PFall[:, i * 512:(i + 1) * 512], Buf("pf%d" % i)) for i in range(6)]
    PB = []
    for i in range(2):
        t = es.enter_context(nc.psum_tensor(K.name("pb"), [128, 1024], BF16))
        PB.append((t, Buf("pb%d" % i)))

    def wview(wl, p=128):
        return wl.rearrange("(kc p) n -> p kc n", p=p)

    for tt in range(NTT):
        K.dma("sp", X[:, tt, :], xin[tt * 128:(tt + 1) * 128, :], XB[tt])
    K.dma("pool", ident[:], ident_d, B_ident)

    sc_bf, B_sc = sb(es, [128, 16], BF16, "scbf")
    B_modTl = [Buf("modT%d" % i) for i in range(NL)]

    def adaln_bufs(stack):
        return dict(wb=[sb(stack, [128, 8, 512], BF16, "wada") for _ in range(2)],
                    brow=[sb(stack, [2, 512], F32, "brow") for _ in range(2)],
                    bT=sb(stack, [128, 48], F32, "bT"),
                    rows=[sb(stack, [2, 512], F32, "rows") for _ in range(2)])

    def adaln_gen(l, bufs):
        bT, B_bT = bufs["bT"]
        K.dma("sp", bT[:], b_adaT[l], B_bT)
        yield
        pc_t, B_pc = PF[2]
        wv = wview(w_ada[l])
        for j in range(12):
            wt, B_w = bufs["wb"][j % 2]
            K.dma("pool", wt[:], wv[:, :, j * 512:(j + 1) * 512], B_w)
            yield
            br, B_br = bufs["brow"][j % 2]
            K.dma("sp", br[:], b_ada[l:l + 1, j * 512:(j + 1) * 512].partition_broadcast(2)[:, 0, :], B_br)
            yield
            pr_t, B_pr = PF[j % 2]

            def mm_rows(e):
                for kc in range(8):
                    ins = e.matmul(pr_t[0:2, :], sc_bf[:, 2 * kc:2 * kc + 2], wt[:, kc, :],
                                   start=(kc == 0), stop=(kc == 7))
                return ins
            K.op("pe", mm_rows, [B_sc, B_w], [B_pr])
            yield
            rt, B_r = bufs["rows"][j % 2]
            K.op("dve", lambda e: e.tensor_tensor(out=rt[:], in0=pr_t[0:2, :], in1=br[:], op=ALU.add),
                 [B_pr, B_br], [B_r])
            yield
            K.dma("sp", mod_d[l, :, j * 512:(j + 1) * 512], rt[:], B_mod, src=B_r)
            yield
            for c4 in range(4):
                ch = j * 4 + c4

                def mm_cols(e, c4=c4, ch=ch):
                    for kc in range(8):
                        ins = e.matmul(pc_t[:, 2 * ch:2 * ch + 2], wt[:, kc, c4 * 128:(c4 + 1) * 128],
                                       sc_bf[:, 2 * kc:2 * kc + 2], start=(kc == 0), stop=(kc == 7))
                    return ins
                K.op("pe", mm_cols, [B_sc, B_w], [B_pc])
                yield
        K.op("dve", lambda e: e.tensor_tensor(
            out=modT[:, l, :, :], in0=pc_t[:, 0:96].rearrange("p (c r) -> p c r", r=2),
            in1=bT[:, :].unsqueeze(2).broadcast_to([128, 48, 2]), op=ALU.add), [B_pc, B_bT], [B_modTl[l]])
        yield
        for c0 in (8, 32):
            K.op("dve", lambda e, c0=c0: e.tensor_scalar_add(
                out=modT[:, l, c0:c0 + 8, :], in0=modT[:, l, c0:c0 + 8, :], scalar1=1.0), [B_modTl[l]], [B_modTl[l]])
            yield

    with ExitStack() as ps:
        cc_f, B_ccf = sb(ps, [128, 16], F32, "ccf")
        K.dma("sp", cc_f[:], ccT, B_ccf)
        K.op("act", lambda e: e.activation(out=sc_bf[:], in_=cc_f[:], func=AF.Silu), [B_ccf], [B_sc])
        for _ in adaln_gen(0, adaln_bufs(ps)):
            pass
        K.barrier()

    def run_streams(gens, width=2, extra=None):
        gens = iter(gens)
        active = []
        while True:
            while len(active) < width:
                g = next(gens, None)
                if g is None:
                    break
                active.append(g)
            if not active:
                break
            for g in list(active):
                try:
                    next(g)
                except StopIteration:
                    active.remove(g)
            if extra is not None:
                try:
                    next(extra)
                except StopIteration:
                    extra = None
        if extra is not None:
            for _ in extra:
                pass

    def ln_stats(stk_bufs, tt):
        (st, B_st), (mv, B_mv), (rn, B_rn) = stk_bufs
        K.op("dve", lambda e: e.bn_stats(out=st[:, 0:6], in_=X[:, tt, 0:512]), [XB[tt]], [B_st])
        yield
        K.op("dve", lambda e: e.bn_stats(out=st[:, 6:12], in_=X[:, tt, 512:1024]), [XB[tt]], [B_st])
        yield
        K.op("dve", lambda e: e.bn_aggr(out=mv[:, 0:2], in_=st[:, 0:12]), [B_st], [B_mv])
        yield
        K.op("act", lambda e: e.activation(out=rn[:, 0:1], in_=mv[:, 1:2], func=AF.Sqrt, bias=epsc[:, 0:1], scale=1.0),
             [B_mv, B_eps], [B_rn])
        yield
        K.op("dve", lambda e: e.reciprocal(out=rn[:, 0:1], in_=rn[:, 0:1]), [B_rn], [B_rn])
        yield
        K.op("dve", lambda e: e.scalar_tensor_tensor(out=rn[:, 1:2], in0=mv[:, 0:1], scalar=-1.0, in1=rn[:, 0:1],
                                                     op0=ALU.mult, op1=ALU.mult), [B_mv, B_rn], [B_rn])
        yield

    def ln_mod(l, sub, hT, HB, skip_ctx, extra_fn=None):
        sh0 = 0 if sub == 1 else 24
        sc0 = 8 if sub == 1 else 32
        with ExitStack() as ps:
            W = 2
            stats = [[sb(ps, [128, 12], F32, "st"), sb(ps, [128, 2], F32, "mv"), sb(ps, [128, 2], F32, "rn")]
                     for _ in range(W)]
            xn = [sb(ps, [128, D], BF16, "xn") for _ in range(W)]
            tmp = [sb(ps, [128, 8, 128], F32, "lt") for _ in range(W)]

            def tile(i, bi, tt):
                r = 1 if tt < 2 else 0
                yield from ln_stats(stats[i % W], tt)
                rn, B_rn = stats[i % W][2]
                xt, B_x = xn[i % W]
                K.op("act", lambda e: e.activation(
                    out=xt[:], in_=X[:, tt, :], func=AF.Identity, bias=rn[:, 1:2], scale=rn[:, 0:1]),
                    [XB[tt], B_rn], [B_x])
                yield
                pt, B_p = PB[i % 2]

                def tr(e):
                    for c in range(8):
                        ins = e.transpose(pt[:, c * 128:(c + 1) * 128], xt[:, c * 128:(c + 1) * 128], ident[:])
                    return ins
                K.op("pe", tr, [B_x, B_ident], [B_p])
                yield
                tm, B_t = tmp[i % W]
                A = modT[:, l, sc0:sc0 + 8, r:r + 1].broadcast_to([128, 8, 128])
                Bv = modT[:, l, sh0:sh0 + 8, r:r + 1].broadcast_to([128, 8, 128])
                K.op("dve", lambda e: e.tensor_tensor(
                    out=tm[:], in0=pt[:, :].rearrange("p (c t) -> p c t", t=128), in1=A, op=ALU.mult),
                    [B_p, B_modTl[l]], [B_t])
                yield
                K.op("dve", lambda e: e.tensor_tensor(
                    out=hT[:, :, tt * 128:(tt + 1) * 128], in0=tm[:], in1=Bv, op=ALU.add),
                    [B_t, B_modTl[l]], [HB[bi]])
                yield
            gl = []
            i = 0
            for bi, (t0, nb) in enumerate(TB):
                if skip_ctx and bi == 0:
                    continue
                for tt in range(t0 // 128, (t0 + nb) // 128):
                    gl.append(tile(i, bi, tt))
                    i += 1
            run_streams(gl, W, extra_fn(ps) if extra_fn is not None else None)
            K.barrier()

    def rms_rope(raw, B_raw, G, d, gain, B_gain, outbf, B_out, rope, tp):
        (sq, B_sq), (ss, B_ss), (nn, B_nn), (t1, B_t1), (t2, B_t2) = tp
        n = G * d
        K.op("pool", lambda e: e.tensor_tensor(out=sq[:, 0:n], in0=raw, in1=raw, op=ALU.mult), [B_raw], [B_sq])
        yield
        K.op("dve", lambda e: e.tensor_reduce(out=ss[:, 0:G], in_=sq[:, 0:n].rearrange("p (g d) -> p g d", d=d),
                                              axis=AX.X, op=ALU.add), [B_sq], [B_ss])
        yield
        K.op("act", lambda e: e.activation(out=ss[:, 0:G], in_=ss[:, 0:G], func=AF.Sqrt, bias=epsc[:, 0:1], scale=1.0 / d),
             [B_ss, B_eps], [B_ss])
        yield
        K.op("dve", lambda e: e.reciprocal(out=ss[:, 0:G], in_=ss[:, 0:G]), [B_ss], [B_ss])
        yield
        rb = ss[:, 0:G].unsqueeze(2).broadcast_to([128, G, d])
        K.op("dve", lambda e: e.tensor_tensor(out=nn[:, 0:n].rearrange("p (g d) -> p g d", d=d),
                                              in0=raw.rearrange("p (g d) -> p g d", d=d), in1=rb, op=ALU.mult),
             [B_raw, B_ss], [B_nn])
        yield
        if rope is None:
            K.op("pool", lambda e: e.tensor_tensor(out=outbf, in0=nn[:, 0:n], in1=gain, op=ALU.mult),
                 [B_nn, B_gain], [B_out])
            yield
            return
        cos, sin, B_rp = rope
        h = d // 2
        K.op("pool", lambda e: e.tensor_tensor(out=nn[:, 0:n], in0=nn[:, 0:n], in1=gain, op=ALU.mult),
             [B_nn, B_gain], [B_nn])
        yield
        n3 = nn[:, 0:n].rearrange("p (g d) -> p g d", d=d)
        o3 = outbf.rearrange("p (g d) -> p g d", d=d)
        cb = cos.unsqueeze(1).broadcast_to([128, G, h])
        sbb = sin.unsqueeze(1).broadcast_to([128, G, h])
        a1 = t1[:, 0:G * h].rearrange("p (g d) -> p g d", d=h)
        a2 = t2[:, 0:G * h].rearrange("p (g d) -> p g d", d=h)
        K.op("dve", lambda e: e.tensor_tensor(out=a1, in0=n3[:, :, 0:h], in1=cb, op=ALU.mult), [B_nn, B_rp], [B_t1])
        yield
        K.op("dve", lambda e: e.tensor_tensor(out=a2, in0=n3[:, :, h:d], in1=sbb, op=ALU.mult), [B_nn, B_rp], [B_t2])
        yield
        K.op("dve", lambda e: e.tensor_tensor(out=o3[:, :, 0:h], in0=a1, in1=a2, op=ALU.subtract),
             [B_t1, B_t2], [B_out])
        yield
        K.op("dve", lambda e: e.tensor_tensor(out=a1, in0=n3[:, :, 0:h], in1=sbb, op=ALU.mult), [B_nn, B_rp], [B_t1])
        yield
        K.op("dve", lambda e: e.tensor_tensor(out=a2, in0=n3[:, :, h:d], in1=cb, op=ALU.mult), [B_nn, B_rp], [B_t2])
        yield
        K.op("dve", lambda e: e.tensor_tensor(out=o3[:, :, h:d], in0=a1, in1=a2, op=ALU.add),
             [B_t1, B_t2], [B_out])
        yield

    def bcast_row(stack, src_row_ap, n, q="sp", reads=(), pfx="bc"):
        t, B = sb(stack, [128, n], F32, pfx)
        K.dma(q, t[:], src_row_ap.partition_broadcast(128)[:, 0, :], B, reads)
        return t, B

    for l in range(n_layers):
        last = (l == NL - 1)
        blocks_q = [bi for bi in range(5) if not (last and bi == 0)]
        tiles_q = list(range(2 if last else 0, NTT))

        with ExitStack() as hs:
            hT, _ = sb(hs, [128, 8, T], BF16, "hT")
            HB = [Buf("hT%d" % i) for i in range(5)]
            wt_es = ExitStack()
            wtm_b = [sb(wt_es, [128, 8, 512], BF16, "wtm") for _ in range(3)]
            wvl = wview(w_tm[l])
            for pi_, (c0_, nc_) in enumerate(((0, 512), (512, 256), (768, 512))):
                K.dma("pool", wtm_b[pi_][0][:, :, 0:nc_], wvl[:, :, c0_:c0_ + nc_], wtm_b[pi_][1])
            ln_mod(l, 1, hT, HB, False,
                   (lambda st, l=l: adaln_gen(l + 1, adaln_bufs(st))) if l + 1 < n_layers else None)
            if debug and l == 0:
                hdbg = nc.dram_tensor("hT_dbg", [128, 8, T], BF16, kind="ExternalOutput").ap()
                B_hd = K.dram_buf("hdbg")
                K.dma("sp", hdbg, hT[:], B_hd, src=HB[0])
                K.barrier()

            with ExitStack() as ps:
                gkv, B_gkv = bcast_row(ps, g_mkv[l:l + 1, :], 256)
                gq_, B_gq_ = bcast_row(ps, g_mq[l:l + 1, :], 256)
                ggq, B_ggq = bcast_row(ps, g_gq[l:l + 1, :], 512)
                ggk, B_ggk = bcast_row(ps, g_gk[l:l + 1, :], 128)
                rG, B_rG = sb(ps, [128, 16, 2, 32], F32, "ropeG")
                K.dma("sp", rG[:], ropeG_d.rearrange("(t p) a d -> p t a d", p=128), B_rG)
                wps = wtm_b
                raws = [sb(ps, [128, 512], F32, "raw") for _ in range(4)]
                tps = [[sb(ps, [128, 512], F32, "sq"), sb(ps, [128, 8], F32, "ss"), sb(ps, [128, 512], F32, "nn"),
                        sb(ps, [128, 256], F32, "t1"), sb(ps, [128, 256], F32, "t2")] for _ in range(4)]
                obf = [sb(ps, [128, 512], BF16, "obf") for _ in range(4)]
                stg = [sb(ps, [128, 4, 512], BF16, "stg") for _ in range(4)]
                vst = [sb(ps, [128, 128], BF16, "vst") for _ in range(4)]
                wvl = wview(w_tm[l])
                pieces = [("A", 0, 512), ("B", 512, 256), ("C", 768, 512)]
                W2 = 4
                PH = [(PB[i % 2][0][:, (i // 2) * 512:(i // 2 + 1) * 512], Buf("ph%d" % i)) for i in range(4)]
                blk_done = {}

                def p2_tile(it, pn, ncol, wt, B_w, bi, t0, nb, ti, tt, st_t, B_stg, key):
                    pp, B_pp = PF[it % W2]
                    raw, B_raw = raws[it % W2]
                    tp = tps[it % W2]
                    ob, B_ob = obf[it % W2]
                    ptb, B_ptb = PH[it % W2]

                    def mm(e):
                        for kc in range(8):
                            ins = e.matmul(pp[:, 0:ncol], hT[:, kc, tt * 128:(tt + 1) * 128],
                                           wt[:, kc, 0:ncol], start=(kc == 0), stop=(kc == 7))
                        return ins
                    K.op("pe", mm, [HB[bi], B_w], [B_pp])
                    yield
                    K.op("act", lambda e: e.copy(out=raw[:, 0:ncol], in_=pp[:, 0:ncol]), [B_pp], [B_raw])
                    yield
                    isx = tt >= 2
                    rp = (rG[:, tt - 2, 0, :], rG[:, tt - 2, 1, :], B_rG) if isx else None
                    if pn == "A":
                        yield from rms_rope(raw[:, 0:256], B_raw, 1, 256, gkv[:], B_gkv, ob[:, 0:256], B_ob, None, tp)
                        yield from rms_rope(raw[:, 256:384], B_raw, 2, 64, ggk[:], B_ggk, ob[:, 256:384], B_ob, rp, tp)
                        vt, B_v = vst[it % W2]
                        K.op("act", lambda e: e.copy(out=vt[:], in_=raw[:, 384:512]), [B_raw], [B_v])
                        yield
                        K.dma("sp", vg_d[tt * 128:(tt + 1) * 128, :], vt[:], B_vg, src=B_v)
                        ntr = 3
                    elif pn == "B":
                        yield from rms_rope(raw[:, 0:256], B_raw, 1, 256, gq_[:], B_gq_, ob[:, 0:256], B_ob, None, tp)
                        ntr = 2
                    else:
                        yield from rms_rope(raw[:, 0:512], B_raw, 8, 64, ggq[:], B_ggq, ob[:, 0:512], B_ob, rp, tp)
                        ntr = 4

                    def tr(e):
                        for c in range(ntr):
                            ins = e.transpose(ptb[:, c * 128:(c + 1) * 128], ob[:, c * 128:(c + 1) * 128], ident[:])
                        return ins
                    K.op("pe", tr, [B_ob, B_ident], [B_ptb])
                    yield
                    K.op("act", lambda e: e.copy(
                        out=st_t[:, 0:ntr, ti * 128:(ti + 1) * 128],
                        in_=ptb[:, 0:ntr * 128].rearrange("p (c t) -> p c t", t=128)), [B_ptb], [B_stg])
                    yield
                    blk_done[key] = blk_done.get(key, 0) + 1
                    if blk_done[key] == nb // 128:
                        if pn == "A":
                            K.dma("sp", ckvT_d.rearrange("(c p) t -> p c t", p=128)[:, :, t0:t0 + nb],
                                  st_t[:, 0:2, 0:nb], B_ckvT, src=B_stg)
                            for kv in range(2):
                                K.dma("sp", kgT_d[kv, :, t0:t0 + nb], st_t[kv * 64:(kv + 1) * 64, 2, 0:nb], B_kgT, src=B_stg)
                        elif pn == "B":
                            K.dma("sp", cqT_d.rearrange("(c p) t -> p c t", p=128)[:, :, t0:t0 + nb],
                                  st_t[:, 0:2, 0:nb], B_cqT, src=B_stg)
                        else:
                            for hh in range(8):
                                K.dma("sp", qgT_d[hh, :, t0:t0 + nb],
                                      st_t[(hh % 2) * 64:(hh % 2) * 64 + 64, hh // 2, 0:nb], B_qgT, src=B_stg)
                gl = []
                it = 0
                ib = 0
                for pi, (pn, c0, ncol) in enumerate(pieces):
                    wt, B_w = wps[pi % 3]
                    for bi, (t0, nb) in enumerate(TB):
                        if pn != "A" and bi not in blocks_q:
                            continue
                        st_t, B_stg = stg[ib % 4]
                        ib += 1
                        for ti, tt in enumerate(range(t0 // 128, (t0 + nb) // 128)):
                            gl.append(p2_tile(it, pn, ncol, wt, B_w, bi, t0, nb, ti, tt, st_t, B_stg, (pi, bi)))
                            it += 1
                run_streams(gl, W2)
                K.barrier()
            wt_es.close()

            with ExitStack() as ps:
                rM, B_rM = sb(ps, [128, 2, T], F32, "ropeM")
                K.dma("sp", rM[:], ropeM_d, B_rM)
                cs, B_cs = sb(ps, [128, 256], BF16, "cs128")
                K.dma("sp", cs[:], cs128_d, B_cs)
                bg, B_bg = sb(ps, [128, 24], F32, "bgate")
                K.dma("sp", bg[:], b_gateT[l], B_bg)
                wkr, B_wkr = sb(ps, [128, 8, 32], BF16, "wkr")
                wkrr, B_wkrr = sb(ps, [128, 8, 32], BF16, "wkrr")
                wfl = wview(w_fm[l])
                K.dma("pool", wkr[:], wfl[:, :, 0:32], B_wkr)
                K.op("dve", lambda e: e.tensor_scalar_mul(out=wkrr[:, :, 0:16], in0=wkr[:, :, 16:32], scalar1=-1.0),
                     [B_wkr], [B_wkrr])
                K.op("dve", lambda e: e.tensor_copy(out=wkrr[:, :, 16:32], in_=wkr[:, :, 0:16]), [B_wkr], [B_wkrr])
                kt1 = [sb(ps, [32, 512], F32, "kt1") for _ in range(2)]
                kt2 = [sb(ps, [32, 512], F32, "kt2") for _ in range(2)]
                kst = [sb(ps, [32, 512], BF16, "kst") for _ in range(2)]
                for bi, (t0, nb) in enumerate(TB):
                    pk, B_pk = PF[4]
                    pkr, B_pkr = PF[5]
                    for (w_, B_w_, p_, B_p_) in ((wkr, B_wkr, pk, B_pk), (wkrr, B_wkrr, pkr, B_pkr)):
                        def mmk(e, w_=w_, p_=p_, t0=t0, nb=nb):
                            for kc in range(8):
                                ins = e.matmul(p_[0:32, 0:nb], w_[:, kc, :], hT[:, kc, t0:t0 + nb],
                                               start=(kc == 0), stop=(kc == 7))
                            return ins
                        K.op("pe", mmk, [HB[bi], B_w_], [B_p_])
                    a1, B_a1 = kt1[bi % 2]
                    a2, B_a2 = kt2[bi % 2]
                    ks, B_ks = kst[bi % 2]
                    K.op("dve", lambda e, a1=a1, pk=pk, t0=t0, nb=nb: e.tensor_tensor(
                        out=a1[:, 0:nb], in0=pk[0:32, 0:nb], in1=rM[0:32, 0, t0:t0 + nb], op=ALU.mult),
                        [B_pk, B_rM], [B_a1])
                    K.op("dve", lambda e, a2=a2, pkr=pkr, t0=t0, nb=nb: e.tensor_tensor(
                        out=a2[:, 0:nb], in0=pkr[0:32, 0:nb], in1=rM[0:32, 1, t0:t0 + nb], op=ALU.mult),
                        [B_pkr, B_rM], [B_a2])
                    K.op("dve", lambda e, a1=a1, a2=a2, ks=ks, nb=nb: e.tensor_tensor(
                        out=ks[:, 0:nb], in0=a1[:, 0:nb], in1=a2[:, 0:nb], op=ALU.add), [B_a1, B_a2], [B_ks])
                    for hh in range(8):
                        K.dma("sp", kmT_d[hh, 64:96, t0:t0 + nb], ks[:, 0:nb], B_kmT, src=B_ks)

                wps = [sb(ps, [128, 8, 512], BF16, "wfm") for _ in range(2)]
                fT = [sb(ps, [128, 512], BF16, "fT") for _ in range(2)]
                zst = [sb(ps, [128, 4, 256], BF16, "zst") for _ in range(4)]
                gst = [sb(ps, [128, 512], BF16, "gst") for _ in range(3)]
                ip = 0
                for pj in range(7):
                    wt, B_w = wps[pj % 2]
                    K.dma("pool", wt[:], wfl[:, :, 32 + pj * 512:32 + (pj + 1) * 512], B_w)
                    for bi in blocks_q:
                        t0, nb = TB[bi]
                        for c4 in range(4):
                            pp, B_pp = PF[ip % 2]
                            ip += 1

                            def mm(e, wt=wt, pp=pp, c4=c4, t0=t0, nb=nb):
                                for kc in range(8):
                                    ins = e.matmul(pp[:, 0:nb], wt[:, kc, c4 * 128:(c4 + 1) * 128], hT[:, kc, t0:t0 + nb],
                                                   start=(kc == 0), stop=(kc == 7))
                                return ins
                            K.op("pe", mm, [HB[bi], B_w], [B_pp])
                            if pj == 0:
                                ft, B_ft = fT[c4 % 2]
                                K.op("act", lambda e, ft=ft, pp=pp, nb=nb: e.copy(out=ft[:, 0:nb], in_=pp[:, 0:nb]),
                                     [B_pp], [B_ft])
                                for ti in range(nb // 128):
                                    pz, B_pz = PF[2 + (ti % 2)]
                                    K.op("pe", lambda e, pz=pz, ft=ft, ti=ti: e.matmul(
                                        pz[:, 0:256], ft[:, ti * 128:(ti + 1) * 128], cs[:], start=True, stop=True),
                                        [B_ft, B_cs], [B_pz])
                                    zt, B_zt = zst[ti]
                                    K.op("dve", lambda e, zt=zt, pz=pz, c4=c4: e.tensor_copy(out=zt[:, c4, :], in_=pz[:, 0:256]),
                                         [B_pz], [B_zt])
                                    if c4 == 3:
                                        tt = t0 // 128 + ti
                                        K.dma("sp", z_d[tt * 128:(tt + 1) * 128, :, :], zt[:], B_z, src=B_zt)
                            else:
                                ch = (pj - 1) * 4 + c4
                                gt, B_gt = gst[ip % 3]
                                K.op("act", lambda e, gt=gt, pp=pp, nb=nb, ch=ch: e.activation(
                                    out=gt[:, 0:nb], in_=pp[:, 0:nb], func=AF.Sigmoid, bias=bg[:, ch:ch + 1], scale=1.0),
                                    [B_pp, B_bg], [B_gt])
                                K.dma("sp", gT_d[ch * 128:(ch + 1) * 128, t0:t0 + nb], gt[:, 0:nb], B_gT, src=B_gt)
                K.barrier()

                with ExitStack() as p4:
                    wuq, B_wuq = sb(p4, [128, 2, 768], BF16, "wuq")
                    wuqr, B_wuqr = sb(p4, [128, 2, 768], BF16, "wuqr")
                    wuk, B_wuk = sb(p4, [128, 2, 512], BF16, "wuk")
                    wuv, B_wuv = sb(p4, [128, 2, 512], BF16, "wuv")
                    K.dma("pool", wuq[:], wview(w_uq[l]), B_wuq)
                    K.dma("pool", wuk[:], wview(w_uk[l]), B_wuk)
                    K.dma("pool", wuv[:], wview(w_uv[l]), B_wuv)
                    K.op("dve", lambda e: e.tensor_copy(out=wuqr[:], in_=wuq[:]), [B_wuq], [B_wuqr])
                    q4 = wuq[:, :, :].rearrange("p j (h d) -> p j h d", d=96)
                    q4r = wuqr[:, :, :].rearrange("p j (h d) -> p j h d", d=96)
                    for jc in range(2):
                        K.op("dve", lambda e, jc=jc: e.tensor_scalar_mul(out=q4r[:, jc, :, 64:80], in0=q4[:, jc, :, 80:96],
                                                                          scalar1=-1.0), [B_wuq, B_wuqr], [B_wuqr])
                        K.op("dve", lambda e, jc=jc: e.tensor_copy(out=q4r[:, jc, :, 80:96], in_=q4[:, jc, :, 64:80]),
                             [B_wuq, B_wuqr], [B_wuqr])
                    cin = [sb(p4, [128, 2, 512], BF16, "cin") for _ in range(2)]
                    qst = [sb(p4, [128, 512], BF16, "qst") for _ in range(3)]
                    qa = [sb(p4, [128, 512], F32, "qa") for _ in range(3)]
                    qb = [sb(p4, [128, 512], F32, "qb") for _ in range(3)]
                    vstm = [sb(p4, [128, 512], BF16, "vstm") for _ in range(2)]
                    cq_loaded = {}

                    def ensure_cq(ii):
                        if ii >= len(blocks_q) or ii in cq_loaded:
                            return
                        t0, nb = TB[blocks_q[ii]]
                        ci, B_ci = cin[ii % 2]
                        K.dma("sp", ci[:, :, 0:nb], cqT_d.rearrange("(c p) t -> p c t", p=128)[:, :, t0:t0 + nb],
                              B_ci, [B_cqT])
                        cq_loaded[ii] = True

                    def q_head(ih, ii, hh):
                        ensure_cq(ii)
                        t0, nb = TB[blocks_q[ii]]
                        ci, B_ci = cin[ii % 2]
                        pq, B_pq = PF[(ih % 3) * 2]
                        pqr, B_pqr = PF[(ih % 3) * 2 + 1]
                        for (w_, B_w_, p_, B_p_) in ((wuq, B_wuq, pq, B_pq), (wuqr, B_wuqr, pqr, B_pqr)):
                            def mmq(e, w_=w_, p_=p_):
                                for jc in range(2):
                                    ins = e.matmul(p_[0:96, 0:nb], w_[:, jc, hh * 96:(hh + 1) * 96], ci[:, jc, 0:nb],
                                                   start=(jc == 0), stop=(jc == 1))
                                return ins
                            K.op("pe", mmq, [B_ci, B_w_], [B_p_])
                            yield
                        qs, B_qs = qst[ih % 3]
                        a1, B_a1 = qa[ih % 3]
                        a2, B_a2 = qb[ih % 3]
                        K.op("act", lambda e: e.copy(out=qs[0:64, 0:nb], in_=pq[0:64, 0:nb]), [B_pq], [B_qs])
                        yield
                        K.op("dve", lambda e: e.tensor_tensor(
                            out=a1[64:96, 0:nb], in0=pq[64:96, 0:nb], in1=rM[64:96, 0, t0:t0 + nb], op=ALU.mult),
                            [B_pq, B_rM], [B_a1])
                        yield
                        K.op("dve", lambda e: e.tensor_tensor(
                            out=a2[64:96, 0:nb], in0=pqr[64:96, 0:nb], in1=rM[64:96, 1, t0:t0 + nb], op=ALU.mult),
                            [B_pqr, B_rM], [B_a2])
                        yield
                        K.op("dve", lambda e: e.tensor_tensor(
                            out=qs[64:96, 0:nb], in0=a1[64:96, 0:nb], in1=a2[64:96, 0:nb], op=ALU.add),
                            [B_a1, B_a2], [B_qs])
                        yield
                        K.dma("sp", qmT_d[hh, :, t0:t0 + nb], qs[0:96, 0:nb], B_qmT, src=B_qs)
                        if hh == 4:
                            ensure_cq(ii + 1)
                    gl = []
                    ih = 0
                    for ii in range(len(blocks_q)):
                        for hh in range(8):
                            gl.append(q_head(ih, ii, hh))
                            ih += 1
                    run_streams(gl, 3)
                    for bi, (t0, nb) in enumerate(TB):
                        ci, B_ci = cin[bi % 2]
                        K.dma("sp", ci[:, :, 0:nb], ckvT_d.rearrange("(c p) t -> p c t", p=128)[:, :, t0:t0 + nb],
                              B_ci, [B_ckvT])
                        for hp in range(4):
                            pp, B_pp = PF[hp % 2]

                            def mmk2(e, pp=pp, hp=hp, ci=ci, nb=nb):
                                for jc in range(2):
                                    ins = e.matmul(pp[:, 0:nb], wuk[:, jc, hp * 128:(hp + 1) * 128], ci[:, jc, 0:nb],
                                                   start=(jc == 0), stop=(jc == 1))
                                return ins
                            K.op("pe", mmk2, [B_ci, B_wuk], [B_pp])
                            qs, B_qs = qst[hp % 3]
                            K.op("act", lambda e, qs=qs, pp=pp, nb=nb: e.copy(out=qs[:, 0:nb], in_=pp[:, 0:nb]), [B_pp], [B_qs])
                            for s in range(2):
                                K.dma("sp", kmT_d[2 * hp + s, 0:64, t0:t0 + nb], qs[s * 64:(s + 1) * 64, 0:nb], B_kmT, src=B_qs)
                        for ti in range(nb // 128):
                            tt = t0 // 128 + ti
                            pv, B_pv = PF[2 + ti % 2]

                            def mmv(e, pv=pv, ci=ci, ti=ti):
                                for jc in range(2):
                                    ins = e.matmul(pv[:, :], ci[:, jc, ti * 128:(ti + 1) * 128], wuv[:, jc, :],
                                                   start=(jc == 0), stop=(jc == 1))
                                return ins
                            K.op("pe", mmv, [B_ci, B_wuv], [B_pv])
                            vb, B_vb = vstm[ti % 2]
                            K.op("dve", lambda e, vb=vb, pv=pv: e.tensor_copy(out=vb[:], in_=pv[:, :]), [B_pv], [B_vb])
                            K.dma("sp", vm_d[tt * 128:(tt + 1) * 128, :], vb[:], B_vm, src=B_vb)
                    K.barrier()

        with ExitStack() as ps:
            Zs, B_Zs = sb(ps, [128, 16, 4, 256], BF16, "Zs")
            K.dma("sp", Zs[:], z_d[TC:T, :, :].rearrange("(tt p) g c -> p tt g c", p=128), B_Zs, [B_z])
            Ct = [sb(ps, [128, 16, 512], BF16, "Ct") for _ in range(2)]
            St = [sb(ps, [128, 16, 512], BF16, "St") for _ in range(2)]
            fst = [sb(ps, [128, 512], BF16, "fst") for _ in range(3)]
            ig = 0
            for fb in range(4):
                cT, B_c = Ct[fb % 2]
                sT, B_s = St[fb % 2]
                K.dma("sp", cT[:], dftc_d[fb], B_c)
                K.dma("act", sT[:], dfts_d[fb], B_s)
                for g in range(4):
                    py, B_py = PF[ig % 2]

                    def mmf(e, py=py, g=g, cT=cT, sT=sT):
                        for tt in range(16):
                            e.matmul(py[:, :], Zs[:, tt, g, 0:128], cT[:, tt, :], start=(tt == 0), stop=False)
                            ins = e.matmul(py[:, :], Zs[:, tt, g, 128:256], sT[:, tt, :], start=False, stop=(tt == 15))
                        return ins
                    K.op("pe", mmf, [B_Zs, B_c, B_s], [B_py])
                    ft, B_ft = fst[ig % 3]
                    ig += 1
                    K.op("act", lambda e, ft=ft, py=py: e.copy(out=ft[:], in_=py[:, :]), [B_py], [B_ft])
                    K.dma("sp", fmT_d[g * 128:(g + 1) * 128, TC + fb * 512:TC + (fb + 1) * 512], ft[:], B_fmT, src=B_ft)
            if not last:
                Zc, B_Zc = sb(ps, [128, 2, 4, 256], BF16, "Zc")
                K.dma("sp", Zc[:], z_d[0:TC, :, :].rearrange("(tt p) g c -> p tt g c", p=128), B_Zc, [B_z])
                c2, B_c2 = sb(ps, [128, 2, 256], BF16, "c256")
                s2, B_s2 = sb(ps, [128, 2, 256], BF16, "s256")
                K.dma("sp", c2[:], dftc256_d.rearrange("(tt p) f -> p tt f", p=128), B_c2)
                K.dma("sp", s2[:], dfts256_d.rearrange("(tt p) f -> p tt f", p=128), B_s2)
                for g in range(4):
                    py, B_py = PF[ig % 2]

                    def mmfc(e, py=py, g=g):
                        for tt in range(2):
                            e.matmul(py[:, 0:256], Zc[:, tt, g, 0:128], c2[:, tt, :], start=(tt == 0), stop=False)
                            ins = e.matmul(py[:, 0:256], Zc[:, tt, g, 128:256], s2[:, tt, :], start=False, stop=(tt == 1))
                        return ins
                    K.op("pe", mmfc, [B_Zc, B_c2, B_s2], [B_py])
                    ft, B_ft = fst[ig % 3]
                    ig += 1
                    K.op("act", lambda e, ft=ft, py=py: e.copy(out=ft[:, 0:256], in_=py[:, 0:256]), [B_py], [B_ft])
                    K.dma("sp", fmT_d[g * 128:(g + 1) * 128, 0:TC], ft[:, 0:256], B_fmT, src=B_ft)
            K.barrier()

        def attention(KT_d, B_KT, QT_d, B_QT, V_d, B_V, OUT_d, B_OUT, dk, scale, nkv_total, q_per_kv, kv_group):
            with ExitStack() as ps:
                groups = []
                for kv0 in range(0, nkv_total, kv_group):
                    nkv = kv_group
                    KT, B_K = sb(ps, [dk, nkv, T], BF16, "KT")
                    K.dma("sp", KT[:], KT_d[kv0:kv0 + nkv, :, :].rearrange("h d t -> d h t"), B_K, [B_KT])
                    VA, _ = sb(ps, [128, NTT, nkv, 128], BF16, "VA")
                    VAB = [Buf("VA%d" % i) for i in range(NTT)]
                    K.op("pool", lambda e, VA=VA: e.memset(VA[:, :, :, 64:128], 1.0), [], VAB)
                    for kt in range(NTT):
                        K.dma("sp", VA[:, kt, :, 0:64],
                              V_d[kt * 128:(kt + 1) * 128, kv0 * 64:(kv0 + nkv) * 64].rearrange("p (h d) -> p h d", d=64),
                              VAB[kt], [B_V])
                    groups.append((kv0, KT, B_K, VA, VAB))
                nkv = kv_group
                nq = nkv * q_per_kv
                Qs_all = [sb(ps, [dk, nq, 512], BF16, "Q") for _ in range(2)]
                PT = [sb(ps, [128, 2, 512], BF16, "PT") for _ in range(3)]
                rs = [sb(ps, [128, 512], F32, "rs") for _ in range(2)]
                ost = [sb(ps, [64, 512], BF16, "ost") for _ in range(2)]
                kglob = [0]
                for (kv0, KT, B_K, VA, VAB) in groups:
                    Qs = Qs_all
                    SP = [(PFall[:, 0:1024].rearrange("p (u n) -> p u n", n=512), [PF[0][1], PF[1][1]]),
                          (PFall[:, 1024:2048].rearrange("p (u n) -> p u n", n=512), [PF[2][1], PF[3][1]])]
                    items = []
                    qloaded = {}

                    def ensure_q(ii):
                        if ii >= len(blocks_q) or ii in qloaded:
                            return
                        t0, nb = TB[blocks_q[ii]]
                        Q, B_Q = Qs[ii % 2]
                        K.dma("sp", Q[:, :, 0:nb],
                              QT_d[kv0 * q_per_kv:kv0 * q_per_kv + nq, :, t0:t0 + nb].rearrange("h d t -> d h t"),
                              B_Q, [B_QT])
                        qloaded[ii] = True
                    hc = 0
                    for ii, bi in enumerate(blocks_q):
                        kts = [0, 1] if bi == 0 else list(range(NTT))
                        npair = len(kts) // 2
                        for j in range(nq):
                            for pi in range(npair):
                                items.append((ii, bi, j, pi, npair, kts[2 * pi:2 * pi + 2], hc))
                            hc += 1

                    def emit_S(k):
                        if k >= len(items):
                            return
                        ii, bi, j, pi, npair, pr, hcx = items[k]
                        ensure_q(ii)
                        t0, nb = TB[bi]
                        Q, B_Q = Qs[ii % 2]
                        S, SB = SP[k % 2]
                        kvj = j // q_per_kv

                        def f(e):
                            for u, kt in enumerate(pr):
                                ins = e.matmul(S[:, u, 0:nb], KT[:, kvj, kt * 128:(kt + 1) * 128], Q[:, j, 0:nb],
                                               start=True, stop=True)
                            return ins
                        K.op("pe", f, [B_K, B_Q], SB)
                    emit_S(0)
                    emit_S(1)
                    for k, (ii, bi, j, pi, npair, pr, hcx) in enumerate(items):
                        t0, nb = TB[bi]
                        S, SB = SP[k % 2]
                        pt, B_pt = PT[k % 3]
                        kvj = j // q_per_kv
                        hglob = kv0 * q_per_kv + j
                        pO, B_pO = PF[4 + hcx % 2]
                        K.op("act", lambda e: e.activation(out=pt[:, :, 0:nb], in_=S[:, :, 0:nb], func=AF.Exp, scale=scale),
                             SB, [B_pt])

                        def pv(e):
                            for u, kt in enumerate(pr):
                                ins = e.matmul(pO[:, 0:nb], VA[:, kt, kvj, :], pt[:, u, 0:nb],
                                               start=(pi == 0 and u == 0), stop=(pi == npair - 1 and u == 1))
                            return ins
                        K.op("pe", pv, [VAB[pr[0]], VAB[pr[1]], B_pt], [B_pO])
                        if pi == 0:
                            ensure_q(ii + 1)
                        emit_S(k + 2)
                        if pi == npair - 1:
                            rs_t, B_rs = rs[hcx % 2]
                            os_t, B_os = ost[hcx % 2]
                            K.op("dve", lambda e: e.reciprocal(out=rs_t[64:128, 0:nb], in_=pO[64:128, 0:nb]), [B_pO], [B_rs])
                            K.op("dve", lambda e: e.tensor_tensor(out=os_t[0:64, 0:nb], in0=pO[0:64, 0:nb],
                                                                  in1=rs_t[64:128, 0:nb], op=ALU.mult), [B_pO, B_rs], [B_os])
                            K.dma("sp", OUT_d[hglob * 64:(hglob + 1) * 64, t0:t0 + nb], os_t[0:64, 0:nb], B_OUT, src=B_os)

                K.barrier()

        attention(kmT_d, B_kmT, qmT_d, B_qmT, vm_d, B_vm, amT_d, B_amT, 96, 96 ** -0.5, 8, 1, 4)
        def attention_gqa():
            scale = 64 ** -0.5
            with ExitStack() as ps:
                KT2, B_K = sb(ps, [128, 2, T], BF16, "KT2")
                for half in range(2):
                    K.dma("sp", KT2[half * 64:(half + 1) * 64, :, :], kgT_d.rearrange("h d t -> d h t"), B_K, [B_kgT])
                VA, _ = sb(ps, [128, NTT, 2, 128], BF16, "VAg")
                VAB = [Buf("VAg%d" % i) for i in range(NTT)]
                K.op("pool", lambda e: e.memset(VA[:, :, :, 64:128], 1.0), [], VAB)
                for kt in range(NTT):
                    K.dma("sp", VA[:, kt, :, 0:64],
                          vg_d[kt * 128:(kt + 1) * 128, :].rearrange("p (h d) -> p h d", d=64), VAB[kt], [B_vg])
                Qs = [sb(ps, [128, 4, 512], BF16, "Qg") for _ in range(2)]
                PT = [sb(ps, [128, 2, 512], BF16, "PTg") for _ in range(3)]
                rs = [sb(ps, [128, 2, 512], F32, "rsg") for _ in range(2)]
                ost = [sb(ps, [64, 2, 512], BF16, "ostg") for _ in range(2)]
                SP = [(PFall[:, 0:1024].rearrange("p (u n) -> p u n", n=512), [PF[0][1], PF[1][1]]),
                      (PFall[:, 1024:2048].rearrange("p (u n) -> p u n", n=512), [PF[2][1], PF[3][1]])]
                qloaded = {}

                def ensure_q(ii):
                    if ii >= len(blocks_q) or ii in qloaded:
                        return
                    t0, nb = TB[blocks_q[ii]]
                    Q, B_Q = Qs[ii % 2]
                    K.dma("sp", Q[:, :, 0:nb], qgT_d.rearrange("h d t -> (h d) t").rearrange("(c p) t -> p c t", p=128)[:, :, t0:t0 + nb],
                          B_Q, [B_qgT])
                    qloaded[ii] = True
                items = []
                pc = 0
                for ii, bi in enumerate(blocks_q):
                    kts = [0, 1] if bi == 0 else list(range(NTT))
                    for c in range(4):
                        for ki, kt in enumerate(kts):
                            items.append((ii, bi, c, ki, len(kts), kt, pc))
                        pc += 1

                def emit_S(k):
                    if k >= len(items):
                        return
                    ii, bi, c, ki, nk, kt, pcx = items[k]
                    ensure_q(ii)
                    t0, nb = TB[bi]
                    Q, B_Q = Qs[ii % 2]
                    S, SB = SP[k % 2]
                    kv = c // 2

                    def f(e):
                        for u in range(2):
                            ins = e.matmul(S[:, u, 0:nb], KT2[u * 64:(u + 1) * 64, kv, kt * 128:(kt + 1) * 128],
                                           Q[u * 64:(u + 1) * 64, c, 0:nb], start=True, stop=True)
                        return ins
                    K.op("pe", f, [B_K, B_Q], SB)
                emit_S(0)
                emit_S(1)
                for k, (ii, bi, c, ki, nk, kt, pcx) in enumerate(items):
                    t0, nb = TB[bi]
                    S, SB = SP[k % 2]
                    pt, B_pt = PT[k % 3]
                    kv = c // 2
                    pOs = [PF[4], PF[5]]
                    K.op("act", lambda e: e.activation(out=pt[:, :, 0:nb], in_=S[:, :, 0:nb], func=AF.Exp, scale=scale),
                         SB, [B_pt])

                    def pv(e):
                        for u in range(2):
                            ins = e.matmul(pOs[u][0][:, 0:nb], VA[:, kt, kv, :], pt[:, u, 0:nb],
                                           start=(ki == 0), stop=(ki == nk - 1))
                        return ins
                    K.op("pe", pv, [VAB[kt], B_pt], [pOs[0][1], pOs[1][1]])
                    if ki == 0:
                        ensure_q(ii + 1)
                    emit_S(k + 2)
                    if ki == nk - 1:
                        rs_t, B_rs = rs[pcx % 2]
                        os_t, B_os = ost[pcx % 2]
                        O2 = PFall[:, 2048:3072].rearrange("p (u n) -> p u n", n=512)
                        K.op("act", lambda e: e.activation(out=rs_t[64:128, :, 0:nb], in_=O2[64:128, :, 0:nb], func=AF.Ln),
                             [pOs[0][1], pOs[1][1]], [B_rs])
                        K.op("act", lambda e: e.activation(out=rs_t[64:128, :, 0:nb], in_=rs_t[64:128, :, 0:nb], func=AF.Exp,
                                                           scale=-1.0), [B_rs], [B_rs])
                        K.op("dve", lambda e: e.tensor_tensor(out=os_t[0:64, :, 0:nb], in0=O2[0:64, :, 0:nb],
                                                              in1=rs_t[64:128, :, 0:nb], op=ALU.mult),
                             [pOs[0][1], pOs[1][1], B_rs], [B_os])
                        for u in range(2):
                            hglob = 2 * c + u
                            K.dma("sp", agT_d[hglob * 64:(hglob + 1) * 64, t0:t0 + nb], os_t[0:64, u, 0:nb], B_agT, src=B_os)
                K.barrier()

        w8 = ExitStack()
        wbr = []
        for nm, wsrc in (("wfo", w_fo), ("wmo", w_mo), ("wgo", w_go)):
            t, B = sb(w8, [128, 4, D], BF16, nm)
            K.dma("pool", t[:], wview(wsrc[l]), B)
            wbr.append((t, B))
        wo, B_wo = sb(w8, [128, 8, D], BF16, "wo")
        K.dma("pool", wo[:], wview(w_o[l]), B_wo)
        attention_gqa()

        def residual_ln(tt, halves, gate_t, B_gate, lg, B_lg, lb, B_lb, stats, tmps, idx):
            for hf, (pp, B_pp) in enumerate(halves):
                tm, B_t = tmps[(2 * idx + hf) % len(tmps)]
                K.op("dve", lambda e: e.tensor_tensor(
                    out=tm[:], in0=pp, in1=gate_t[:, hf * 512:(hf + 1) * 512], op=ALU.mult), [B_pp, B_gate], [B_t])
                yield
                K.op("dve", lambda e: e.scalar_tensor_tensor(
                    out=X[:, tt, hf * 512:(hf + 1) * 512], in0=X[:, tt, hf * 512:(hf + 1) * 512], scalar=ALPHA,
                    in1=tm[:], op0=ALU.mult, op1=ALU.add), [B_t, XB[tt]], [XB[tt]])
                yield
            yield from ln_stats(stats[idx % len(stats)], tt)
            rn, B_rn = stats[idx % len(stats)][2]
            K.op("act", lambda e: e.activation(out=X[:, tt, :], in_=X[:, tt, :], func=AF.Identity,
                                               bias=rn[:, 1:2], scale=rn[:, 0:1]), [XB[tt], B_rn], [XB[tt]])
            yield
            K.op("dve", lambda e: e.tensor_tensor(out=X[:, tt, :], in0=X[:, tt, :], in1=lg[:], op=ALU.mult),
                 [XB[tt], B_lg], [XB[tt]])
            yield
            K.op("pool", lambda e: e.tensor_tensor(out=X[:, tt, :], in0=X[:, tt, :], in1=lb[:], op=ALU.add),
                 [XB[tt], B_lb], [XB[tt]])
            yield

        with ExitStack() as ps:
            lg, B_lg = bcast_row(ps, ln1_g[l:l + 1, :], D)
            lb, B_lb = bcast_row(ps, ln1_b[l:l + 1, :], D)
            g1 = []
            for r in range(2):
                g1.append(bcast_row(ps, mod_d[l, r:r + 1, 2048:3072], D, reads=[B_mod], pfx="g1"))
            brin = [sb(ps, [128, 4, 512], BF16, "brin") for _ in range(3)]
            gin = [sb(ps, [128, 8, 512], BF16, "gin") for _ in range(2)]
            mg, B_mg = sb(ps, [128, 8, 512], F32, "mg")
            mtmp = [sb(ps, [128, 512], F32, "mtmp") for _ in range(2)]
            mbf, B_mbf = sb(ps, [128, 8, 512], BF16, "mbf")
            stats = [[sb(ps, [128, 12], F32, "st"), sb(ps, [128, 2], F32, "mv"), sb(ps, [128, 2], F32, "rn")]
                     for _ in range(3)]
            rtmp = [sb(ps, [128, 512], F32, "rtmp") for _ in range(6)]
            srcs = [(fmT_d, B_fmT), (amT_d, B_amT), (agT_d, B_agT)]
            ic = 0
            igin = 0
            itile = 0
            for bi in blocks_q:
                t0, nb = TB[bi]
                for br in range(3):
                    bt, B_bt = brin[br]
                    K.dma("sp", bt[:, :, 0:nb], srcs[br][0].rearrange("(c p) t -> p c t", p=128)[:, :, t0:t0 + nb],
                          B_bt, [srcs[br][1]])
                    gt, B_gt = gin[igin % 2]
                    igin += 1
                    K.dma("sp", gt[:, :, 0:nb],
                          gT_d[br * 1024:(br + 1) * 1024, :].rearrange("(c p) t -> p c t", p=128)[:, :, t0:t0 + nb],
                          B_gt, [B_gT])
                    wt, B_w = wbr[br]
                    for oc in range(8):
                        pp, B_pp = PF[ic % 2]
                        ic += 1

                        def mmb(e, pp=pp, wt=wt, bt=bt, oc=oc, nb=nb):
                            for kc in range(4):
                                ins = e.matmul(pp[:, 0:nb], wt[:, kc, oc * 128:(oc + 1) * 128], bt[:, kc, 0:nb],
                                               start=(kc == 0), stop=(kc == 3))
                            return ins
                        K.op("pe", mmb, [B_bt, B_w], [B_pp])
                        if br == 0:
                            K.op("dve", lambda e, pp=pp, gt=gt, oc=oc, nb=nb: e.tensor_tensor(
                                out=mg[:, oc, 0:nb], in0=pp[:, 0:nb], in1=gt[:, oc, 0:nb], op=ALU.mult),
                                [B_pp, B_gt], [B_mg])
                        else:
                            tm, B_t = mtmp[ic % 2]
                            K.op("dve", lambda e, tm=tm, pp=pp, gt=gt, oc=oc, nb=nb: e.tensor_tensor(
                                out=tm[:, 0:nb], in0=pp[:, 0:nb], in1=gt[:, oc, 0:nb], op=ALU.mult),
                                [B_pp, B_gt], [B_t])
                            if br == 1:
                                K.op("pool", lambda e, tm=tm, oc=oc, nb=nb: e.tensor_tensor(
                                    out=mg[:, oc, 0:nb], in0=mg[:, oc, 0:nb], in1=tm[:, 0:nb], op=ALU.add),
                                    [B_t, B_mg], [B_mg])
                            else:
                                K.op("pool", lambda e, tm=tm, oc=oc, nb=nb: e.tensor_tensor(
                                    out=mbf[:, oc, 0:nb], in0=mg[:, oc, 0:nb], in1=tm[:, 0:nb], op=ALU.add),
                                    [B_t, B_mg], [B_mbf])
                def p8_tile(itile, ti, tt):
                    halves = []
                    for hf in range(2):
                        pp, B_pp = PF[hf + 2 * (itile % 3)]

                        def mmo(e, pp=pp, hf=hf):
                            for kc in range(8):
                                ins = e.matmul(pp[:, :], mbf[:, kc, ti * 128:(ti + 1) * 128], wo[:, kc, hf * 512:(hf + 1) * 512],
                                               start=(kc == 0), stop=(kc == 7))
                            return ins
                        K.op("pe", mmo, [B_mbf, B_wo], [B_pp])
                        yield
                        halves.append((pp[:, :], B_pp))
                    gate_t, B_gate = g1[1 if tt < 2 else 0]
                    yield from residual_ln(tt, halves, gate_t, B_gate, lg, B_lg, lb, B_lb, stats, rtmp, itile)
                gl = []
                for ti in range(nb // 128):
                    gl.append(p8_tile(itile, ti, t0 // 128 + ti))
                    itile += 1
                run_streams(gl, 3)
            K.barrier()
        w8.close()

        w11 = ExitStack()
        w2s, B_w2 = sb(w11, [128, 32, D], BF16, "w2s")
        w1_es = ExitStack()
        wps = [sb(w1_es, [128, 8, 512], BF16, "w1p") for _ in range(2)]
        wv1 = wview(w1[l])
        wv2 = wview(w2[l])
        for pj in range(2):
            K.dma("pool", wps[pj][0][:], wv1[:, :, pj * 512:(pj + 1) * 512], wps[pj][1])
        for q4i in range(4):
            K.dma("pool", w2s[:, q4i * 8:(q4i + 1) * 8, :], wv2[:, q4i * 8:(q4i + 1) * 8, :], B_w2)
        with ExitStack() as hs:
            hT, _ = sb(hs, [128, 8, T], BF16, "h2T")
            HB = [Buf("h2T%d" % i) for i in range(5)]
            ln_mod(l, 2, hT, HB, last)
            with ExitStack() as ps:
                rl = [sb(ps, [128, 512], F32, "rl") for _ in range(2)]
                ast = [sb(ps, [128, 512], BF16, "ast") for _ in range(3)]
                ip = 0
                for pj in range(8):
                    wt, B_w = wps[pj % 2]
                    if pj >= 2:
                        K.dma("pool", wt[:], wv1[:, :, pj * 512:(pj + 1) * 512], B_w)
                    for bi in blocks_q:
                        t0, nb = TB[bi]
                        for c4 in range(4):
                            ch = pj * 4 + c4
                            pp, B_pp = PF[ip % 4]
                            r_t, B_r = rl[ip % 2]
                            a_t, B_a = ast[ip % 3]
                            ip += 1

                            def mm1(e, wt=wt, pp=pp, c4=c4, t0=t0, nb=nb):
                                for kc in range(8):
                                    ins = e.matmul(pp[:, 0:nb], wt[:, kc, c4 * 128:(c4 + 1) * 128], hT[:, kc, t0:t0 + nb],
                                                   start=(kc == 0), stop=(kc == 7))
                                return ins
                            K.op("pe", mm1, [HB[bi], B_w], [B_pp])
                            K.op("act", lambda e, r_t=r_t, pp=pp, nb=nb: e.activation(out=r_t[:, 0:nb], in_=pp[:, 0:nb],
                                                                                     func=AF.Relu), [B_pp], [B_r])
                            K.op("dve", lambda e, r_t=r_t, a_t=a_t, nb=nb: e.tensor_tensor(
                                out=a_t[:, 0:nb], in0=r_t[:, 0:nb], in1=r_t[:, 0:nb], op=ALU.mult), [B_r], [B_a])
                            K.dma("sp", aT_d[ch * 128:(ch + 1) * 128, t0:t0 + nb], a_t[:, 0:nb], B_aT, src=B_a)
                K.barrier()

        w1_es.close()
        with ExitStack() as ps:
            lg, B_lg = bcast_row(ps, ln2_g[l:l + 1, :], D)
            lb, B_lb = bcast_row(ps, ln2_b[l:l + 1, :], D)
            g2 = []
            for r in range(2):
                g2.append(bcast_row(ps, mod_d[l, r:r + 1, 5120:6144], D, reads=[B_mod], pfx="g2"))
            ain = [sb(ps, [128, 32, 256], BF16, "ain") for _ in range(2)]
            stats = [[sb(ps, [128, 12], F32, "st"), sb(ps, [128, 2], F32, "mv"), sb(ps, [128, 2], F32, "rn")]
                     for _ in range(3)]
            rtmp = [sb(ps, [128, 512], F32, "rtmp") for _ in range(6)]
            aloaded = {}
            t0s = list(range(TC if last else 0, T, 256))

            def ensure_a(ia):
                if ia >= len(t0s) or ia in aloaded:
                    return
                at, B_at = ain[ia % 2]
                K.dma("sp", at[:], aT_d.rearrange("(c p) t -> p c t", p=128)[:, :, t0s[ia]:t0s[ia] + 256], B_at, [B_aT])
                aloaded[ia] = True

            def p11_tile(itile, ia, ti):
                ensure_a(ia)
                at, B_at = ain[ia % 2]
                tt = t0s[ia] // 128 + ti
                halves = []
                for hf in range(2):
                    pp, B_pp = PF[hf + 2 * (itile % 2)]

                    def mm2(e, pp=pp, hf=hf):
                        for kc in range(32):
                            ins = e.matmul(pp[:, :], at[:, kc, ti * 128:(ti + 1) * 128], w2s[:, kc, hf * 512:(hf + 1) * 512],
                                           start=(kc == 0), stop=(kc == 31))
                        return ins
                    K.op("pe", mm2, [B_at, B_w2], [B_pp])
                    yield
                    halves.append((pp[:, :], B_pp))
                if ti == 1:
                    ensure_a(ia + 1)
                gate_t, B_gate = g2[1 if tt < 2 else 0]
                yield from residual_ln(tt, halves, gate_t, B_gate, lg, B_lg, lb, B_lb, stats, rtmp, itile)
            gl = []
            itile = 0
            for ia in range(len(t0s)):
                for ti in range(2):
                    gl.append(p11_tile(itile, ia, ti))
                    itile += 1
            run_streams(gl, 2)
            K.barrier()
        w11.close()

    for tt in range(2, NTT):
        K.dma("sp", y_d[(tt - 2) * 128:(tt - 1) * 128, :], X[:, tt, :], B_y, src=XB[tt])
    K._wait("sp", B_y.w)
    K.barrier()


def _consts():
    c = {}
    c["ident"] = np.eye(128, dtype=np.float32)
    k = np.arange(128)
    ang = 2.0 * np.pi * ((k[:, None] * k[None, :]) % 128) / 128.0
    c["cs128"] = (np.concatenate([np.cos(ang), np.sin(ang)], axis=1) / np.sqrt(128.0)).astype(ml_dtypes.bfloat16)
    for S, suf in ((TX, ""), (TC, "256")):
        t = np.arange(S)
        a = 2.0 * np.pi * ((t[:, None] * t[None, :]) % S) / S
        cc_ = (np.cos(a) / np.sqrt(S)).astype(ml_dtypes.bfloat16)
        ss_ = (-np.sin(a) / np.sqrt(S)).astype(ml_dtypes.bfloat16)
        if S == TX:
            cc_ = np.ascontiguousarray(cc_.reshape(16, 128, 4, 512).transpose(2, 1, 0, 3))
            ss_ = np.ascontiguousarray(ss_.reshape(16, 128, 4, 512).transpose(2, 1, 0, 3))
        c["dftc" + suf] = cc_
        c["dfts" + suf] = ss_
    rows = (np.arange(TX) // 64).astype(np.float32)
    cols = (np.arange(TX) % 64).astype(np.float32)

    def angles(d_rot):
        n = d_rot // 4
        freqs = (np.float32(10000.0) ** (-np.arange(n, dtype=np.float32) / np.float32(n))).astype(np.float32)
        return np.concatenate([rows[:, None] * freqs, cols[:, None] * freqs], axis=-1).astype(np.float32)
    am = angles(32)
    ropeM = np.zeros((128, 2, T), np.float32)
    ropeM[:, 0, :TC] = 1.0
    j = np.arange(128) % 32 % 16
    ropeM[:, 0, TC:] = np.cos(am).astype(np.float32)[:, j].T
    ropeM[:, 1, TC:] = np.sin(am).astype(np.float32)[:, j].T
    c["ropeM"] = ropeM
    ag = angles(64)
    c["ropeG"] = np.stack([np.cos(ag), np.sin(ag)], axis=1).astype(np.float32)
    return c


_CACHE = {}


def _prep(inputs):
    f = lambda a: np.ascontiguousarray(np.asarray(a, dtype=np.float32))
    w_in = f(inputs["w_in"])
    tm_cols = np.concatenate([np.arange(0, 256), np.arange(288, 416), np.arange(416, 544),
                              np.arange(1056, 1312), np.arange(1312, 1824)])
    fm_cols = np.concatenate([np.arange(256, 288), np.arange(544, 1056), np.arange(1824, 4896)])
    sh = {
        "w_ada": f(inputs["w_ada"]), "b_ada": f(inputs["b_ada"]),
        "b_adaT": np.ascontiguousarray(f(inputs["b_ada"]).reshape(NL, 48, 128).transpose(0, 2, 1)),
        "w_tm": np.ascontiguousarray(w_in[:, :, tm_cols]), "w_fm": np.ascontiguousarray(w_in[:, :, fm_cols]),
        "b_gateT": np.ascontiguousarray(f(inputs["b_gate"]).reshape(NL, 24, 128).transpose(0, 2, 1)),
        "g_mq": f(inputs["mla_q_g"]), "g_mkv": f(inputs["mla_kv_g"]),
        "g_gq": np.ascontiguousarray(np.tile(f(inputs["gqa_q_g"]), (1, 8))),
        "g_gk": np.ascontiguousarray(np.tile(f(inputs["gqa_k_g"]), (1, 2))),
        "w_uq": f(inputs["w_uq"]), "w_uk": f(inputs["w_uk"]), "w_uv": f(inputs["w_uv"]),
        "w_fo": f(inputs["w_fo"]), "w_mo": f(inputs["w_mo"]), "w_go": f(inputs["w_go"]), "w_o": f(inputs["w_o"]),
        "ln1_g": f(inputs["ln1_g"]), "ln1_b": f(inputs["ln1_b"]), "ln2_g": f(inputs["ln2_g"]), "ln2_b": f(inputs["ln2_b"]),
        "w1": f(inputs["w1"]), "w2": f(inputs["w2"]),
    }
    sh.update(_consts())
    x = f(inputs["x"])
    ctx = f(inputs["ctx"])
    c = f(inputs["c"])
    c_ctx = f(inputs["c_ctx"])
    maps = []
    for b in range(8):
        m = dict(sh)
        m["xin"] = np.ascontiguousarray(np.concatenate([ctx[b], x[b]], axis=0))
        cc = np.stack([c[b], c_ctx], axis=0)
        m["ccT"] = np.ascontiguousarray(cc.reshape(2, 8, 128).transpose(2, 1, 0).reshape(128, 16))
        maps.append(m)
    return maps


def kernel(**inputs):
    maps = _prep(inputs)
    if "nc" not in _CACHE:
        _CACHE["nc"] = build(NL, False)
    res = run_bass_kernel_spmd(_CACHE["nc"], maps, core_ids=list(range(8)))
    return np.stack([np.asarray(r["y"], dtype=np.float32) for r in res.results], axis=0)
```

```python
import numpy as np
import ml_dtypes
from contextlib import ExitStack
import concourse.bass as bass
import concourse.mybir as mybir
from concourse.bass_utils import run_bass_kernel_spmd

F32 = mybir.dt.float32
BF16 = mybir.dt.bfloat16
AF = mybir.ActivationFunctionType
ALU = mybir.AluOpType
AX = mybir.AxisListType

D = 1024
T = 2304
TC = 256
TX = 2048
NTT = 18
TB = [(0, 256), (256, 512), (768, 512), (1280, 512), (1792, 512)]
EPS = 1e-6
ALPHA = float((2.0 * 4) ** 0.25)
NL = 4


class Buf:
    def __init__(self, name, dram=False):
        self.name = name
        self.w = None
        self.r = {}
        self.dram = dram
        self.sem = None
        self.phase = -1


class Rot:
    def __init__(self, items):
        self.items = list(items)
        self.i = 0

    def next(self):
        it = self.items[self.i % len(self.items)]
        self.i += 1
        return it


class KB:
    def __init__(self, nc, es, npool=56):
        self.nc = nc
        self.eng = {"pe": nc.tensor, "act": nc.scalar, "dve": nc.vector, "pool": nc.gpsimd, "sp": nc.sync}
        self.sem = {k: es.enter_context(nc.semaphore("s_" + k)) for k in ["pe", "act", "dve", "pool"]}
        self.cnt = {k: 0 for k in self.sem}
        self.waited = {}
        self.semcnt = {}
        self.pool = [es.enter_context(nc.semaphore("dq%d" % i)) for i in range(npool)]
        self.pool_i = 0
        self.phase = 0
        self.dram_sems = []
        self.es = es
        self.uid = 0

    def name(self, p):
        self.uid += 1
        return "%s_%d" % (p, self.uid)

    def dram_buf(self, name):
        return Buf(name, dram=True)

    def _wait(self, e, tok):
        if tok is None:
            return
        sem, val = tok
        key = (e, id(sem))
        if self.waited.get(key, 0) >= val:
            return
        self.eng[e].wait_ge(sem, val)
        self.waited[key] = val

    def _deps(self, e, reads, writes):
        for b in reads:
            self._wait(e, b.w)
        for b in writes:
            if not b.dram:
                self._wait(e, b.w)
            for t in list(b.r.values()):
                self._wait(e, t)

    def _post(self, tok, reads, writes):
        for b in reads:
            b.r[id(tok[0])] = tok
        for b in writes:
            b.w = tok
            if not b.dram:
                b.r = {}

    def op(self, e, fn, reads=(), writes=()):
        self._deps(e, reads, writes)
        ins = fn(self.eng[e])
        self.cnt[e] += 1
        ins.then_inc(self.sem[e], 1)
        self._post((self.sem[e], self.cnt[e]), reads, writes)

    def dma(self, q, out, in_, dst, reads=(), src=None, **kw):
        holder = dst
        if dst.dram:
            assert src is not None
            holder = src
        if holder.sem is None or holder.phase != self.phase:
            assert self.pool_i < len(self.pool), "dma sem pool exhausted"
            holder.sem = self.pool[self.pool_i]
            self.pool_i += 1
            holder.phase = self.phase
            self.semcnt.setdefault(id(holder.sem), 0)
        reads = list(reads)
        if src is not None and src not in reads:
            reads.append(src)
        self._deps(q, reads, [dst])
        ins = self.eng[q].dma_start(out=out, in_=in_, **kw)
        self.semcnt[id(holder.sem)] += 16
        ins.then_inc(holder.sem, 16)
        self._post((holder.sem, self.semcnt[id(holder.sem)]), reads, [dst])

    def barrier(self):
        toks = [(self.sem[k], self.cnt[k]) for k in self.sem if self.cnt[k] > 0]
        for s in self.pool[: self.pool_i]:
            c = self.semcnt.get(id(s), 0)
            if c > 0:
                toks.append((s, c))
        for e in self.eng:
            for t in toks:
                self._wait(e, t)
        self.pool_i = 0
        self.phase += 1


def build(n_layers=NL, debug=False):
    nc = bass.Bass("TRN2", target_bir_lowering=False)
    es = ExitStack()
    with es:
        _build(nc, es, n_layers, debug)
    return nc


def _build(nc, es, n_layers, debug):
    def din(name, shape, dt=F32):
        return nc.dram_tensor(name, list(shape), dt, kind="ExternalInput").ap()

    skind = "ExternalOutput" if debug else "Internal"

    def dscr(name, shape, dt=BF16):
        return nc.dram_tensor(name, list(shape), dt, kind=skind).ap()

    xin = din("xin", [T, D])
    ccT = din("ccT", [128, 16])
    w_ada = din("w_ada", [NL, D, 6144])
    b_ada = din("b_ada", [NL, 6144])
    b_adaT = din("b_adaT", [NL, 128, 48])
    w_tm = din("w_tm", [NL, D, 1280])
    w_fm = din("w_fm", [NL, D, 3616])
    b_gateT = din("b_gateT", [NL, 128, 24])
    g_mq = din("g_mq", [NL, 256])
    g_mkv = din("g_mkv", [NL, 256])
    g_gq = din("g_gq", [NL, 512])
    g_gk = din("g_gk", [NL, 128])
    w_uq = din("w_uq", [NL, 256, 768])
    w_uk = din("w_uk", [NL, 256, 512])
    w_uv = din("w_uv", [NL, 256, 512])
    w_fo = din("w_fo", [NL, 512, D])
    w_mo = din("w_mo", [NL, 512, D])
    w_go = din("w_go", [NL, 512, D])
    w_o = din("w_o", [NL, D, D])
    ln1_g = din("ln1_g", [NL, D])
    ln1_b = din("ln1_b", [NL, D])
    ln2_g = din("ln2_g", [NL, D])
    ln2_b = din("ln2_b", [NL, D])
    w1 = din("w1", [NL, D, 4096])
    w2 = din("w2", [NL, 4096, D])
    ident_d = din("ident", [128, 128])
    cs128_d = din("cs128", [128, 256], BF16)
    dftc_d = din("dftc", [4, 128, 16, 512], BF16)
    dfts_d = din("dfts", [4, 128, 16, 512], BF16)
    dftc256_d = din("dftc256", [TC, TC], BF16)
    dfts256_d = din("dfts256", [TC, TC], BF16)
    ropeM_d = din("ropeM", [128, 2, T])
    ropeG_d = din("ropeG", [TX, 2, 32])
    y_d = nc.dram_tensor("y", [TX, D], F32, kind="ExternalOutput").ap()

    mod_d = dscr("mod_d", [NL, 2, 6144], F32)
    kmT_d = dscr("kmT_d", [8, 96, T])
    qmT_d = dscr("qmT_d", [8, 96, T])
    vm_d = dscr("vm_d", [T, 512])
    kgT_d = dscr("kgT_d", [2, 64, T])
    qgT_d = dscr("qgT_d", [8, 64, T])
    vg_d = dscr("vg_d", [T, 128])
    ckvT_d = dscr("ckvT_d", [256, T])
    cqT_d = dscr("cqT_d", [256, T])
    z_d = dscr("z_d", [T, 4, 256])
    fmT_d = dscr("fmT_d", [512, T])
    amT_d = dscr("amT_d", [512, T])
    agT_d = dscr("agT_d", [512, T])
    gT_d = dscr("gT_d", [3072, T])
    aT_d = dscr("aT_d", [4096, T])

    K = KB(nc, es)
    B_mod = K.dram_buf("mod")
    B_kmT = K.dram_buf("kmT")
    B_qmT = K.dram_buf("qmT")
    B_vm = K.dram_buf("vm")
    B_kgT = K.dram_buf("kgT")
    B_qgT = K.dram_buf("qgT")
    B_vg = K.dram_buf("vg")
    B_ckvT = K.dram_buf("ckvT")
    B_cqT = K.dram_buf("cqT")
    B_z = K.dram_buf("z")
    B_fmT = K.dram_buf("fmT")
    B_amT = K.dram_buf("amT")
    B_agT = K.dram_buf("agT")
    B_gT = K.dram_buf("gT")
    B_aT = K.dram_buf("aT")
    B_y = K.dram_buf("y")

    def sb(stack, shape, dt, pfx="t"):
        t = stack.enter_context(nc.sbuf_tensor(K.name(pfx), list(shape), dt))
        return t, Buf(pfx)

    X, _ = sb(es, [128, NTT, D], F32, "X")
    XB = [Buf("X%d" % i) for i in range(NTT)]
    ident, B_ident = sb(es, [128, 128], BF16, "ident")
    modT, B_modT = sb(es, [128, NL, 48, 2], F32, "modT")
    epsc, B_eps = sb(es, [128, 1], F32, "epsc")
    K.op("pool", lambda e: e.memset(epsc[:], EPS), [], [B_eps])
    PFall = es.enter_context(nc.psum_tensor(K.name("pf"), [128, 6 * 512], F32))
    PF = [(PFall[:, i * 512:(i + 1) * 512], Buf("pf%d" % i)) for i in range(6)]
    PB = []
    for i in range(2):
        t = es.enter_context(nc.psum_tensor(K.name("pb"), [128, 1024], BF16))
        PB.append((t, Buf("pb%d" % i)))

    def wview(wl, p=128):
        return wl.rearrange("(kc p) n -> p kc n", p=p)

    for tt in range(NTT):
        K.dma("sp", X[:, tt, :], xin[tt * 128:(tt + 1) * 128, :], XB[tt])
    K.dma("pool", ident[:], ident_d, B_ident)

    sc_bf, B_sc = sb(es, [128, 16], BF16, "scbf")
    B_modTl = [Buf("modT%d" % i) for i in range(NL)]

    def adaln_bufs(stack):
        return dict(wb=[sb(stack, [128, 8, 512], BF16, "wada") for _ in range(2)],
                    brow=[sb(stack, [2, 512], F32, "brow") for _ in range(2)],
                    bT=sb(stack, [128, 48], F32, "bT"),
                    rows=[sb(stack, [2, 512], F32, "rows") for _ in range(2)])

    def adaln_gen(l, bufs):
        bT, B_bT = bufs["bT"]
        K.dma("sp", bT[:], b_adaT[l], B_bT)
        yield
        pc_t, B_pc = PF[2]
        wv = wview(w_ada[l])
        for j in range(12):
            wt, B_w = bufs["wb"][j % 2]
            K.dma("pool", wt[:], wv[:, :, j * 512:(j + 1) * 512], B_w)
            yield
            br, B_br = bufs["brow"][j % 2]
            K.dma("sp", br[:], b_ada[l:l + 1, j * 512:(j + 1) * 512].partition_broadcast(2)[:, 0, :], B_br)
            yield
            pr_t, B_pr = PF[j % 2]

            def mm_rows(e):
                for kc in range(8):
                    ins = e.matmul(pr_t[0:2, :], sc_bf[:, 2 * kc:2 * kc + 2], wt[:, kc, :],
                                   start=(kc == 0), stop=(kc == 7))
                return ins
            K.op("pe", mm_rows, [B_sc, B_w], [B_pr])
            yield
            rt, B_r = bufs["rows"][j % 2]
            K.op("dve", lambda e: e.tensor_tensor(out=rt[:], in0=pr_t[0:2, :], in1=br[:], op=ALU.add),
                 [B_pr, B_br], [B_r])
            yield
            K.dma("sp", mod_d[l, :, j * 512:(j + 1) * 512], rt[:], B_mod, src=B_r)
            yield
            for c4 in range(4):
                ch = j * 4 + c4

                def mm_cols(e, c4=c4, ch=ch):
                    for kc in range(8):
                        ins = e.matmul(pc_t[:, 2 * ch:2 * ch + 2], wt[:, kc, c4 * 128:(c4 + 1) * 128],
                                       sc_bf[:, 2 * kc:2 * kc + 2], start=(kc == 0), stop=(kc == 7))
                    return ins
                K.op("pe", mm_cols, [B_sc, B_w], [B_pc])
                yield
        K.op("dve", lambda e: e.tensor_tensor(
            out=modT[:, l, :, :], in0=pc_t[:, 0:96].rearrange("p (c r) -> p c r", r=2),
            in1=bT[:, :].unsqueeze(2).broadcast_to([128, 48, 2]), op=ALU.add), [B_pc, B_bT], [B_modTl[l]])
        yield
        for c0 in (8, 32):
            K.op("dve", lambda e, c0=c0: e.tensor_scalar_add(
                out=modT[:, l, c0:c0 + 8, :], in0=modT[:, l, c0:c0 + 8, :], scalar1=1.0), [B_modTl[l]], [B_modTl[l]])
            yield

    with ExitStack() as ps:
        cc_f, B_ccf = sb(ps, [128, 16], F32, "ccf")
        K.dma("sp", cc_f[:], ccT, B_ccf)
        K.op("act", lambda e: e.activation(out=sc_bf[:], in_=cc_f[:], func=AF.Silu), [B_ccf], [B_sc])
        for _ in adaln_gen(0, adaln_bufs(ps)):
            pass
        K.barrier()

    def run_streams(gens, width=2, extra=None):
        gens = iter(gens)
        active = []
        while True:
            while len(active) < width:
                g = next(gens, None)
                if g is None:
                    break
                active.append(g)
            if not active:
                break
            for g in list(active):
                try:
                    next(g)
                except StopIteration:
                    active.remove(g)
            if extra is not None:
                try:
                    next(extra)
                except StopIteration:
                    extra = None
        if extra is not None:
            for _ in extra:
                pass

    def ln_stats(stk_bufs, tt):
        (st, B_st), (mv, B_mv), (rn, B_rn) = stk_bufs
        K.op("dve", lambda e: e.bn_stats(out=st[:, 0:6], in_=X[:, tt, 0:512]), [XB[tt]], [B_st])
        yield
        K.op("dve", lambda e: e.bn_stats(out=st[:, 6:12], in_=X[:, tt, 512:1024]), [XB[tt]], [B_st])
        yield
        K.op("dve", lambda e: e.bn_aggr(out=mv[:, 0:2], in_=st[:, 0:12]), [B_st], [B_mv])
        yield
        K.op("act", lambda e: e.activation(out=rn[:, 0:1], in_=mv[:, 1:2], func=AF.Sqrt, bias=epsc[:, 0:1], scale=1.0),
             [B_mv, B_eps], [B_rn])
        yield
        K.op("dve", lambda e: e.reciprocal(out=rn[:, 0:1], in_=rn[:, 0:1]), [B_rn], [B_rn])
        yield
        K.op("dve", lambda e: e.scalar_tensor_tensor(out=rn[:, 1:2], in0=mv[:, 0:1], scalar=-1.0, in1=rn[:, 0:1],
                                                     op0=ALU.mult, op1=ALU.mult), [B_mv, B_rn], [B_rn])
        yield

    def ln_mod(l, sub, hT, HB, skip_ctx, extra_fn=None):
        sh0 = 0 if sub == 1 else 24
        sc0 = 8 if sub == 1 else 32
        with ExitStack() as ps:
            W = 2
            stats = [[sb(ps, [128, 12], F32, "st"), sb(ps, [128, 2], F32, "mv"), sb(ps, [128, 2], F32, "rn")]
                     for _ in range(W)]
            xn = [sb(ps, [128, D], BF16, "xn") for _ in range(W)]
            tmp = [sb(ps, [128, 8, 128], F32, "lt") for _ in range(W)]

            def tile(i, bi, tt):
                r = 1 if tt < 2 else 0
                yield from ln_stats(stats[i % W], tt)
                rn, B_rn = stats[i % W][2]
                xt, B_x = xn[i % W]
                K.op("act", lambda e: e.activation(
                    out=xt[:], in_=X[:, tt, :], func=AF.Identity, bias=rn[:, 1:2], scale=rn[:, 0:1]),
                    [XB[tt], B_rn], [B_x])
                yield
                pt, B_p = PB[i % 2]

                def tr(e):
                    for c in range(8):
                        ins = e.transpose(pt[:, c * 128:(c + 1) * 128], xt[:, c * 128:(c + 1) * 128], ident[:])
                    return ins
                K.op("pe", tr, [B_x, B_ident], [B_p])
                yield
                tm, B_t = tmp[i % W]
                A = modT[:, l, sc0:sc0 + 8, r:r + 1].broadcast_to([128, 8, 128])
                Bv = modT[:, l, sh0:sh0 + 8, r:r + 1].broadcast_to([128, 8, 128])
                K.op("dve", lambda e: e.tensor_tensor(
                    out=tm[:], in0=pt[:, :].rearrange("p (c t) -> p c t", t=128), in1=A, op=ALU.mult),
                    [B_p, B_modTl[l]], [B_t])
                yield
                K.op("dve", lambda e: e.tensor_tensor(
                    out=hT[:, :, tt * 128:(tt + 1) * 128], in0=tm[:], in1=Bv, op=ALU.add),
                    [B_t, B_modTl[l]], [HB[bi]])
                yield
            gl = []
            i = 0
            for bi, (t0, nb) in enumerate(TB):
                if skip_ctx and bi == 0:
                    continue
                for tt in range(t0 // 128, (t0 + nb) // 128):
                    gl.append(tile(i, bi, tt))
                    i += 1
            run_streams(gl, W, extra_fn(ps) if extra_fn is not None else None)
            K.barrier()

    def rms_rope(raw, B_raw, G, d, gain, B_gain, outbf, B_out, rope, tp):
        (sq, B_sq), (ss, B_ss), (nn, B_nn), (t1, B_t1), (t2, B_t2) = tp
        n = G * d
        K.op("pool", lambda e: e.tensor_tensor(out=sq[:, 0:n], in0=raw, in1=raw, op=ALU.mult), [B_raw], [B_sq])
        yield
        K.op("dve", lambda e: e.tensor_reduce(out=ss[:, 0:G], in_=sq[:, 0:n].rearrange("p (g d) -> p g d", d=d),
                                              axis=AX.X, op=ALU.add), [B_sq], [B_ss])
        yield
        K.op("act", lambda e: e.activation(out=ss[:, 0:G], in_=ss[:, 0:G], func=AF.Sqrt, bias=epsc[:, 0:1], scale=1.0 / d),
             [B_ss, B_eps], [B_ss])
        yield
        K.op("dve", lambda e: e.reciprocal(out=ss[:, 0:G], in_=ss[:, 0:G]), [B_ss], [B_ss])
        yield
        rb = ss[:, 0:G].unsqueeze(2).broadcast_to([128, G, d])
        K.op("dve", lambda e: e.tensor_tensor(out=nn[:, 0:n].rearrange("p (g d) -> p g d", d=d),
                                              in0=raw.rearrange("p (g d) -> p g d", d=d), in1=rb, op=ALU.mult),
             [B_raw, B_ss], [B_nn])
        yield
        if rope is None:
            K.op("pool", lambda e: e.tensor_tensor(out=outbf, in0=nn[:, 0:n], in1=gain, op=ALU.mult),
                 [B_nn, B_gain], [B_out])
            yield
            return
        cos, sin, B_rp = rope
        h = d // 2
        K.op("pool", lambda e: e.tensor_tensor(out=nn[:, 0:n], in0=nn[:, 0:n], in1=gain, op=ALU.mult),
             [B_nn, B_gain], [B_nn])
        yield
        n3 = nn[:, 0:n].rearrange("p (g d) -> p g d", d=d)
        o3 = outbf.rearrange("p (g d) -> p g d", d=d)
        cb = cos.unsqueeze(1).broadcast_to([128, G, h])
        sbb = sin.unsqueeze(1).broadcast_to([128, G, h])
        a1 = t1[:, 0:G * h].rearrange("p (g d) -> p g d", d=h)
        a2 = t2[:, 0:G * h].rearrange("p (g d) -> p g d", d=h)
        K.op("dve", lambda e: e.tensor_tensor(out=a1, in0=n3[:, :, 0:h], in1=cb, op=ALU.mult), [B_nn, B_rp], [B_t1])
        yield
        K.op("dve", lambda e: e.tensor_tensor(out=a2, in0=n3[:, :, h:d], in1=sbb, op=ALU.mult), [B_nn, B_rp], [B_t2])
        yield
        K.op("dve", lambda e: e.tensor_tensor(out=o3[:, :, 0:h], in0=a1, in1=a2, op=ALU.subtract),
             [B_t1, B_t2], [B_out])
        yield
        K.op("dve", lambda e: e.tensor_tensor(out=a1, in0=n3[:, :, 0:h], in1=sbb, op=ALU.mult), [B_nn, B_rp], [B_t1])
        yield
        K.op("dve", lambda e: e.tensor_tensor(out=a2, in0=n3[:, :, h:d], in1=cb, op=ALU.mult), [B_nn, B_rp], [B_t2])
        yield
        K.op("dve", lambda e: e.tensor_tensor(out=o3[:, :, h:d], in0=a1, in1=a2, op=ALU.add),
             [B_t1, B_t2], [B_out])
        yield

    def bcast_row(stack, src_row_ap, n, q="sp", reads=(), pfx="bc"):
        t, B = sb(stack, [128, n], F32, pfx)
        K.dma(q, t[:], src_row_ap.partition_broadcast(128)[:, 0, :], B, reads)
        return t, B

    for l in range(n_layers):
        last = (l == NL - 1)
        blocks_q = [bi for bi in range(5) if not (last and bi == 0)]
        tiles_q = list(range(2 if last else 0, NTT))

        with ExitStack() as hs:
            hT, _ = sb(hs, [128, 8, T], BF16, "hT")
            HB = [Buf("hT%d" % i) for i in range(5)]
            wt_es = ExitStack()
            wtm_b = [sb(wt_es, [128, 8, 512], BF16, "wtm") for _ in range(3)]
            wvl = wview(w_tm[l])
            for pi_, (c0_, nc_) in enumerate(((0, 512), (512, 256), (768, 512))):
                K.dma("pool", wtm_b[pi_][0][:, :, 0:nc_], wvl[:, :, c0_:c0_ + nc_], wtm_b[pi_][1])
            ln_mod(l, 1, hT, HB, False,
                   (lambda st, l=l: adaln_gen(l + 1, adaln_bufs(st))) if l + 1 < n_layers else None)
            if debug and l == 0:
                hdbg = nc.dram_tensor("hT_dbg", [128, 8, T], BF16, kind="ExternalOutput").ap()
                B_hd = K.dram_buf("hdbg")
                K.dma("sp", hdbg, hT[:], B_hd, src=HB[0])
                K.barrier()

            with ExitStack() as ps:
                gkv, B_gkv = bcast_row(ps, g_mkv[l:l + 1, :], 256)
                gq_, B_gq_ = bcast_row(ps, g_mq[l:l + 1, :], 256)
                ggq, B_ggq = bcast_row(ps, g_gq[l:l + 1, :], 512)
                ggk, B_ggk = bcast_row(ps, g_gk[l:l + 1, :], 128)
                rG, B_rG = sb(ps, [128, 16, 2, 32], F32, "ropeG")
                K.dma("sp", rG[:], ropeG_d.rearrange("(t p) a d -> p t a d", p=128), B_rG)
                wps = wtm_b
                raws = [sb(ps, [128, 512], F32, "raw") for _ in range(4)]
                tps = [[sb(ps, [128, 512], F32, "sq"), sb(ps, [128, 8], F32, "ss"), sb(ps, [128, 512], F32, "nn"),
                        sb(ps, [128, 256], F32, "t1"), sb(ps, [128, 256], F32, "t2")] for _ in range(4)]
                obf = [sb(ps, [128, 512], BF16, "obf") for _ in range(4)]
                stg = [sb(ps, [128, 4, 512], BF16, "stg") for _ in range(4)]
                vst = [sb(ps, [128, 128], BF16, "vst") for _ in range(4)]
                wvl = wview(w_tm[l])
                pieces = [("A", 0, 512), ("B", 512, 256), ("C", 768, 512)]
                W2 = 4
                PH = [(PB[i % 2][0][:, (i // 2) * 512:(i // 2 + 1) * 512], Buf("ph%d" % i)) for i in range(4)]
                blk_done = {}

                def p2_tile(it, pn, ncol, wt, B_w, bi, t0, nb, ti, tt, st_t, B_stg, key):
                    pp, B_pp = PF[it % W2]
                    raw, B_raw = raws[it % W2]
                    tp = tps[it % W2]
                    ob, B_ob = obf[it % W2]
                    ptb, B_ptb = PH[it % W2]

                    def mm(e):
                        for kc in range(8):
                            ins = e.matmul(pp[:, 0:ncol], hT[:, kc, tt * 128:(tt + 1) * 128],
                                           wt[:, kc, 0:ncol], start=(kc == 0), stop=(kc == 7))
                        return ins
                    K.op("pe", mm, [HB[bi], B_w], [B_pp])
                    yield
                    K.op("act", lambda e: e.copy(out=raw[:, 0:ncol], in_=pp[:, 0:ncol]), [B_pp], [B_raw])
                    yield
                    isx = tt >= 2
                    rp = (rG[:, tt - 2, 0, :], rG[:, tt - 2, 1, :], B_rG) if isx else None
                    if pn == "A":
                        yield from rms_rope(raw[:, 0:256], B_raw, 1, 256, gkv[:], B_gkv, ob[:, 0:256], B_ob, None, tp)
                        yield from rms_rope(raw[:, 256:384], B_raw, 2, 64, ggk[:], B_ggk, ob[:, 256:384], B_ob, rp, tp)
                        vt, B_v = vst[it % W2]
                        K.op("act", lambda e: e.copy(out=vt[:], in_=raw[:, 384:512]), [B_raw], [B_v])
                        yield
                        K.dma("sp", vg_d[tt * 128:(tt + 1) * 128, :], vt[:], B_vg, src=B_v)
                        ntr = 3
                    elif pn == "B":
                        yield from rms_rope(raw[:, 0:256], B_raw, 1, 256, gq_[:], B_gq_, ob[:, 0:256], B_ob, None, tp)
                        ntr = 2
                    else:
                        yield from rms_rope(raw[:, 0:512], B_raw, 8, 64, ggq[:], B_ggq, ob[:, 0:512], B_ob, rp, tp)
                        ntr = 4

                    def tr(e):
                        for c in range(ntr):
                            ins = e.transpose(ptb[:, c * 128:(c + 1) * 128], ob[:, c * 128:(c + 1) * 128], ident[:])
                        return ins
                    K.op("pe", tr, [B_ob, B_ident], [B_ptb])
                    yield
                    K.op("act", lambda e: e.copy(
                        out=st_t[:, 0:ntr, ti * 128:(ti + 1) * 128],
                        in_=ptb[:, 0:ntr * 128].rearrange("p (c t) -> p c t", t=128)), [B_ptb], [B_stg])
                    yield
                    blk_done[key] = blk_done.get(key, 0) + 1
                    if blk_done[key] == nb // 128:
                        if pn == "A":
                            K.dma("sp", ckvT_d.rearrange("(c p) t -> p c t", p=128)[:, :, t0:t0 + nb],
                                  st_t[:, 0:2, 0:nb], B_ckvT, src=B_stg)
                            for kv in range(2):
                                K.dma("sp", kgT_d[kv, :, t0:t0 + nb], st_t[kv * 64:(kv + 1) * 64, 2, 0:nb], B_kgT, src=B_stg)
                        elif pn == "B":
                            K.dma("sp", cqT_d.rearrange("(c p) t -> p c t", p=128)[:, :, t0:t0 + nb],
                                  st_t[:, 0:2, 0:nb], B_cqT, src=B_stg)
                        else:
                            for hh in range(8):
                                K.dma("sp", qgT_d[hh, :, t0:t0 + nb],
                                      st_t[(hh % 2) * 64:(hh % 2) * 64 + 64, hh // 2, 0:nb], B_qgT, src=B_stg)
                gl = []
                it = 0
                ib = 0
                for pi, (pn, c0, ncol) in enumerate(pieces):
                    wt, B_w = wps[pi % 3]
                    for bi, (t0, nb) in enumerate(TB):
                        if pn != "A" and bi not in blocks_q:
                            continue
                        st_t, B_stg = stg[ib % 4]
                        ib += 1
                        for ti, tt in enumerate(range(t0 // 128, (t0 + nb) // 128)):
                            gl.append(p2_tile(it, pn, ncol, wt, B_w, bi, t0, nb, ti, tt, st_t, B_stg, (pi, bi)))
                            it += 1
                run_streams(gl, W2)
                K.barrier()
            wt_es.close()

            with ExitStack() as ps:
                rM, B_rM = sb(ps, [128, 2, T], F32, "ropeM")
                K.dma("sp", rM[:], ropeM_d, B_rM)
                cs, B_cs = sb(ps, [128, 256], BF16, "cs128")
                K.dma("sp", cs[:], cs128_d, B_cs)
                bg, B_bg = sb(ps, [128, 24], F32, "bgate")
                K.dma("sp", bg[:], b_gateT[l], B_bg)
                wkr, B_wkr = sb(ps, [128, 8, 32], BF16, "wkr")
                wkrr, B_wkrr = sb(ps, [128, 8, 32], BF16, "wkrr")
                wfl = wview(w_fm[l])
                K.dma("pool", wkr[:], wfl[:, :, 0:32], B_wkr)
                K.op("dve", lambda e: e.tensor_scalar_mul(out=wkrr[:, :, 0:16], in0=wkr[:, :, 16:32], scalar1=-1.0),
                     [B_wkr], [B_wkrr])
                K.op("dve", lambda e: e.tensor_copy(out=wkrr[:, :, 16:32], in_=wkr[:, :, 0:16]), [B_wkr], [B_wkrr])
                kt1 = [sb(ps, [32, 512], F32, "kt1") for _ in range(2)]
                kt2 = [sb(ps, [32, 512], F32, "kt2") for _ in range(2)]
                kst = [sb(ps, [32, 512], BF16, "kst") for _ in range(2)]
                for bi, (t0, nb) in enumerate(TB):
                    pk, B_pk = PF[4]
                    pkr, B_pkr = PF[5]
                    for (w_, B_w_, p_, B_p_) in ((wkr, B_wkr, pk, B_pk), (wkrr, B_wkrr, pkr, B_pkr)):
                        def mmk(e, w_=w_, p_=p_, t0=t0, nb=nb):
                            for kc in range(8):
                                ins = e.matmul(p_[0:32, 0:nb], w_[:, kc, :], hT[:, kc, t0:t0 + nb],
                                               start=(kc == 0), stop=(kc == 7))
                            return ins
                        K.op("pe", mmk, [HB[bi], B_w_], [B_p_])
                    a1, B_a1 = kt1[bi % 2]
                    a2, B_a2 = kt2[bi % 2]
                    ks, B_ks = kst[bi % 2]
                    K.op("dve", lambda e, a1=a1, pk=pk, t0=t0, nb=nb: e.tensor_tensor(
                        out=a1[:, 0:nb], in0=pk[0:32, 0:nb], in1=rM[0:32, 0, t0:t0 + nb], op=ALU.mult),
                        [B_pk, B_rM], [B_a1])
                    K.op("dve", lambda e, a2=a2, pkr=pkr, t0=t0, nb=nb: e.tensor_tensor(
                        out=a2[:, 0:nb], in0=pkr[0:32, 0:nb], in1=rM[0:32, 1, t0:t0 + nb], op=ALU.mult),
                        [B_pkr, B_rM], [B_a2])
                    K.op("dve", lambda e, a1=a1, a2=a2, ks=ks, nb=nb: e.tensor_tensor(
                        out=ks[:, 0:nb], in0=a1[:, 0:nb], in1=a2[:, 0:nb], op=ALU.add), [B_a1, B_a2], [B_ks])
                    for hh in range(8):
                        K.dma("sp", kmT_d[hh, 64:96, t0:t0 + nb], ks[:, 0:nb], B_kmT, src=B_ks)

                wps = [sb(ps, [128, 8, 512], BF16, "wfm") for _ in range(2)]
                fT = [sb(ps, [128, 512], BF16, "fT") for _ in range(2)]
                zst = [sb(ps, [128, 4, 256], BF16, "zst") for _ in range(4)]
                gst = [sb(ps, [128, 512], BF16, "gst") for _ in range(3)]
                ip = 0
                for pj in range(7):
                    wt, B_w = wps[pj % 2]
                    K.dma("pool", wt[:], wfl[:, :, 32 + pj * 512:32 + (pj + 1) * 512], B_w)
                    for bi in blocks_q:
                        t0, nb = TB[bi]
                        for c4 in range(4):
                            pp, B_pp = PF[ip % 2]
                            ip += 1

                            def mm(e, wt=wt, pp=pp, c4=c4, t0=t0, nb=nb):
                                for kc in range(8):
                                    ins = e.matmul(pp[:, 0:nb], wt[:, kc, c4 * 128:(c4 + 1) * 128], hT[:, kc, t0:t0 + nb],
                                                   start=(kc == 0), stop=(kc == 7))
                                return ins
                            K.op("pe", mm, [HB[bi], B_w], [B_pp])
                            if pj == 0:
                                ft, B_ft = fT[c4 % 2]
                                K.op("act", lambda e, ft=ft, pp=pp, nb=nb: e.copy(out=ft[:, 0:nb], in_=pp[:, 0:nb]),
                                     [B_pp], [B_ft])
                                for ti in range(nb // 128):
                                    pz, B_pz = PF[2 + (ti % 2)]
                                    K.op("pe", lambda e, pz=pz, ft=ft, ti=ti: e.matmul(
                                        pz[:, 0:256], ft[:, ti * 128:(ti + 1) * 128], cs[:], start=True, stop=True),
                                        [B_ft, B_cs], [B_pz])
                                    zt, B_zt = zst[ti]
                                    K.op("dve", lambda e, zt=zt, pz=pz, c4=c4: e.tensor_copy(out=zt[:, c4, :], in_=pz[:, 0:256]),
                                         [B_pz], [B_zt])
                                    if c4 == 3:
                                        tt = t0 // 128 + ti
                                        K.dma("sp", z_d[tt * 128:(tt + 1) * 128, :, :], zt[:], B_z, src=B_zt)
                            else:
                                ch = (pj - 1) * 4 + c4
                                gt, B_gt = gst[ip % 3]
                                K.op("act", lambda e, gt=gt, pp=pp, nb=nb, ch=ch: e.activation(
                                    out=gt[:, 0:nb], in_=pp[:, 0:nb], func=AF.Sigmoid, bias=bg[:, ch:ch + 1], scale=1.0),
                                    [B_pp, B_bg], [B_gt])
                                K.dma("sp", gT_d[ch * 128:(ch + 1) * 128, t0:t0 + nb], gt[:, 0:nb], B_gT, src=B_gt)
                K.barrier()

                with ExitStack() as p4:
                    wuq, B_wuq = sb(p4, [128, 2, 768], BF16, "wuq")
                    wuqr, B_wuqr = sb(p4, [128, 2, 768], BF16, "wuqr")
                    wuk, B_wuk = sb(p4, [128, 2, 512], BF16, "wuk")
                    wuv, B_wuv = sb(p4, [128, 2, 512], BF16, "wuv")
                    K.dma("pool", wuq[:], wview(w_uq[l]), B_wuq)
                    K.dma("pool", wuk[:], wview(w_uk[l]), B_wuk)
                    K.dma("pool", wuv[:], wview(w_uv[l]), B_wuv)
                    K.op("dve", lambda e: e.tensor_copy(out=wuqr[:], in_=wuq[:]), [B_wuq], [B_wuqr])
                    q4 = wuq[:, :, :].rearrange("p j (h d) -> p j h d", d=96)
                    q4r = wuqr[:, :, :].rearrange("p j (h d) -> p j h d", d=96)
                    for jc in range(2):
                        K.op("dve", lambda e, jc=jc: e.tensor_scalar_mul(out=q4r[:, jc, :, 64:80], in0=q4[:, jc, :, 80:96],
                                                                          scalar1=-1.0), [B_wuq, B_wuqr], [B_wuqr])
                        K.op("dve", lambda e, jc=jc: e.tensor_copy(out=q4r[:, jc, :, 80:96], in_=q4[:, jc, :, 64:80]),
                             [B_wuq, B_wuqr], [B_wuqr])
                    cin = [sb(p4, [128, 2, 512], BF16, "cin") for _ in range(2)]
                    qst = [sb(p4, [128, 512], BF16, "qst") for _ in range(3)]
                    qa = [sb(p4, [128, 512], F32, "qa") for _ in range(3)]
                    qb = [sb(p4, [128, 512], F32, "qb") for _ in range(3)]
                    vstm = [sb(p4, [128, 512], BF16, "vstm") for _ in range(2)]
                    cq_loaded = {}

                    def ensure_cq(ii):
                        if ii >= len(blocks_q) or ii in cq_loaded:
                            return
                        t0, nb = TB[blocks_q[ii]]
                        ci, B_ci = cin[ii % 2]
                        K.dma("sp", ci[:, :, 0:nb], cqT_d.rearrange("(c p) t -> p c t", p=128)[:, :, t0:t0 + nb],
                              B_ci, [B_cqT])
                        cq_loaded[ii] = True

                    def q_head(ih, ii, hh):
                        ensure_cq(ii)
                        t0, nb = TB[blocks_q[ii]]
                        ci, B_ci = cin[ii % 2]
                        pq, B_pq = PF[(ih % 3) * 2]
                        pqr, B_pqr = PF[(ih % 3) * 2 + 1]
                        for (w_, B_w_, p_, B_p_) in ((wuq, B_wuq, pq, B_pq), (wuqr, B_wuqr, pqr, B_pqr)):
                            def mmq(e, w_=w_, p_=p_):
                                for jc in range(2):
                                    ins = e.matmul(p_[0:96, 0:nb], w_[:, jc, hh * 96:(hh + 1) * 96], ci[:, jc, 0:nb],
                                                   start=(jc == 0), stop=(jc == 1))
                                return ins
                            K.op("pe", mmq, [B_ci, B_w_], [B_p_])
                            yield
                        qs, B_qs = qst[ih % 3]
                        a1, B_a1 = qa[ih % 3]
                        a2, B_a2 = qb[ih % 3]
                        K.op("act", lambda e: e.copy(out=qs[0:64, 0:nb], in_=pq[0:64, 0:nb]), [B_pq], [B_qs])
                        yield
                        K.op("dve", lambda e: e.tensor_tensor(
                            out=a1[64:96, 0:nb], in0=pq[64:96, 0:nb], in1=rM[64:96, 0, t0:t0 + nb], op=ALU.mult),
                            [B_pq, B_rM], [B_a1])
                        yield
                        K.op("dve", lambda e: e.tensor_tensor(
                            out=a2[64:96, 0:nb], in0=pqr[64:96, 0:nb], in1=rM[64:96, 1, t0:t0 + nb], op=ALU.mult),
                            [B_pqr, B_rM], [B_a2])
                        yield
                        K.op("dve", lambda e: e.tensor_tensor(
                            out=qs[64:96, 0:nb], in0=a1[64:96, 0:nb], in1=a2[64:96, 0:nb], op=ALU.add),
                            [B_a1, B_a2], [B_qs])
                        yield
                        K.dma("sp", qmT_d[hh, :, t0:t0 + nb], qs[0:96, 0:nb], B_qmT, src=B_qs)
                        if hh == 4:
                            ensure_cq(ii + 1)
                    gl = []
                    ih = 0
                    for ii in range(len(blocks_q)):
                        for hh in range(8):
                            gl.append(q_head(ih, ii, hh))
                            ih += 1
                    run_streams(gl, 3)
                    for bi, (t0, nb) in enumerate(TB):
                        ci, B_ci = cin[bi % 2]
                        K.dma("sp", ci[:, :, 0:nb], ckvT_d.rearrange("(c p) t -> p c t", p=128)[:, :, t0:t0 + nb],
                              B_ci, [B_ckvT])
                        for hp in range(4):
                            pp, B_pp = PF[hp % 2]

                            def mmk2(e, pp=pp, hp=hp, ci=ci, nb=nb):
                                for jc in range(2):
                                    ins = e.matmul(pp[:, 0:nb], wuk[:, jc, hp * 128:(hp + 1) * 128], ci[:, jc, 0:nb],
                                                   start=(jc == 0), stop=(jc == 1))
                                return ins
                            K.op("pe", mmk2, [B_ci, B_wuk], [B_pp])
                            qs, B_qs = qst[hp % 3]
                            K.op("act", lambda e, qs=qs, pp=pp, nb=nb: e.copy(out=qs[:, 0:nb], in_=pp[:, 0:nb]), [B_pp], [B_qs])
                            for s in range(2):
                                K.dma("sp", kmT_d[2 * hp + s, 0:64, t0:t0 + nb], qs[s * 64:(s + 1) * 64, 0:nb], B_kmT, src=B_qs)
                        for ti in range(nb // 128):
                            tt = t0 // 128 + ti
                            pv, B_pv = PF[2 + ti % 2]

                            def mmv(e, pv=pv, ci=ci, ti=ti):
                                for jc in range(2):
                                    ins = e.matmul(pv[:, :], ci[:, jc, ti * 128:(ti + 1) * 128], wuv[:, jc, :],
                                                   start=(jc == 0), stop=(jc == 1))
                                return ins
                            K.op("pe", mmv, [B_ci, B_wuv], [B_pv])
                            vb, B_vb = vstm[ti % 2]
                            K.op("dve", lambda e, vb=vb, pv=pv: e.tensor_copy(out=vb[:], in_=pv[:, :]), [B_pv], [B_vb])
                            K.dma("sp", vm_d[tt * 128:(tt + 1) * 128, :], vb[:], B_vm, src=B_vb)
                    K.barrier()

        with ExitStack() as ps:
            Zs, B_Zs = sb(ps, [128, 16, 4, 256], BF16, "Zs")
            K.dma("sp", Zs[:], z_d[TC:T, :, :].rearrange("(tt p) g c -> p tt g c", p=128), B_Zs, [B_z])
            Ct = [sb(ps, [128, 16, 512], BF16, "Ct") for _ in range(2)]
            St = [sb(ps, [128, 16, 512], BF16, "St") for _ in range(2)]
            fst = [sb(ps, [128, 512], BF16, "fst") for _ in range(3)]
            ig = 0
            def load_tab(fb):
                if fb < 4:
                    K.dma("sp", Ct[fb % 2][0][:], dftc_d[fb], Ct[fb % 2][1])
                    K.dma("act", St[fb % 2][0][:], dfts_d[fb], St[fb % 2][1])
            load_tab(0)
            for fb in range(4):
                cT, B_c = Ct[fb % 2]
                sT, B_s = St[fb % 2]
                load_tab(fb + 1)
                for g in range(4):
                    py, B_py = PF[ig % 2]

                    def mmf(e, py=py, g=g, cT=cT, sT=sT):
                        for tt in range(16):
                            e.matmul(py[:, :], Zs[:, tt, g, 0:128], cT[:, tt, :], start=(tt == 0), stop=False)
                            ins = e.matmul(py[:, :], Zs[:, tt, g, 128:256], sT[:, tt, :], start=False, stop=(tt == 15))
                        return ins
                    K.op("pe", mmf, [B_Zs, B_c, B_s], [B_py])
                    ft, B_ft = fst[ig % 3]
                    ig += 1
                    K.op("act", lambda e, ft=ft, py=py: e.copy(out=ft[:], in_=py[:, :]), [B_py], [B_ft])
                    K.dma("sp", fmT_d[g * 128:(g + 1) * 128, TC + fb * 512:TC + (fb + 1) * 512], ft[:], B_fmT, src=B_ft)
            if not last:
                Zc, B_Zc = sb(ps, [128, 2, 4, 256], BF16, "Zc")
                K.dma("sp", Zc[:], z_d[0:TC, :, :].rearrange("(tt p) g c -> p tt g c", p=128), B_Zc, [B_z])
                c2, B_c2 = sb(ps, [128, 2, 256], BF16, "c256")
                s2, B_s2 = sb(ps, [128, 2, 256], BF16, "s256")
                K.dma("sp", c2[:], dftc256_d.rearrange("(tt p) f -> p tt f", p=128), B_c2)
                K.dma("sp", s2[:], dfts256_d.rearrange("(tt p) f -> p tt f", p=128), B_s2)
                for g in range(4):
                    py, B_py = PF[ig % 2]

                    def mmfc(e, py=py, g=g):
                        for tt in range(2):
                            e.matmul(py[:, 0:256], Zc[:, tt, g, 0:128], c2[:, tt, :], start=(tt == 0), stop=False)
                            ins = e.matmul(py[:, 0:256], Zc[:, tt, g, 128:256], s2[:, tt, :], start=False, stop=(tt == 1))
                        return ins
                    K.op("pe", mmfc, [B_Zc, B_c2, B_s2], [B_py])
                    ft, B_ft = fst[ig % 3]
                    ig += 1
                    K.op("act", lambda e, ft=ft, py=py: e.copy(out=ft[:, 0:256], in_=py[:, 0:256]), [B_py], [B_ft])
                    K.dma("sp", fmT_d[g * 128:(g + 1) * 128, 0:TC], ft[:, 0:256], B_fmT, src=B_ft)
            K.barrier()

        def attention(KT_d, B_KT, QT_d, B_QT, V_d, B_V, OUT_d, B_OUT, dk, scale, nkv_total, q_per_kv, kv_group):
            with ExitStack() as ps:
                groups = []
                for kv0 in range(0, nkv_total, kv_group):
                    nkv = kv_group
                    KT, B_K = sb(ps, [dk, nkv, T], BF16, "KT")
                    K.dma("sp", KT[:], KT_d[kv0:kv0 + nkv, :, :].rearrange("h d t -> d h t"), B_K, [B_KT])
                    VA, _ = sb(ps, [128, NTT, nkv, 128], BF16, "VA")
                    VAB = [Buf("VA%d" % i) for i in range(NTT)]
                    K.op("pool", lambda e, VA=VA: e.memset(VA[:, :, :, 64:128], 1.0), [], VAB)
                    for kt in range(NTT):
                        K.dma("sp", VA[:, kt, :, 0:64],
                              V_d[kt * 128:(kt + 1) * 128, kv0 * 64:(kv0 + nkv) * 64].rearrange("p (h d) -> p h d", d=64),
                              VAB[kt], [B_V])
                    groups.append((kv0, KT, B_K, VA, VAB))
                nkv = kv_group
                nq = nkv * q_per_kv
                Qs_all = [sb(ps, [dk, nq, 512], BF16, "Q") for _ in range(2)]
                PT = [sb(ps, [128, 2, 512], BF16, "PT") for _ in range(3)]
                rs = [sb(ps, [128, 512], F32, "rs") for _ in range(2)]
                ost = [sb(ps, [64, 512], BF16, "ost") for _ in range(2)]
                kglob = [0]
                for (kv0, KT, B_K, VA, VAB) in groups:
                    Qs = Qs_all
                    SP = [(PFall[:, 0:1024].rearrange("p (u n) -> p u n", n=512), [PF[0][1], PF[1][1]]),
                          (PFall[:, 1024:2048].rearrange("p (u n) -> p u n", n=512), [PF[2][1], PF[3][1]])]
                    items = []
                    qloaded = {}

                    def ensure_q(ii):
                        if ii >= len(blocks_q) or ii in qloaded:
                            return
                        t0, nb = TB[blocks_q[ii]]
                        Q, B_Q = Qs[ii % 2]
                        K.dma("sp", Q[:, :, 0:nb],
                              QT_d[kv0 * q_per_kv:kv0 * q_per_kv + nq, :, t0:t0 + nb].rearrange("h d t -> d h t"),
                              B_Q, [B_QT])
                        qloaded[ii] = True
                    hc = 0
                    for ii, bi in enumerate(blocks_q):
                        kts = [0, 1] if bi == 0 else list(range(NTT))
                        npair = len(kts) // 2
                        for j in range(nq):
                            for pi in range(npair):
                                items.append((ii, bi, j, pi, npair, kts[2 * pi:2 * pi + 2], hc))
                            hc += 1

                    def emit_S(k):
                        if k >= len(items):
                            return
                        ii, bi, j, pi, npair, pr, hcx = items[k]
                        ensure_q(ii)
                        t0, nb = TB[bi]
                        Q, B_Q = Qs[ii % 2]
                        S, SB = SP[k % 2]
                        kvj = j // q_per_kv

                        def f(e):
                            for u, kt in enumerate(pr):
                                ins = e.matmul(S[:, u, 0:nb], KT[:, kvj, kt * 128:(kt + 1) * 128], Q[:, j, 0:nb],
                                               start=True, stop=True)
                            return ins
                        K.op("pe", f, [B_K, B_Q], SB)
                    emit_S(0)
                    emit_S(1)
                    for k, (ii, bi, j, pi, npair, pr, hcx) in enumerate(items):
                        t0, nb = TB[bi]
                        S, SB = SP[k % 2]
                        pt, B_pt = PT[k % 3]
                        kvj = j // q_per_kv
                        hglob = kv0 * q_per_kv + j
                        pO, B_pO = PF[4 + hcx % 2]
                        K.op("act", lambda e: e.activation(out=pt[:, :, 0:nb], in_=S[:, :, 0:nb], func=AF.Exp, scale=scale),
                             SB, [B_pt])

                        def pv(e):
                            for u, kt in enumerate(pr):
                                ins = e.matmul(pO[:, 0:nb], VA[:, kt, kvj, :], pt[:, u, 0:nb],
                                               start=(pi == 0 and u == 0), stop=(pi == npair - 1 and u == 1))
                            return ins
                        K.op("pe", pv, [VAB[pr[0]], VAB[pr[1]], B_pt], [B_pO])
                        if pi == 0:
                            ensure_q(ii + 1)
                        emit_S(k + 2)
                        if pi == npair - 1:
                            rs_t, B_rs = rs[hcx % 2]
                            os_t, B_os = ost[hcx % 2]
                            K.op("dve", lambda e: e.reciprocal(out=rs_t[64:128, 0:nb], in_=pO[64:128, 0:nb]), [B_pO], [B_rs])
                            K.op("dve", lambda e: e.tensor_tensor(out=os_t[0:64, 0:nb], in0=pO[0:64, 0:nb],
                                                                  in1=rs_t[64:128, 0:nb], op=ALU.mult), [B_pO, B_rs], [B_os])
                            K.dma("sp", OUT_d[hglob * 64:(hglob + 1) * 64, t0:t0 + nb], os_t[0:64, 0:nb], B_OUT, src=B_os)

                K.barrier()

        attention(kmT_d, B_kmT, qmT_d, B_qmT, vm_d, B_vm, amT_d, B_amT, 96, 96 ** -0.5, 8, 1, 4)
        def attention_gqa():
            scale = 64 ** -0.5
            with ExitStack() as ps:
                KT2, B_K = sb(ps, [128, 2, T], BF16, "KT2")
                for half in range(2):
                    K.dma("sp", KT2[half * 64:(half + 1) * 64, :, :], kgT_d.rearrange("h d t -> d h t"), B_K, [B_kgT])
                VA, _ = sb(ps, [128, NTT, 2, 128], BF16, "VAg")
                VAB = [Buf("VAg%d" % i) for i in range(NTT)]
                K.op("pool", lambda e: e.memset(VA[:, :, :, 64:128], 1.0), [], VAB)
                for kt in range(NTT):
                    K.dma("sp", VA[:, kt, :, 0:64],
                          vg_d[kt * 128:(kt + 1) * 128, :].rearrange("p (h d) -> p h d", d=64), VAB[kt], [B_vg])
                Qs = [sb(ps, [128, 4, 512], BF16, "Qg") for _ in range(2)]
                PT = [sb(ps, [128, 2, 512], BF16, "PTg") for _ in range(3)]
                rs = [sb(ps, [128, 2, 512], F32, "rsg") for _ in range(2)]
                ost = [sb(ps, [64, 2, 512], BF16, "ostg") for _ in range(2)]
                SP = [(PFall[:, 0:1024].rearrange("p (u n) -> p u n", n=512), [PF[0][1], PF[1][1]]),
                      (PFall[:, 1024:2048].rearrange("p (u n) -> p u n", n=512), [PF[2][1], PF[3][1]])]
                qloaded = {}

                def ensure_q(ii):
                    if ii >= len(blocks_q) or ii in qloaded:
                        return
                    t0, nb = TB[blocks_q[ii]]
                    Q, B_Q = Qs[ii % 2]
                    K.dma("sp", Q[:, :, 0:nb], qgT_d.rearrange("h d t -> (h d) t").rearrange("(c p) t -> p c t", p=128)[:, :, t0:t0 + nb],
                          B_Q, [B_qgT])
                    qloaded[ii] = True
                items = []
                pc = 0
                for ii, bi in enumerate(blocks_q):
                    kts = [0, 1] if bi == 0 else list(range(NTT))
                    for c in range(4):
                        for ki, kt in enumerate(kts):
                            items.append((ii, bi, c, ki, len(kts), kt, pc))
                        pc += 1

                def emit_S(k):
                    if k >= len(items):
                        return
                    ii, bi, c, ki, nk, kt, pcx = items[k]
                    ensure_q(ii)
                    t0, nb = TB[bi]
                    Q, B_Q = Qs[ii % 2]
                    S, SB = SP[k % 2]
                    kv = c // 2

                    def f(e):
                        for u in range(2):
                            ins = e.matmul(S[:, u, 0:nb], KT2[u * 64:(u + 1) * 64, kv, kt * 128:(kt + 1) * 128],
                                           Q[u * 64:(u + 1) * 64, c, 0:nb], start=True, stop=True)
                        return ins
                    K.op("pe", f, [B_K, B_Q], SB)
                emit_S(0)
                emit_S(1)
                for k, (ii, bi, c, ki, nk, kt, pcx) in enumerate(items):
                    t0, nb = TB[bi]
                    S, SB = SP[k % 2]
                    pt, B_pt = PT[k % 3]
                    kv = c // 2
                    pOs = [PF[4], PF[5]]
                    K.op("act", lambda e: e.activation(out=pt[:, :, 0:nb], in_=S[:, :, 0:nb], func=AF.Exp, scale=scale),
                         SB, [B_pt])

                    def pv(e):
                        for u in range(2):
                            ins = e.matmul(pOs[u][0][:, 0:nb], VA[:, kt, kv, :], pt[:, u, 0:nb],
                                           start=(ki == 0), stop=(ki == nk - 1))
                        return ins
                    K.op("pe", pv, [VAB[kt], B_pt], [pOs[0][1], pOs[1][1]])
                    if ki == 0:
                        ensure_q(ii + 1)
                    emit_S(k + 2)
                    if ki == nk - 1:
                        rs_t, B_rs = rs[pcx % 2]
                        os_t, B_os = ost[pcx % 2]
                        O2 = PFall[:, 2048:3072].rearrange("p (u n) -> p u n", n=512)
                        K.op("act", lambda e: e.activation(out=rs_t[64:128, :, 0:nb], in_=O2[64:128, :, 0:nb], func=AF.Ln),
                             [pOs[0][1], pOs[1][1]], [B_rs])
                        K.op("act", lambda e: e.activation(out=rs_t[64:128, :, 0:nb], in_=rs_t[64:128, :, 0:nb], func=AF.Exp,
                                                           scale=-1.0), [B_rs], [B_rs])
                        K.op("dve", lambda e: e.tensor_tensor(out=os_t[0:64, :, 0:nb], in0=O2[0:64, :, 0:nb],
                                                              in1=rs_t[64:128, :, 0:nb], op=ALU.mult),
                             [pOs[0][1], pOs[1][1], B_rs], [B_os])
                        for u in range(2):
                            hglob = 2 * c + u
                            K.dma("sp", agT_d[hglob * 64:(hglob + 1) * 64, t0:t0 + nb], os_t[0:64, u, 0:nb], B_agT, src=B_os)
                K.barrier()

        w8 = ExitStack()
        wbr = []
        for nm, wsrc in (("wfo", w_fo), ("wmo", w_mo), ("wgo", w_go)):
            t, B = sb(w8, [128, 4, D], BF16, nm)
            K.dma("pool", t[:], wview(wsrc[l]), B)
            wbr.append((t, B))
        wo, B_wo = sb(w8, [128, 8, D], BF16, "wo")
        K.dma("pool", wo[:], wview(w_o[l]), B_wo)
        attention_gqa()

        def residual_ln(tt, halves, gate_t, B_gate, lg, B_lg, lb, B_lb, stats, tmps, idx):
            for hf, (pp, B_pp) in enumerate(halves):
                tm, B_t = tmps[(2 * idx + hf) % len(tmps)]
                K.op("dve", lambda e: e.tensor_tensor(
                    out=tm[:], in0=pp, in1=gate_t[:, hf * 512:(hf + 1) * 512], op=ALU.mult), [B_pp, B_gate], [B_t])
                yield
                K.op("dve", lambda e: e.scalar_tensor_tensor(
                    out=X[:, tt, hf * 512:(hf + 1) * 512], in0=X[:, tt, hf * 512:(hf + 1) * 512], scalar=ALPHA,
                    in1=tm[:], op0=ALU.mult, op1=ALU.add), [B_t, XB[tt]], [XB[tt]])
                yield
            yield from ln_stats(stats[idx % len(stats)], tt)
            rn, B_rn = stats[idx % len(stats)][2]
            K.op("act", lambda e: e.activation(out=X[:, tt, :], in_=X[:, tt, :], func=AF.Identity,
                                               bias=rn[:, 1:2], scale=rn[:, 0:1]), [XB[tt], B_rn], [XB[tt]])
            yield
            K.op("dve", lambda e: e.tensor_tensor(out=X[:, tt, :], in0=X[:, tt, :], in1=lg[:], op=ALU.mult),
                 [XB[tt], B_lg], [XB[tt]])
            yield
            K.op("pool", lambda e: e.tensor_tensor(out=X[:, tt, :], in0=X[:, tt, :], in1=lb[:], op=ALU.add),
                 [XB[tt], B_lb], [XB[tt]])
            yield

        with ExitStack() as ps:
            lg, B_lg = bcast_row(ps, ln1_g[l:l + 1, :], D)
            lb, B_lb = bcast_row(ps, ln1_b[l:l + 1, :], D)
            g1 = []
            for r in range(2):
                g1.append(bcast_row(ps, mod_d[l, r:r + 1, 2048:3072], D, reads=[B_mod], pfx="g1"))
            brin = [sb(ps, [128, 4, 512], BF16, "brin") for _ in range(3)]
            gin = [sb(ps, [128, 8, 512], BF16, "gin") for _ in range(2)]
            mg, B_mg = sb(ps, [128, 8, 512], F32, "mg")
            mtmp = [sb(ps, [128, 512], F32, "mtmp") for _ in range(2)]
            mbf, B_mbf = sb(ps, [128, 8, 512], BF16, "mbf")
            stats = [[sb(ps, [128, 12], F32, "st"), sb(ps, [128, 2], F32, "mv"), sb(ps, [128, 2], F32, "rn")]
                     for _ in range(2)]
            rtmp = [sb(ps, [128, 512], F32, "rtmp") for _ in range(4)]
            srcs = [(fmT_d, B_fmT), (amT_d, B_amT), (agT_d, B_agT)]
            ic = 0
            igin = 0
            itile = 0
            for bi in blocks_q:
                t0, nb = TB[bi]
                for br in range(3):
                    bt, B_bt = brin[br]
                    K.dma("sp", bt[:, :, 0:nb], srcs[br][0].rearrange("(c p) t -> p c t", p=128)[:, :, t0:t0 + nb],
                          B_bt, [srcs[br][1]])
                    gt, B_gt = gin[igin % 2]
                    igin += 1
                    K.dma("sp", gt[:, :, 0:nb],
                          gT_d[br * 1024:(br + 1) * 1024, :].rearrange("(c p) t -> p c t", p=128)[:, :, t0:t0 + nb],
                          B_gt, [B_gT])
                    wt, B_w = wbr[br]
                    for oc in range(8):
                        pp, B_pp = PF[ic % 2]
                        ic += 1

                        def mmb(e, pp=pp, wt=wt, bt=bt, oc=oc, nb=nb):
                            for kc in range(4):
                                ins = e.matmul(pp[:, 0:nb], wt[:, kc, oc * 128:(oc + 1) * 128], bt[:, kc, 0:nb],
                                               start=(kc == 0), stop=(kc == 3))
                            return ins
                        K.op("pe", mmb, [B_bt, B_w], [B_pp])
                        if br == 0:
                            K.op("dve", lambda e, pp=pp, gt=gt, oc=oc, nb=nb: e.tensor_tensor(
                                out=mg[:, oc, 0:nb], in0=pp[:, 0:nb], in1=gt[:, oc, 0:nb], op=ALU.mult),
                                [B_pp, B_gt], [B_mg])
                        else:
                            tm, B_t = mtmp[ic % 2]
                            K.op("dve", lambda e, tm=tm, pp=pp, gt=gt, oc=oc, nb=nb: e.tensor_tensor(
                                out=tm[:, 0:nb], in0=pp[:, 0:nb], in1=gt[:, oc, 0:nb], op=ALU.mult),
                                [B_pp, B_gt], [B_t])
                            if br == 1:
                                K.op("pool", lambda e, tm=tm, oc=oc, nb=nb: e.tensor_tensor(
                                    out=mg[:, oc, 0:nb], in0=mg[:, oc, 0:nb], in1=tm[:, 0:nb], op=ALU.add),
                                    [B_t, B_mg], [B_mg])
                            else:
                                K.op("pool", lambda e, tm=tm, oc=oc, nb=nb: e.tensor_tensor(
                                    out=mbf[:, oc, 0:nb], in0=mg[:, oc, 0:nb], in1=tm[:, 0:nb], op=ALU.add),
                                    [B_t, B_mg], [B_mbf])
                def p8_tile(itile, ti, tt):
                    halves = []
                    for hf in range(2):
                        pp, B_pp = PF[2 + hf + 2 * (itile % 2)]

                        def mmo(e, pp=pp, hf=hf):
                            for kc in range(8):
                                ins = e.matmul(pp[:, :], mbf[:, kc, ti * 128:(ti + 1) * 128], wo[:, kc, hf * 512:(hf + 1) * 512],
                                               start=(kc == 0), stop=(kc == 7))
                            return ins
                        K.op("pe", mmo, [B_mbf, B_wo], [B_pp])
                        yield
                        halves.append((pp[:, :], B_pp))
                    gate_t, B_gate = g1[1 if tt < 2 else 0]
                    yield from residual_ln(tt, halves, gate_t, B_gate, lg, B_lg, lb, B_lb, stats, rtmp, itile)
                gl = []
                for ti in range(nb // 128):
                    gl.append(p8_tile(itile, ti, t0 // 128 + ti))
                    itile += 1
                run_streams(gl, 2)
            K.barrier()
        w8.close()

        w11 = ExitStack()
        w2s, B_w2 = sb(w11, [128, 32, D], BF16, "w2s")
        w1_es = ExitStack()
        wps = [sb(w1_es, [128, 8, 512], BF16, "w1p") for _ in range(2)]
        wv1 = wview(w1[l])
        wv2 = wview(w2[l])
        for pj in range(2):
            K.dma("pool", wps[pj][0][:], wv1[:, :, pj * 512:(pj + 1) * 512], wps[pj][1])
        for q4i in range(4):
            K.dma("pool", w2s[:, q4i * 8:(q4i + 1) * 8, :], wv2[:, q4i * 8:(q4i + 1) * 8, :], B_w2)
        with ExitStack() as hs:
            hT, _ = sb(hs, [128, 8, T], BF16, "h2T")
            HB = [Buf("h2T%d" % i) for i in range(5)]
            ln_mod(l, 2, hT, HB, last)
            with ExitStack() as ps:
                rl = [sb(ps, [128, 512], F32, "rl") for _ in range(2)]
                ast = [sb(ps, [128, 512], BF16, "ast") for _ in range(3)]
                ip = 0
                for pj in range(8):
                    wt, B_w = wps[pj % 2]
                    if pj >= 2:
                        K.dma("pool", wt[:], wv1[:, :, pj * 512:(pj + 1) * 512], B_w)
                    for bi in blocks_q:
                        t0, nb = TB[bi]
                        for c4 in range(4):
                            ch = pj * 4 + c4
                            pp, B_pp = PF[ip % 4]
                            r_t, B_r = rl[ip % 2]
                            a_t, B_a = ast[ip % 3]
                            ip += 1

                            def mm1(e, wt=wt, pp=pp, c4=c4, t0=t0, nb=nb):
                                for kc in range(8):
                                    ins = e.matmul(pp[:, 0:nb], wt[:, kc, c4 * 128:(c4 + 1) * 128], hT[:, kc, t0:t0 + nb],
                                                   start=(kc == 0), stop=(kc == 7))
                                return ins
                            K.op("pe", mm1, [HB[bi], B_w], [B_pp])
                            K.op("act", lambda e, r_t=r_t, pp=pp, nb=nb: e.activation(out=r_t[:, 0:nb], in_=pp[:, 0:nb],
                                                                                     func=AF.Relu), [B_pp], [B_r])
                            K.op("dve", lambda e, r_t=r_t, a_t=a_t, nb=nb: e.tensor_tensor(
                                out=a_t[:, 0:nb], in0=r_t[:, 0:nb], in1=r_t[:, 0:nb], op=ALU.mult), [B_r], [B_a])
                            K.dma("sp", aT_d[ch * 128:(ch + 1) * 128, t0:t0 + nb], a_t[:, 0:nb], B_aT, src=B_a)
                K.barrier()

        w1_es.close()
        with ExitStack() as ps:
            lg, B_lg = bcast_row(ps, ln2_g[l:l + 1, :], D)
            lb, B_lb = bcast_row(ps, ln2_b[l:l + 1, :], D)
            g2 = []
            for r in range(2):
                g2.append(bcast_row(ps, mod_d[l, r:r + 1, 5120:6144], D, reads=[B_mod], pfx="g2"))
            ain = [sb(ps, [128, 32, 256], BF16, "ain") for _ in range(2)]
            stats = [[sb(ps, [128, 12], F32, "st"), sb(ps, [128, 2], F32, "mv"), sb(ps, [128, 2], F32, "rn")]
                     for _ in range(2)]
            rtmp = [sb(ps, [128, 512], F32, "rtmp") for _ in range(4)]
            aloaded = {}
            t0s = list(range(TC if last else 0, T, 256))

            def ensure_a(ia):
                if ia >= len(t0s) or ia in aloaded:
                    return
                at, B_at = ain[ia % 2]
                K.dma("sp", at[:], aT_d.rearrange("(c p) t -> p c t", p=128)[:, :, t0s[ia]:t0s[ia] + 256], B_at, [B_aT])
                aloaded[ia] = True

            def p11_tile(itile, ia, ti):
                ensure_a(ia)
                at, B_at = ain[ia % 2]
                tt = t0s[ia] // 128 + ti
                halves = []
                for hf in range(2):
                    pp, B_pp = PF[hf + 2 * (itile % 2)]

                    def mm2(e, pp=pp, hf=hf):
                        for kc in range(32):
                            ins = e.matmul(pp[:, :], at[:, kc, ti * 128:(ti + 1) * 128], w2s[:, kc, hf * 512:(hf + 1) * 512],
                                           start=(kc == 0), stop=(kc == 31))
                        return ins
                    K.op("pe", mm2, [B_at, B_w2], [B_pp])
                    yield
                    halves.append((pp[:, :], B_pp))
                if ti == 1:
                    ensure_a(ia + 1)
                gate_t, B_gate = g2[1 if tt < 2 else 0]
                yield from residual_ln(tt, halves, gate_t, B_gate, lg, B_lg, lb, B_lb, stats, rtmp, itile)
            gl = []
            itile = 0
            for ia in range(len(t0s)):
                for ti in range(2):
                    gl.append(p11_tile(itile, ia, ti))
                    itile += 1
            run_streams(gl, 2)
            K.barrier()
        w11.close()

    for tt in range(2, NTT):
        K.dma("sp", y_d[(tt - 2) * 128:(tt - 1) * 128, :], X[:, tt, :], B_y, src=XB[tt])
    K._wait("sp", B_y.w)
    K.barrier()


def _consts():
    c = {}
    c["ident"] = np.eye(128, dtype=np.float32)
    k = np.arange(128)
    ang = 2.0 * np.pi * ((k[:, None] * k[None, :]) % 128) / 128.0
    c["cs128"] = (np.concatenate([np.cos(ang), np.sin(ang)], axis=1) / np.sqrt(128.0)).astype(ml_dtypes.bfloat16)
    for S, suf in ((TX, ""), (TC, "256")):
        t = np.arange(S)
        a = 2.0 * np.pi * ((t[:, None] * t[None, :]) % S) / S
        cc_ = (np.cos(a) / np.sqrt(S)).astype(ml_dtypes.bfloat16)
        ss_ = (-np.sin(a) / np.sqrt(S)).astype(ml_dtypes.bfloat16)
        if S == TX:
            cc_ = np.ascontiguousarray(cc_.reshape(16, 128, 4, 512).transpose(2, 1, 0, 3))
            ss_ = np.ascontiguousarray(ss_.reshape(16, 128, 4, 512).transpose(2, 1, 0, 3))
        c["dftc" + suf] = cc_
        c["dfts" + suf] = ss_
    rows = (np.arange(TX) // 64).astype(np.float32)
    cols = (np.arange(TX) % 64).astype(np.float32)

    def angles(d_rot):
        n = d_rot // 4
        freqs = (np.float32(10000.0) ** (-np.arange(n, dtype=np.float32) / np.float32(n))).astype(np.float32)
        return np.concatenate([rows[:, None] * freqs, cols[:, None] * freqs], axis=-1).astype(np.float32)
    am = angles(32)
    ropeM = np.zeros((128, 2, T), np.float32)
    ropeM[:, 0, :TC] = 1.0
    j = np.arange(128) % 32 % 16
    ropeM[:, 0, TC:] = np.cos(am).astype(np.float32)[:, j].T
    ropeM[:, 1, TC:] = np.sin(am).astype(np.float32)[:, j].T
    c["ropeM"] = ropeM
    ag = angles(64)
    c["ropeG"] = np.stack([np.cos(ag), np.sin(ag)], axis=1).astype(np.float32)
    return c


_CACHE = {}


def _prep(inputs):
    f = lambda a: np.ascontiguousarray(np.asarray(a, dtype=np.float32))
    w_in = f(inputs["w_in"])
    tm_cols = np.concatenate([np.arange(0, 256), np.arange(288, 416), np.arange(416, 544),
                              np.arange(1056, 1312), np.arange(1312, 1824)])
    fm_cols = np.concatenate([np.arange(256, 288), np.arange(544, 1056), np.arange(1824, 4896)])
    sh = {
        "w_ada": f(inputs["w_ada"]), "b_ada": f(inputs["b_ada"]),
        "b_adaT": np.ascontiguousarray(f(inputs["b_ada"]).reshape(NL, 48, 128).transpose(0, 2, 1)),
        "w_tm": np.ascontiguousarray(w_in[:, :, tm_cols]), "w_fm": np.ascontiguousarray(w_in[:, :, fm_cols]),
        "b_gateT": np.ascontiguousarray(f(inputs["b_gate"]).reshape(NL, 24, 128).transpose(0, 2, 1)),
        "g_mq": f(inputs["mla_q_g"]), "g_mkv": f(inputs["mla_kv_g"]),
        "g_gq": np.ascontiguousarray(np.tile(f(inputs["gqa_q_g"]), (1, 8))),
        "g_gk": np.ascontiguousarray(np.tile(f(inputs["gqa_k_g"]), (1, 2))),
        "w_uq": f(inputs["w_uq"]), "w_uk": f(inputs["w_uk"]), "w_uv": f(inputs["w_uv"]),
        "w_fo": f(inputs["w_fo"]), "w_mo": f(inputs["w_mo"]), "w_go": f(inputs["w_go"]), "w_o": f(inputs["w_o"]),
        "ln1_g": f(inputs["ln1_g"]), "ln1_b": f(inputs["ln1_b"]), "ln2_g": f(inputs["ln2_g"]), "ln2_b": f(inputs["ln2_b"]),
        "w1": f(inputs["w1"]), "w2": f(inputs["w2"]),
    }
    sh.update(_consts())
    x = f(inputs["x"])
    ctx = f(inputs["ctx"])
    c = f(inputs["c"])
    c_ctx = f(inputs["c_ctx"])
    maps = []
    for b in range(8):
        m = dict(sh)
        m["xin"] = np.ascontiguousarray(np.concatenate([ctx[b], x[b]], axis=0))
        cc = np.stack([c[b], c_ctx], axis=0)
        m["ccT"] = np.ascontiguousarray(cc.reshape(2, 8, 128).transpose(2, 1, 0).reshape(128, 16))
        maps.append(m)
    return maps


def kernel(**inputs):
    maps = _prep(inputs)
    if "nc" not in _CACHE:
        _CACHE["nc"] = build(NL, False)
    res = run_bass_kernel_spmd(_CACHE["nc"], maps, core_ids=list(range(8)))
    return np.stack([np.asarray(r["y"], dtype=np.float32) for r in res.results], axis=0)
```

```python
import numpy as np
import ml_dtypes
from contextlib import ExitStack
import concourse.bass as bass
import concourse.mybir as mybir
from concourse.bass_utils import run_bass_kernel_spmd

F32 = mybir.dt.float32
BF16 = mybir.dt.bfloat16
AF = mybir.ActivationFunctionType
ALU = mybir.AluOpType
AX = mybir.AxisListType

D = 1024
T = 2304
TC = 256
TX = 2048
NTT = 18
TB = [(0, 256), (256, 512), (768, 512), (1280, 512), (1792, 512)]
EPS = 1e-6
ALPHA = float((2.0 * 4) ** 0.25)
NL = 4


class Buf:
    def __init__(self, name, dram=False):
        self.name = name
        self.w = None
        self.r = {}
        self.dram = dram
        self.sem = None
        self.phase = -1


class Rot:
    def __init__(self, items):
        self.items = list(items)
        self.i = 0

    def next(self):
        it = self.items[self.i % len(self.items)]
        self.i += 1
        return it


class KB:
    def __init__(self, nc, es, npool=56):
        self.nc = nc
        self.eng = {"pe": nc.tensor, "act": nc.scalar, "dve": nc.vector, "pool": nc.gpsimd, "sp": nc.sync}
        self.sem = {k: es.enter_context(nc.semaphore("s_" + k)) for k in ["pe", "act", "dve", "pool"]}
        self.cnt = {k: 0 for k in self.sem}
        self.waited = {}
        self.semcnt = {}
        self.pool = [es.enter_context(nc.semaphore("dq%d" % i)) for i in range(npool)]
        self.pool_i = 0
        self.phase = 0
        self.dram_sems = []
        self.es = es
        self.uid = 0

    def name(self, p):
        self.uid += 1
        return "%s_%d" % (p, self.uid)

    def dram_buf(self, name):
        return Buf(name, dram=True)

    def _wait(self, e, tok):
        if tok is None:
            return
        sem, val = tok
        key = (e, id(sem))
        if self.waited.get(key, 0) >= val:
            return
        self.eng[e].wait_ge(sem, val)
        self.waited[key] = val

    def _deps(self, e, reads, writes):
        for b in reads:
            self._wait(e, b.w)
        for b in writes:
            if not b.dram:
                self._wait(e, b.w)
            for t in list(b.r.values()):
                self._wait(e, t)

    def _post(self, tok, reads, writes):
        for b in reads:
            b.r[id(tok[0])] = tok
        for b in writes:
            b.w = tok
            if not b.dram:
                b.r = {}

    def op(self, e, fn, reads=(), writes=()):
        self._deps(e, reads, writes)
        ins = fn(self.eng[e])
        self.cnt[e] += 1
        ins.then_inc(self.sem[e], 1)
        self._post((self.sem[e], self.cnt[e]), reads, writes)

    def dma(self, q, out, in_, dst, reads=(), src=None, **kw):
        holder = dst
        if dst.dram:
            assert src is not None
            holder = src
        if holder.sem is None or holder.phase != self.phase:
            assert self.pool_i < len(self.pool), "dma sem pool exhausted"
            holder.sem = self.pool[self.pool_i]
            self.pool_i += 1
            holder.phase = self.phase
            self.semcnt.setdefault(id(holder.sem), 0)
        reads = list(reads)
        if src is not None and src not in reads:
            reads.append(src)
        self._deps(q, reads, [dst])
        ins = self.eng[q].dma_start(out=out, in_=in_, **kw)
        self.semcnt[id(holder.sem)] += 16
        ins.then_inc(holder.sem, 16)
        self._post((holder.sem, self.semcnt[id(holder.sem)]), reads, [dst])

    def barrier(self):
        toks = [(self.sem[k], self.cnt[k]) for k in self.sem if self.cnt[k] > 0]
        for s in self.pool[: self.pool_i]:
            c = self.semcnt.get(id(s), 0)
            if c > 0:
                toks.append((s, c))
        for e in self.eng:
            for t in toks:
                self._wait(e, t)
        self.pool_i = 0
        self.phase += 1


def build(n_layers=NL, debug=False):
    nc = bass.Bass("TRN2", target_bir_lowering=False)
    es = ExitStack()
    with es:
        _build(nc, es, n_layers, debug)
    return nc


def _build(nc, es, n_layers, debug):
    def din(name, shape, dt=F32):
        return nc.dram_tensor(name, list(shape), dt, kind="ExternalInput").ap()

    skind = "ExternalOutput" if debug else "Internal"

    def dscr(name, shape, dt=BF16):
        return nc.dram_tensor(name, list(shape), dt, kind=skind).ap()

    xin = din("xin", [T, D])
    ccT = din("ccT", [128, 16])
    w_ada = din("w_ada", [NL, D, 6144])
    b_ada = din("b_ada", [NL, 6144])
    b_adaT = din("b_adaT", [NL, 128, 48])
    w_tm = din("w_tm", [NL, D, 1280])
    w_fm = din("w_fm", [NL, D, 3616])
    b_gateT = din("b_gateT", [NL, 128, 24])
    g_mq = din("g_mq", [NL, 256])
    g_mkv = din("g_mkv", [NL, 256])
    g_gq = din("g_gq", [NL, 512])
    g_gk = din("g_gk", [NL, 128])
    w_uq = din("w_uq", [NL, 256, 768])
    w_uk = din("w_uk", [NL, 256, 512])
    w_uv = din("w_uv", [NL, 256, 512])
    w_fo = din("w_fo", [NL, 512, D])
    w_mo = din("w_mo", [NL, 512, D])
    w_go = din("w_go", [NL, 512, D])
    w_o = din("w_o", [NL, D, D])
    ln1_g = din("ln1_g", [NL, D])
    ln1_b = din("ln1_b", [NL, D])
    ln2_g = din("ln2_g", [NL, D])
    ln2_b = din("ln2_b", [NL, D])
    w1 = din("w1", [NL, D, 4096])
    w2 = din("w2", [NL, 4096, D])
    ident_d = din("ident", [128, 128])
    cs128_d = din("cs128", [128, 256], BF16)
    dftc_d = din("dftc", [4, 128, 16, 512], BF16)
    dfts_d = din("dfts", [4, 128, 16, 512], BF16)
    dftc256_d = din("dftc256", [TC, TC], BF16)
    dfts256_d = din("dfts256", [TC, TC], BF16)
    ropeM_d = din("ropeM", [128, 2, T])
    ropeG_d = din("ropeG", [TX, 2, 32])
    y_d = nc.dram_tensor("y", [TX, D], F32, kind="ExternalOutput").ap()

    mod_d = dscr("mod_d", [NL, 2, 6144], F32)
    kmT_d = dscr("kmT_d", [8, 96, T])
    qmT_d = dscr("qmT_d", [8, 96, T])
    vm_d = dscr("vm_d", [T, 512])
    kgT_d = dscr("kgT_d", [2, 64, T])
    qgT_d = dscr("qgT_d", [8, 64, T])
    vg_d = dscr("vg_d", [T, 128])
    ckvT_d = dscr("ckvT_d", [256, T])
    cqT_d = dscr("cqT_d", [256, T])
    z_d = dscr("z_d", [T, 4, 256])
    fmT_d = dscr("fmT_d", [512, T])
    amT_d = dscr("amT_d", [512, T])
    agT_d = dscr("agT_d", [512, T])
    gT_d = dscr("gT_d", [3072, T])
    aT_d = dscr("aT_d", [4096, T])

    K = KB(nc, es)
    B_mod = K.dram_buf("mod")
    B_kmT = K.dram_buf("kmT")
    B_qmT = K.dram_buf("qmT")
    B_vm = K.dram_buf("vm")
    B_kgT = K.dram_buf("kgT")
    B_qgT = K.dram_buf("qgT")
    B_vg = K.dram_buf("vg")
    B_ckvT = K.dram_buf("ckvT")
    B_cqT = K.dram_buf("cqT")
    B_z = K.dram_buf("z")
    B_fmT = K.dram_buf("fmT")
    B_amT = K.dram_buf("amT")
    B_agT = K.dram_buf("agT")
    B_gT = K.dram_buf("gT")
    B_aT = K.dram_buf("aT")
    B_y = K.dram_buf("y")

    def sb(stack, shape, dt, pfx="t"):
        t = stack.enter_context(nc.sbuf_tensor(K.name(pfx), list(shape), dt))
        return t, Buf(pfx)

    X, _ = sb(es, [128, NTT, D], F32, "X")
    XB = [Buf("X%d" % i) for i in range(NTT)]
    ident, B_ident = sb(es, [128, 128], BF16, "ident")
    modT, B_modT = sb(es, [128, NL, 48, 2], F32, "modT")
    epsc, B_eps = sb(es, [128, 1], F32, "epsc")
    K.op("pool", lambda e: e.memset(epsc[:], EPS), [], [B_eps])
    PFall = es.enter_context(nc.psum_tensor(K.name("pf"), [128, 6 * 512], F32))
    PF = [(PFall[:, i * 512:(i + 1) * 512], Buf("pf%d" % i)) for i in range(6)]
    PB = []
    for i in range(2):
        t = es.enter_context(nc.psum_tensor(K.name("pb"), [128, 1024], BF16))
        PB.append((t, Buf("pb%d" % i)))

    def wview(wl, p=128):
        return wl.rearrange("(kc p) n -> p kc n", p=p)

    for tt in range(NTT):
        K.dma("sp", X[:, tt, :], xin[tt * 128:(tt + 1) * 128, :], XB[tt])
    K.dma("pool", ident[:], ident_d, B_ident)

    sc_bf, B_sc = sb(es, [128, 16], BF16, "scbf")
    B_modTl = [Buf("modT%d" % i) for i in range(NL)]

    def adaln_bufs(stack):
        return dict(wb=[sb(stack, [128, 8, 512], BF16, "wada") for _ in range(2)],
                    brow=[sb(stack, [2, 512], F32, "brow") for _ in range(2)],
                    bT=sb(stack, [128, 48], F32, "bT"),
                    rows=[sb(stack, [2, 512], F32, "rows") for _ in range(2)])

    def adaln_gen(l, bufs):
        bT, B_bT = bufs["bT"]
        K.dma("sp", bT[:], b_adaT[l], B_bT)
        yield
        pc_t, B_pc = PF[2]
        wv = wview(w_ada[l])
        for j in range(12):
            wt, B_w = bufs["wb"][j % 2]
            K.dma("pool", wt[:], wv[:, :, j * 512:(j + 1) * 512], B_w)
            yield
            br, B_br = bufs["brow"][j % 2]
            K.dma("sp", br[:], b_ada[l:l + 1, j * 512:(j + 1) * 512].partition_broadcast(2)[:, 0, :], B_br)
            yield
            pr_t, B_pr = PF[j % 2]

            def mm_rows(e):
                for kc in range(8):
                    ins = e.matmul(pr_t[0:2, :], sc_bf[:, 2 * kc:2 * kc + 2], wt[:, kc, :],
                                   start=(kc == 0), stop=(kc == 7))
                return ins
            K.op("pe", mm_rows, [B_sc, B_w], [B_pr])
            yield
            rt, B_r = bufs["rows"][j % 2]
            K.op("dve", lambda e: e.tensor_tensor(out=rt[:], in0=pr_t[0:2, :], in1=br[:], op=ALU.add),
                 [B_pr, B_br], [B_r])
            yield
            K.dma("sp", mod_d[l, :, j * 512:(j + 1) * 512], rt[:], B_mod, src=B_r)
            yield
            for c4 in range(4):
                ch = j * 4 + c4

                def mm_cols(e, c4=c4, ch=ch):
                    for kc in range(8):
                        ins = e.matmul(pc_t[:, 2 * ch:2 * ch + 2], wt[:, kc, c4 * 128:(c4 + 1) * 128],
                                       sc_bf[:, 2 * kc:2 * kc + 2], start=(kc == 0), stop=(kc == 7))
                    return ins
                K.op("pe", mm_cols, [B_sc, B_w], [B_pc])
                yield
        K.op("dve", lambda e: e.tensor_tensor(
            out=modT[:, l, :, :], in0=pc_t[:, 0:96].rearrange("p (c r) -> p c r", r=2),
            in1=bT[:, :].unsqueeze(2).broadcast_to([128, 48, 2]), op=ALU.add), [B_pc, B_bT], [B_modTl[l]])
        yield
        for c0 in (8, 32):
            K.op("dve", lambda e, c0=c0: e.tensor_scalar_add(
                out=modT[:, l, c0:c0 + 8, :], in0=modT[:, l, c0:c0 + 8, :], scalar1=1.0), [B_modTl[l]], [B_modTl[l]])
            yield

    with ExitStack() as ps:
        cc_f, B_ccf = sb(ps, [128, 16], F32, "ccf")
        K.dma("sp", cc_f[:], ccT, B_ccf)
        K.op("act", lambda e: e.activation(out=sc_bf[:], in_=cc_f[:], func=AF.Silu), [B_ccf], [B_sc])
        for _ in adaln_gen(0, adaln_bufs(ps)):
            pass
        K.barrier()

    def run_streams(gens, width=2, extra=None):
        gens = iter(gens)
        active = []
        while True:
            while len(active) < width:
                g = next(gens, None)
                if g is None:
                    break
                active.append(g)
            if not active:
                break
            for g in list(active):
                try:
                    next(g)
                except StopIteration:
                    active.remove(g)
            if extra is not None:
                try:
                    next(extra)
                except StopIteration:
                    extra = None
        if extra is not None:
            for _ in extra:
                pass

    def ln_stats(stk_bufs, tt):
        (st, B_st), (mv, B_mv), (rn, B_rn) = stk_bufs
        K.op("dve", lambda e: e.bn_stats(out=st[:, 0:6], in_=X[:, tt, 0:512]), [XB[tt]], [B_st])
        yield
        K.op("dve", lambda e: e.bn_stats(out=st[:, 6:12], in_=X[:, tt, 512:1024]), [XB[tt]], [B_st])
        yield
        K.op("dve", lambda e: e.bn_aggr(out=mv[:, 0:2], in_=st[:, 0:12]), [B_st], [B_mv])
        yield
        K.op("act", lambda e: e.activation(out=rn[:, 0:1], in_=mv[:, 1:2], func=AF.Sqrt, bias=epsc[:, 0:1], scale=1.0),
             [B_mv, B_eps], [B_rn])
        yield
        K.op("dve", lambda e: e.reciprocal(out=rn[:, 0:1], in_=rn[:, 0:1]), [B_rn], [B_rn])
        yield
        K.op("dve", lambda e: e.scalar_tensor_tensor(out=rn[:, 1:2], in0=mv[:, 0:1], scalar=-1.0, in1=rn[:, 0:1],
                                                     op0=ALU.mult, op1=ALU.mult), [B_mv, B_rn], [B_rn])
        yield

    def ln_mod(l, sub, hT, HB, skip_ctx, extra_fn=None):
        sh0 = 0 if sub == 1 else 24
        sc0 = 8 if sub == 1 else 32
        with ExitStack() as ps:
            W = 2
            stats = [[sb(ps, [128, 12], F32, "st"), sb(ps, [128, 2], F32, "mv"), sb(ps, [128, 2], F32, "rn")]
                     for _ in range(W)]
            xn = [sb(ps, [128, D], BF16, "xn") for _ in range(W)]
            tmp = [sb(ps, [128, 8, 128], F32, "lt") for _ in range(W)]

            def tile(i, bi, tt):
                r = 1 if tt < 2 else 0
                yield from ln_stats(stats[i % W], tt)
                rn, B_rn = stats[i % W][2]
                xt, B_x = xn[i % W]
                K.op("act", lambda e: e.activation(
                    out=xt[:], in_=X[:, tt, :], func=AF.Identity, bias=rn[:, 1:2], scale=rn[:, 0:1]),
                    [XB[tt], B_rn], [B_x])
                yield
                pt, B_p = PB[i % 2]

                def tr(e):
                    for c in range(8):
                        ins = e.transpose(pt[:, c * 128:(c + 1) * 128], xt[:, c * 128:(c + 1) * 128], ident[:])
                    return ins
                K.op("pe", tr, [B_x, B_ident], [B_p])
                yield
                tm, B_t = tmp[i % W]
                A = modT[:, l, sc0:sc0 + 8, r:r + 1].broadcast_to([128, 8, 128])
                Bv = modT[:, l, sh0:sh0 + 8, r:r + 1].broadcast_to([128, 8, 128])
                K.op("dve", lambda e: e.tensor_tensor(
                    out=tm[:], in0=pt[:, :].rearrange("p (c t) -> p c t", t=128), in1=A, op=ALU.mult),
                    [B_p, B_modTl[l]], [B_t])
                yield
                K.op("dve", lambda e: e.tensor_tensor(
                    out=hT[:, :, tt * 128:(tt + 1) * 128], in0=tm[:], in1=Bv, op=ALU.add),
                    [B_t, B_modTl[l]], [HB[bi]])
                yield
            gl = []
            i = 0
            for bi, (t0, nb) in enumerate(TB):
                if skip_ctx and bi == 0:
                    continue
                for tt in range(t0 // 128, (t0 + nb) // 128):
                    gl.append(tile(i, bi, tt))
                    i += 1
            run_streams(gl, W, extra_fn(ps) if extra_fn is not None else None)
            K.barrier()

    def rms_rope(raw, B_raw, G, d, gain, B_gain, outbf, B_out, rope, tp):
        (sq, B_sq), (ss, B_ss), (nn, B_nn), (t1, B_t1), (t2, B_t2) = tp
        n = G * d
        K.op("pool", lambda e: e.tensor_tensor(out=sq[:, 0:n], in0=raw, in1=raw, op=ALU.mult), [B_raw], [B_sq])
        yield
        K.op("dve", lambda e: e.tensor_reduce(out=ss[:, 0:G], in_=sq[:, 0:n].rearrange("p (g d) -> p g d", d=d),
                                              axis=AX.X, op=ALU.add), [B_sq], [B_ss])
        yield
        K.op("act", lambda e: e.activation(out=ss[:, 0:G], in_=ss[:, 0:G], func=AF.Sqrt, bias=epsc[:, 0:1], scale=1.0 / d),
             [B_ss, B_eps], [B_ss])
        yield
        K.op("dve", lambda e: e.reciprocal(out=ss[:, 0:G], in_=ss[:, 0:G]), [B_ss], [B_ss])
        yield
        rb = ss[:, 0:G].unsqueeze(2).broadcast_to([128, G, d])
        K.op("dve", lambda e: e.tensor_tensor(out=nn[:, 0:n].rearrange("p (g d) -> p g d", d=d),
                                              in0=raw.rearrange("p (g d) -> p g d", d=d), in1=rb, op=ALU.mult),
             [B_raw, B_ss], [B_nn])
        yield
        if rope is None:
            K.op("pool", lambda e: e.tensor_tensor(out=outbf, in0=nn[:, 0:n], in1=gain, op=ALU.mult),
                 [B_nn, B_gain], [B_out])
            yield
            return
        cos, sin, B_rp = rope
        h = d // 2
        K.op("pool", lambda e: e.tensor_tensor(out=nn[:, 0:n], in0=nn[:, 0:n], in1=gain, op=ALU.mult),
             [B_nn, B_gain], [B_nn])
        yield
        n3 = nn[:, 0:n].rearrange("p (g d) -> p g d", d=d)
        o3 = outbf.rearrange("p (g d) -> p g d", d=d)
        cb = cos.unsqueeze(1).broadcast_to([128, G, h])
        sbb = sin.unsqueeze(1).broadcast_to([128, G, h])
        a1 = t1[:, 0:G * h].rearrange("p (g d) -> p g d", d=h)
        a2 = t2[:, 0:G * h].rearrange("p (g d) -> p g d", d=h)
        K.op("dve", lambda e: e.tensor_tensor(out=a1, in0=n3[:, :, 0:h], in1=cb, op=ALU.mult), [B_nn, B_rp], [B_t1])
        yield
        K.op("dve", lambda e: e.tensor_tensor(out=a2, in0=n3[:, :, h:d], in1=sbb, op=ALU.mult), [B_nn, B_rp], [B_t2])
        yield
        K.op("dve", lambda e: e.tensor_tensor(out=o3[:, :, 0:h], in0=a1, in1=a2, op=ALU.subtract),
             [B_t1, B_t2], [B_out])
        yield
        K.op("dve", lambda e: e.tensor_tensor(out=a1, in0=n3[:, :, 0:h], in1=sbb, op=ALU.mult), [B_nn, B_rp], [B_t1])
        yield
        K.op("dve", lambda e: e.tensor_tensor(out=a2, in0=n3[:, :, h:d], in1=cb, op=ALU.mult), [B_nn, B_rp], [B_t2])
        yield
        K.op("dve", lambda e: e.tensor_tensor(out=o3[:, :, h:d], in0=a1, in1=a2, op=ALU.add),
             [B_t1, B_t2], [B_out])
        yield

    def bcast_row(stack, src_row_ap, n, q="sp", reads=(), pfx="bc"):
        t, B = sb(stack, [128, n], F32, pfx)
        K.dma(q, t[:], src_row_ap.partition_broadcast(128)[:, 0, :], B, reads)
        return t, B

    for l in range(n_layers):
        last = (l == NL - 1)
        blocks_q = [bi for bi in range(5) if not (last and bi == 0)]
        tiles_q = list(range(2 if last else 0, NTT))

        with ExitStack() as hs:
            hT, _ = sb(hs, [128, 8, T], BF16, "hT")
            HB = [Buf("hT%d" % i) for i in range(5)]
            wt_es = ExitStack()
            wtm_b = [sb(wt_es, [128, 8, 512], BF16, "wtm") for _ in range(3)]
            wvl = wview(w_tm[l])
            for pi_, (c0_, nc_) in enumerate(((0, 512), (512, 256), (768, 512))):
                K.dma("pool", wtm_b[pi_][0][:, :, 0:nc_], wvl[:, :, c0_:c0_ + nc_], wtm_b[pi_][1])
            ln_mod(l, 1, hT, HB, False,
                   (lambda st, l=l: adaln_gen(l + 1, adaln_bufs(st))) if l + 1 < n_layers else None)
            if debug and l == 0:
                hdbg = nc.dram_tensor("hT_dbg", [128, 8, T], BF16, kind="ExternalOutput").ap()
                B_hd = K.dram_buf("hdbg")
                K.dma("sp", hdbg, hT[:], B_hd, src=HB[0])
                K.barrier()

            with ExitStack() as ps:
                gkv, B_gkv = bcast_row(ps, g_mkv[l:l + 1, :], 256)
                gq_, B_gq_ = bcast_row(ps, g_mq[l:l + 1, :], 256)
                ggq, B_ggq = bcast_row(ps, g_gq[l:l + 1, :], 512)
                ggk, B_ggk = bcast_row(ps, g_gk[l:l + 1, :], 128)
                rG, B_rG = sb(ps, [128, 16, 2, 32], F32, "ropeG")
                K.dma("sp", rG[:], ropeG_d.rearrange("(t p) a d -> p t a d", p=128), B_rG)
                wps = wtm_b
                raws = [sb(ps, [128, 512], F32, "raw") for _ in range(4)]
                tps = [[sb(ps, [128, 512], F32, "sq"), sb(ps, [128, 8], F32, "ss"), sb(ps, [128, 512], F32, "nn"),
                        sb(ps, [128, 256], F32, "t1"), sb(ps, [128, 256], F32, "t2")] for _ in range(4)]
                obf = [sb(ps, [128, 512], BF16, "obf") for _ in range(4)]
                stg = [sb(ps, [128, 4, 512], BF16, "stg") for _ in range(4)]
                vst = [sb(ps, [128, 128], BF16, "vst") for _ in range(4)]
                wvl = wview(w_tm[l])
                pieces = [("A", 0, 512), ("B", 512, 256), ("C", 768, 512)]
                W2 = 4
                PH = [(PB[i % 2][0][:, (i // 2) * 512:(i // 2 + 1) * 512], Buf("ph%d" % i)) for i in range(4)]
                blk_done = {}

                def p2_tile(it, pn, ncol, wt, B_w, bi, t0, nb, ti, tt, st_t, B_stg, key):
                    pp, B_pp = PF[it % W2]
                    raw, B_raw = raws[it % W2]
                    tp = tps[it % W2]
                    ob, B_ob = obf[it % W2]
                    ptb, B_ptb = PH[it % W2]

                    def mm(e):
                        for kc in range(8):
                            ins = e.matmul(pp[:, 0:ncol], hT[:, kc, tt * 128:(tt + 1) * 128],
                                           wt[:, kc, 0:ncol], start=(kc == 0), stop=(kc == 7))
                        return ins
                    K.op("pe", mm, [HB[bi], B_w], [B_pp])
                    yield
                    K.op("act", lambda e: e.copy(out=raw[:, 0:ncol], in_=pp[:, 0:ncol]), [B_pp], [B_raw])
                    yield
                    isx = tt >= 2
                    rp = (rG[:, tt - 2, 0, :], rG[:, tt - 2, 1, :], B_rG) if isx else None
                    if pn == "A":
                        yield from rms_rope(raw[:, 0:256], B_raw, 1, 256, gkv[:], B_gkv, ob[:, 0:256], B_ob, None, tp)
                        yield from rms_rope(raw[:, 256:384], B_raw, 2, 64, ggk[:], B_ggk, ob[:, 256:384], B_ob, rp, tp)
                        vt, B_v = vst[it % W2]
                        K.op("act", lambda e: e.copy(out=vt[:], in_=raw[:, 384:512]), [B_raw], [B_v])
                        yield
                        K.dma("sp", vg_d[tt * 128:(tt + 1) * 128, :], vt[:], B_vg, src=B_v)
                        ntr = 3
                    elif pn == "B":
                        yield from rms_rope(raw[:, 0:256], B_raw, 1, 256, gq_[:], B_gq_, ob[:, 0:256], B_ob, None, tp)
                        ntr = 2
                    else:
                        yield from rms_rope(raw[:, 0:512], B_raw, 8, 64, ggq[:], B_ggq, ob[:, 0:512], B_ob, rp, tp)
                        ntr = 4

                    def tr(e):
                        for c in range(ntr):
                            ins = e.transpose(ptb[:, c * 128:(c + 1) * 128], ob[:, c * 128:(c + 1) * 128], ident[:])
                        return ins
                    K.op("pe", tr, [B_ob, B_ident], [B_ptb])
                    yield
                    K.op("act", lambda e: e.copy(
                        out=st_t[:, 0:ntr, ti * 128:(ti + 1) * 128],
                        in_=ptb[:, 0:ntr * 128].rearrange("p (c t) -> p c t", t=128)), [B_ptb], [B_stg])
                    yield
                    blk_done[key] = blk_done.get(key, 0) + 1
                    if blk_done[key] == nb // 128:
                        if pn == "A":
                            K.dma("sp", ckvT_d.rearrange("(c p) t -> p c t", p=128)[:, :, t0:t0 + nb],
                                  st_t[:, 0:2, 0:nb], B_ckvT, src=B_stg)
                            for kv in range(2):
                                K.dma("sp", kgT_d[kv, :, t0:t0 + nb], st_t[kv * 64:(kv + 1) * 64, 2, 0:nb], B_kgT, src=B_stg)
                        elif pn == "B":
                            K.dma("sp", cqT_d.rearrange("(c p) t -> p c t", p=128)[:, :, t0:t0 + nb],
                                  st_t[:, 0:2, 0:nb], B_cqT, src=B_stg)
                        else:
                            for hh in range(8):
                                K.dma("sp", qgT_d[hh, :, t0:t0 + nb],
                                      st_t[(hh % 2) * 64:(hh % 2) * 64 + 64, hh // 2, 0:nb], B_qgT, src=B_stg)
                gl = []
                it = 0
                ib = 0
                for pi, (pn, c0, ncol) in enumerate(pieces):
                    wt, B_w = wps[pi % 3]
                    for bi, (t0, nb) in enumerate(TB):
                        if pn != "A" and bi not in blocks_q:
                            continue
                        st_t, B_stg = stg[ib % 4]
                        ib += 1
                        for ti, tt in enumerate(range(t0 // 128, (t0 + nb) // 128)):
                            gl.append(p2_tile(it, pn, ncol, wt, B_w, bi, t0, nb, ti, tt, st_t, B_stg, (pi, bi)))
                            it += 1
                run_streams(gl, W2)
                K.barrier()
            wt_es.close()

            with ExitStack() as ps:
                rM, B_rM = sb(ps, [128, 2, T], F32, "ropeM")
                K.dma("sp", rM[:], ropeM_d, B_rM)
                cs, B_cs = sb(ps, [128, 256], BF16, "cs128")
                K.dma("sp", cs[:], cs128_d, B_cs)
                bg, B_bg = sb(ps, [128, 24], F32, "bgate")
                K.dma("sp", bg[:], b_gateT[l], B_bg)
                wkr, B_wkr = sb(ps, [128, 8, 32], BF16, "wkr")
                wkrr, B_wkrr = sb(ps, [128, 8, 32], BF16, "wkrr")
                wfl = wview(w_fm[l])
                K.dma("pool", wkr[:], wfl[:, :, 0:32], B_wkr)
                K.op("dve", lambda e: e.tensor_scalar_mul(out=wkrr[:, :, 0:16], in0=wkr[:, :, 16:32], scalar1=-1.0),
                     [B_wkr], [B_wkrr])
                K.op("dve", lambda e: e.tensor_copy(out=wkrr[:, :, 16:32], in_=wkr[:, :, 0:16]), [B_wkr], [B_wkrr])
                kt1 = [sb(ps, [32, 512], F32, "kt1") for _ in range(2)]
                kt2 = [sb(ps, [32, 512], F32, "kt2") for _ in range(2)]
                kst = [sb(ps, [32, 512], BF16, "kst") for _ in range(2)]
                for bi, (t0, nb) in enumerate(TB):
                    pk, B_pk = PF[4]
                    pkr, B_pkr = PF[5]
                    for (w_, B_w_, p_, B_p_) in ((wkr, B_wkr, pk, B_pk), (wkrr, B_wkrr, pkr, B_pkr)):
                        def mmk(e, w_=w_, p_=p_, t0=t0, nb=nb):
                            for kc in range(8):
                                ins = e.matmul(p_[0:32, 0:nb], w_[:, kc, :], hT[:, kc, t0:t0 + nb],
                                               start=(kc == 0), stop=(kc == 7))
                            return ins
                        K.op("pe", mmk, [HB[bi], B_w_], [B_p_])
                    a1, B_a1 = kt1[bi % 2]
                    a2, B_a2 = kt2[bi % 2]
                    ks, B_ks = kst[bi % 2]
                    K.op("dve", lambda e, a1=a1, pk=pk, t0=t0, nb=nb: e.tensor_tensor(
                        out=a1[:, 0:nb], in0=pk[0:32, 0:nb], in1=rM[0:32, 0, t0:t0 + nb], op=ALU.mult),
                        [B_pk, B_rM], [B_a1])
                    K.op("dve", lambda e, a2=a2, pkr=pkr, t0=t0, nb=nb: e.tensor_tensor(
                        out=a2[:, 0:nb], in0=pkr[0:32, 0:nb], in1=rM[0:32, 1, t0:t0 + nb], op=ALU.mult),
                        [B_pkr, B_rM], [B_a2])
                    K.op("dve", lambda e, a1=a1, a2=a2, ks=ks, nb=nb: e.tensor_tensor(
                        out=ks[:, 0:nb], in0=a1[:, 0:nb], in1=a2[:, 0:nb], op=ALU.add), [B_a1, B_a2], [B_ks])
                    for hh in range(8):
                        K.dma("sp", kmT_d[hh, 64:96, t0:t0 + nb], ks[:, 0:nb], B_kmT, src=B_ks)

                wps = [sb(ps, [128, 8, 512], BF16, "wfm") for _ in range(2)]
                fT = [sb(ps, [128, 512], BF16, "fT") for _ in range(2)]
                zst = [sb(ps, [128, 4, 256], BF16, "zst") for _ in range(4)]
                gst = [sb(ps, [128, 512], BF16, "gst") for _ in range(3)]
                ip = 0
                for pj in range(7):
                    wt, B_w = wps[pj % 2]
                    K.dma("pool", wt[:], wfl[:, :, 32 + pj * 512:32 + (pj + 1) * 512], B_w)
                    for bi in blocks_q:
                        t0, nb = TB[bi]
                        for c4 in range(4):
                            pp, B_pp = PF[ip % 2]
                            ip += 1

                            def mm(e, wt=wt, pp=pp, c4=c4, t0=t0, nb=nb):
                                for kc in range(8):
                                    ins = e.matmul(pp[:, 0:nb], wt[:, kc, c4 * 128:(c4 + 1) * 128], hT[:, kc, t0:t0 + nb],
                                                   start=(kc == 0), stop=(kc == 7))
                                return ins
                            K.op("pe", mm, [HB[bi], B_w], [B_pp])
                            if pj == 0:
                                ft, B_ft = fT[c4 % 2]
                                K.op("act", lambda e, ft=ft, pp=pp, nb=nb: e.copy(out=ft[:, 0:nb], in_=pp[:, 0:nb]),
                                     [B_pp], [B_ft])
                                for ti in range(nb // 128):
                                    pz, B_pz = PF[2 + (ti % 2)]
                                    K.op("pe", lambda e, pz=pz, ft=ft, ti=ti: e.matmul(
                                        pz[:, 0:256], ft[:, ti * 128:(ti + 1) * 128], cs[:], start=True, stop=True),
                                        [B_ft, B_cs], [B_pz])
                                    zt, B_zt = zst[ti]
                                    K.op("dve", lambda e, zt=zt, pz=pz, c4=c4: e.tensor_copy(out=zt[:, c4, :], in_=pz[:, 0:256]),
                                         [B_pz], [B_zt])
                                    if c4 == 3:
                                        tt = t0 // 128 + ti
                                        K.dma("sp", z_d[tt * 128:(tt + 1) * 128, :, :], zt[:], B_z, src=B_zt)
                            else:
                                ch = (pj - 1) * 4 + c4
                                gt, B_gt = gst[ip % 3]
                                K.op("act", lambda e, gt=gt, pp=pp, nb=nb, ch=ch: e.activation(
                                    out=gt[:, 0:nb], in_=pp[:, 0:nb], func=AF.Sigmoid, bias=bg[:, ch:ch + 1], scale=1.0),
                                    [B_pp, B_bg], [B_gt])
                                K.dma("sp", gT_d[ch * 128:(ch + 1) * 128, t0:t0 + nb], gt[:, 0:nb], B_gT, src=B_gt)
                K.barrier()

                with ExitStack() as p4:
                    wuq, B_wuq = sb(p4, [128, 2, 768], BF16, "wuq")
                    wuqr, B_wuqr = sb(p4, [128, 2, 768], BF16, "wuqr")
                    wuk, B_wuk = sb(p4, [128, 2, 512], BF16, "wuk")
                    wuv, B_wuv = sb(p4, [128, 2, 512], BF16, "wuv")
                    K.dma("pool", wuq[:], wview(w_uq[l]), B_wuq)
                    K.dma("pool", wuk[:], wview(w_uk[l]), B_wuk)
                    K.dma("pool", wuv[:], wview(w_uv[l]), B_wuv)
                    K.op("dve", lambda e: e.tensor_copy(out=wuqr[:], in_=wuq[:]), [B_wuq], [B_wuqr])
                    q4 = wuq[:, :, :].rearrange("p j (h d) -> p j h d", d=96)
                    q4r = wuqr[:, :, :].rearrange("p j (h d) -> p j h d", d=96)
                    for jc in range(2):
                        K.op("dve", lambda e, jc=jc: e.tensor_scalar_mul(out=q4r[:, jc, :, 64:80], in0=q4[:, jc, :, 80:96],
                                                                          scalar1=-1.0), [B_wuq, B_wuqr], [B_wuqr])
                        K.op("dve", lambda e, jc=jc: e.tensor_copy(out=q4r[:, jc, :, 80:96], in_=q4[:, jc, :, 64:80]),
                             [B_wuq, B_wuqr], [B_wuqr])
                    cin = [sb(p4, [128, 2, 512], BF16, "cin") for _ in range(2)]
                    qst = [sb(p4, [128, 512], BF16, "qst") for _ in range(3)]
                    qa = [sb(p4, [128, 512], F32, "qa") for _ in range(3)]
                    qb = [sb(p4, [128, 512], F32, "qb") for _ in range(3)]
                    vstm = [sb(p4, [128, 512], BF16, "vstm") for _ in range(2)]
                    cq_loaded = {}

                    def ensure_cq(ii):
                        if ii >= len(blocks_q) or ii in cq_loaded:
                            return
                        t0, nb = TB[blocks_q[ii]]
                        ci, B_ci = cin[ii % 2]
                        K.dma("sp", ci[:, :, 0:nb], cqT_d.rearrange("(c p) t -> p c t", p=128)[:, :, t0:t0 + nb],
                              B_ci, [B_cqT])
                        cq_loaded[ii] = True

                    def q_head(ih, ii, hh):
                        ensure_cq(ii)
                        t0, nb = TB[blocks_q[ii]]
                        ci, B_ci = cin[ii % 2]
                        pq, B_pq = PF[(ih % 3) * 2]
                        pqr, B_pqr = PF[(ih % 3) * 2 + 1]
                        for (w_, B_w_, p_, B_p_) in ((wuq, B_wuq, pq, B_pq), (wuqr, B_wuqr, pqr, B_pqr)):
                            def mmq(e, w_=w_, p_=p_):
                                for jc in range(2):
                                    ins = e.matmul(p_[0:96, 0:nb], w_[:, jc, hh * 96:(hh + 1) * 96], ci[:, jc, 0:nb],
                                                   start=(jc == 0), stop=(jc == 1))
                                return ins
                            K.op("pe", mmq, [B_ci, B_w_], [B_p_])
                            yield
                        qs, B_qs = qst[ih % 3]
                        a1, B_a1 = qa[ih % 3]
                        a2, B_a2 = qb[ih % 3]
                        K.op("act", lambda e: e.copy(out=qs[0:64, 0:nb], in_=pq[0:64, 0:nb]), [B_pq], [B_qs])
                        yield
                        K.op("dve", lambda e: e.tensor_tensor(
                            out=a1[64:96, 0:nb], in0=pq[64:96, 0:nb], in1=rM[64:96, 0, t0:t0 + nb], op=ALU.mult),
                            [B_pq, B_rM], [B_a1])
                        yield
                        K.op("dve", lambda e: e.tensor_tensor(
                            out=a2[64:96, 0:nb], in0=pqr[64:96, 0:nb], in1=rM[64:96, 1, t0:t0 + nb], op=ALU.mult),
                            [B_pqr, B_rM], [B_a2])
                        yield
                        K.op("dve", lambda e: e.tensor_tensor(
                            out=qs[64:96, 0:nb], in0=a1[64:96, 0:nb], in1=a2[64:96, 0:nb], op=ALU.add),
                            [B_a1, B_a2], [B_qs])
                        yield
                        K.dma("sp", qmT_d[hh, :, t0:t0 + nb], qs[0:96, 0:nb], B_qmT, src=B_qs)
                        if hh == 4:
                            ensure_cq(ii + 1)
                    gl = []
                    ih = 0
                    for ii in range(len(blocks_q)):
                        for hh in range(8):
                            gl.append(q_head(ih, ii, hh))
                            ih += 1
                    run_streams(gl, 3)
                    for bi, (t0, nb) in enumerate(TB):
                        ci, B_ci = cin[bi % 2]
                        K.dma("sp", ci[:, :, 0:nb], ckvT_d.rearrange("(c p) t -> p c t", p=128)[:, :, t0:t0 + nb],
                              B_ci, [B_ckvT])
                        for hp in range(4):
                            pp, B_pp = PF[hp % 2]

                            def mmk2(e, pp=pp, hp=hp, ci=ci, nb=nb):
                                for jc in range(2):
                                    ins = e.matmul(pp[:, 0:nb], wuk[:, jc, hp * 128:(hp + 1) * 128], ci[:, jc, 0:nb],
                                                   start=(jc == 0), stop=(jc == 1))
                                return ins
                            K.op("pe", mmk2, [B_ci, B_wuk], [B_pp])
                            qs, B_qs = qst[hp % 3]
                            K.op("act", lambda e, qs=qs, pp=pp, nb=nb: e.copy(out=qs[:, 0:nb], in_=pp[:, 0:nb]), [B_pp], [B_qs])
                            for s in range(2):
                                K.dma("sp", kmT_d[2 * hp + s, 0:64, t0:t0 + nb], qs[s * 64:(s + 1) * 64, 0:nb], B_kmT, src=B_qs)
                        for ti in range(nb // 128):
                            tt = t0 // 128 + ti
                            pv, B_pv = PF[2 + ti % 2]

                            def mmv(e, pv=pv, ci=ci, ti=ti):
                                for jc in range(2):
                                    ins = e.matmul(pv[:, :], ci[:, jc, ti * 128:(ti + 1) * 128], wuv[:, jc, :],
                                                   start=(jc == 0), stop=(jc == 1))
                                return ins
                            K.op("pe", mmv, [B_ci, B_wuv], [B_pv])
                            vb, B_vb = vstm[ti % 2]
                            K.op("dve", lambda e, vb=vb, pv=pv: e.tensor_copy(out=vb[:], in_=pv[:, :]), [B_pv], [B_vb])
                            K.dma("sp", vm_d[tt * 128:(tt + 1) * 128, :], vb[:], B_vm, src=B_vb)
                    K.barrier()

        with ExitStack() as ps:
            Zs, B_Zs = sb(ps, [128, 16, 4, 256], BF16, "Zs")
            K.dma("sp", Zs[:], z_d[TC:T, :, :].rearrange("(tt p) g c -> p tt g c", p=128), B_Zs, [B_z])
            Ct = [sb(ps, [128, 16, 512], BF16, "Ct") for _ in range(2)]
            St = [sb(ps, [128, 16, 512], BF16, "St") for _ in range(2)]
            fst = [sb(ps, [128, 512], BF16, "fst") for _ in range(3)]
            ig = 0
            def load_tab(fb):
                if fb < 4:
                    K.dma("sp", Ct[fb % 2][0][:], dftc_d[fb], Ct[fb % 2][1])
                    K.dma("act", St[fb % 2][0][:], dfts_d[fb], St[fb % 2][1])
            load_tab(0)
            for fb in range(4):
                cT, B_c = Ct[fb % 2]
                sT, B_s = St[fb % 2]
                load_tab(fb + 1)
                for g in range(4):
                    py, B_py = PF[ig % 2]

                    def mmf(e, py=py, g=g, cT=cT, sT=sT):
                        for tt in range(16):
                            e.matmul(py[:, :], Zs[:, tt, g, 0:128], cT[:, tt, :], start=(tt == 0), stop=False)
                            ins = e.matmul(py[:, :], Zs[:, tt, g, 128:256], sT[:, tt, :], start=False, stop=(tt == 15))
                        return ins
                    K.op("pe", mmf, [B_Zs, B_c, B_s], [B_py])
                    ft, B_ft = fst[ig % 3]
                    ig += 1
                    K.op("act", lambda e, ft=ft, py=py: e.copy(out=ft[:], in_=py[:, :]), [B_py], [B_ft])
                    K.dma("sp", fmT_d[g * 128:(g + 1) * 128, TC + fb * 512:TC + (fb + 1) * 512], ft[:], B_fmT, src=B_ft)
            if not last:
                Zc, B_Zc = sb(ps, [128, 2, 4, 256], BF16, "Zc")
                K.dma("sp", Zc[:], z_d[0:TC, :, :].rearrange("(tt p) g c -> p tt g c", p=128), B_Zc, [B_z])
                c2, B_c2 = sb(ps, [128, 2, 256], BF16, "c256")
                s2, B_s2 = sb(ps, [128, 2, 256], BF16, "s256")
                K.dma("sp", c2[:], dftc256_d.rearrange("(tt p) f -> p tt f", p=128), B_c2)
                K.dma("sp", s2[:], dfts256_d.rearrange("(tt p) f -> p tt f", p=128), B_s2)
                for g in range(4):
                    py, B_py = PF[ig % 2]

                    def mmfc(e, py=py, g=g):
                        for tt in range(2):
                            e.matmul(py[:, 0:256], Zc[:, tt, g, 0:128], c2[:, tt, :], start=(tt == 0), stop=False)
                            ins = e.matmul(py[:, 0:256], Zc[:, tt, g, 128:256], s2[:, tt, :], start=False, stop=(tt == 1))
                        return ins
                    K.op("pe", mmfc, [B_Zc, B_c2, B_s2], [B_py])
                    ft, B_ft = fst[ig % 3]
                    ig += 1
                    K.op("act", lambda e, ft=ft, py=py: e.copy(out=ft[:, 0:256], in_=py[:, 0:256]), [B_py], [B_ft])
                    K.dma("sp", fmT_d[g * 128:(g + 1) * 128, 0:TC], ft[:, 0:256], B_fmT, src=B_ft)
            K.barrier()

        def attention(KT_d, B_KT, QT_d, B_QT, V_d, B_V, OUT_d, B_OUT, dk, scale, nkv_total, q_per_kv, kv_group):
            with ExitStack() as ps:
                groups = []
                for kv0 in range(0, nkv_total, kv_group):
                    nkv = kv_group
                    KT, B_K = sb(ps, [dk, nkv, T], BF16, "KT")
                    K.dma("sp", KT[:], KT_d[kv0:kv0 + nkv, :, :].rearrange("h d t -> d h t"), B_K, [B_KT])
                    VA, _ = sb(ps, [128, NTT, nkv, 128], BF16, "VA")
                    VAB = [Buf("VA%d" % i) for i in range(NTT)]
                    K.op("pool", lambda e, VA=VA: e.memset(VA[:, :, :, 64:128], 1.0), [], VAB)
                    for kt in range(NTT):
                        K.dma("sp" if kt % 2 == 0 else "act", VA[:, kt, :, 0:64],
                              V_d[kt * 128:(kt + 1) * 128, kv0 * 64:(kv0 + nkv) * 64].rearrange("p (h d) -> p h d", d=64),
                              VAB[kt], [B_V])
                    groups.append((kv0, KT, B_K, VA, VAB))
                nkv = kv_group
                nq = nkv * q_per_kv
                Qs_all = [sb(ps, [dk, nq, 512], BF16, "Q") for _ in range(2)]
                PT = [sb(ps, [128, 2, 512], BF16, "PT") for _ in range(3)]
                rs = [sb(ps, [128, 512], F32, "rs") for _ in range(2)]
                ost = [sb(ps, [64, 512], BF16, "ost") for _ in range(2)]
                kglob = [0]
                for (kv0, KT, B_K, VA, VAB) in groups:
                    Qs = Qs_all
                    SP = [(PFall[:, 0:1024].rearrange("p (u n) -> p u n", n=512), [PF[0][1], PF[1][1]]),
                          (PFall[:, 1024:2048].rearrange("p (u n) -> p u n", n=512), [PF[2][1], PF[3][1]])]
                    items = []
                    qloaded = {}

                    def ensure_q(ii):
                        if ii >= len(blocks_q) or ii in qloaded:
                            return
                        t0, nb = TB[blocks_q[ii]]
                        Q, B_Q = Qs[ii % 2]
                        K.dma("sp", Q[:, :, 0:nb],
                              QT_d[kv0 * q_per_kv:kv0 * q_per_kv + nq, :, t0:t0 + nb].rearrange("h d t -> d h t"),
                              B_Q, [B_QT])
                        qloaded[ii] = True
                    hc = 0
                    for ii, bi in enumerate(blocks_q):
                        kts = [0, 1] if bi == 0 else list(range(NTT))
                        npair = len(kts) // 2
                        for j in range(nq):
                            for pi in range(npair):
                                items.append((ii, bi, j, pi, npair, kts[2 * pi:2 * pi + 2], hc))
                            hc += 1

                    def emit_S(k):
                        if k >= len(items):
                            return
                        ii, bi, j, pi, npair, pr, hcx = items[k]
                        ensure_q(ii)
                        t0, nb = TB[bi]
                        Q, B_Q = Qs[ii % 2]
                        S, SB = SP[k % 2]
                        kvj = j // q_per_kv

                        def f(e):
                            for u, kt in enumerate(pr):
                                ins = e.matmul(S[:, u, 0:nb], KT[:, kvj, kt * 128:(kt + 1) * 128], Q[:, j, 0:nb],
                                               start=True, stop=True)
                            return ins
                        K.op("pe", f, [B_K, B_Q], SB)
                    emit_S(0)
                    emit_S(1)
                    for k, (ii, bi, j, pi, npair, pr, hcx) in enumerate(items):
                        t0, nb = TB[bi]
                        S, SB = SP[k % 2]
                        pt, B_pt = PT[k % 3]
                        kvj = j // q_per_kv
                        hglob = kv0 * q_per_kv + j
                        pO, B_pO = PF[4 + hcx % 2]
                        K.op("act", lambda e: e.activation(out=pt[:, :, 0:nb], in_=S[:, :, 0:nb], func=AF.Exp, scale=scale),
                             SB, [B_pt])

                        def pv(e):
                            for u, kt in enumerate(pr):
                                ins = e.matmul(pO[:, 0:nb], VA[:, kt, kvj, :], pt[:, u, 0:nb],
                                               start=(pi == 0 and u == 0), stop=(pi == npair - 1 and u == 1))
                            return ins
                        K.op("pe", pv, [VAB[pr[0]], VAB[pr[1]], B_pt], [B_pO])
                        if pi == 0:
                            ensure_q(ii + 1)
                        emit_S(k + 2)
                        if pi == npair - 1:
                            rs_t, B_rs = rs[hcx % 2]
                            os_t, B_os = ost[hcx % 2]
                            K.op("dve", lambda e: e.reciprocal(out=rs_t[64:128, 0:nb], in_=pO[64:128, 0:nb]), [B_pO], [B_rs])
                            K.op("dve", lambda e: e.tensor_tensor(out=os_t[0:64, 0:nb], in0=pO[0:64, 0:nb],
                                                                  in1=rs_t[64:128, 0:nb], op=ALU.mult), [B_pO, B_rs], [B_os])
                            K.dma("sp", OUT_d[hglob * 64:(hglob + 1) * 64, t0:t0 + nb], os_t[0:64, 0:nb], B_OUT, src=B_os)

                K.barrier()

        attention(kmT_d, B_kmT, qmT_d, B_qmT, vm_d, B_vm, amT_d, B_amT, 96, 96 ** -0.5, 8, 1, 4)
        def attention_gqa():
            scale = 64 ** -0.5
            with ExitStack() as ps:
                KT2, B_K = sb(ps, [128, 2, T], BF16, "KT2")
                for half in range(2):
                    K.dma("sp", KT2[half * 64:(half + 1) * 64, :, :], kgT_d.rearrange("h d t -> d h t"), B_K, [B_kgT])
                VA, _ = sb(ps, [128, NTT, 2, 128], BF16, "VAg")
                VAB = [Buf("VAg%d" % i) for i in range(NTT)]
                K.op("pool", lambda e: e.memset(VA[:, :, :, 64:128], 1.0), [], VAB)
                for kt in range(NTT):
                    K.dma("sp" if kt % 2 == 0 else "act", VA[:, kt, :, 0:64],
                          vg_d[kt * 128:(kt + 1) * 128, :].rearrange("p (h d) -> p h d", d=64), VAB[kt], [B_vg])
                Qs = [sb(ps, [128, 4, 512], BF16, "Qg") for _ in range(2)]
                PT = [sb(ps, [128, 2, 512], BF16, "PTg") for _ in range(3)]
                rs = [sb(ps, [128, 2, 512], F32, "rsg") for _ in range(2)]
                ost = [sb(ps, [64, 2, 512], BF16, "ostg") for _ in range(2)]
                SP = [(PFall[:, 0:1024].rearrange("p (u n) -> p u n", n=512), [PF[0][1], PF[1][1]]),
                      (PFall[:, 1024:2048].rearrange("p (u n) -> p u n", n=512), [PF[2][1], PF[3][1]])]
                qloaded = {}

                def ensure_q(ii):
                    if ii >= len(blocks_q) or ii in qloaded:
                        return
                    t0, nb = TB[blocks_q[ii]]
                    Q, B_Q = Qs[ii % 2]
                    K.dma("sp", Q[:, :, 0:nb], qgT_d.rearrange("h d t -> (h d) t").rearrange("(c p) t -> p c t", p=128)[:, :, t0:t0 + nb],
                          B_Q, [B_qgT])
                    qloaded[ii] = True
                items = []
                pc = 0
                for ii, bi in enumerate(blocks_q):
                    kts = [0, 1] if bi == 0 else list(range(NTT))
                    for c in range(4):
                        for ki, kt in enumerate(kts):
                            items.append((ii, bi, c, ki, len(kts), kt, pc))
                        pc += 1

                def emit_S(k):
                    if k >= len(items):
                        return
                    ii, bi, c, ki, nk, kt, pcx = items[k]
                    ensure_q(ii)
                    t0, nb = TB[bi]
                    Q, B_Q = Qs[ii % 2]
                    S, SB = SP[k % 2]
                    kv = c // 2

                    def f(e):
                        for u in range(2):
                            ins = e.matmul(S[:, u, 0:nb], KT2[u * 64:(u + 1) * 64, kv, kt * 128:(kt + 1) * 128],
                                           Q[u * 64:(u + 1) * 64, c, 0:nb], start=True, stop=True)
                        return ins
                    K.op("pe", f, [B_K, B_Q], SB)
                emit_S(0)
                emit_S(1)
                for k, (ii, bi, c, ki, nk, kt, pcx) in enumerate(items):
                    t0, nb = TB[bi]
                    S, SB = SP[k % 2]
                    pt, B_pt = PT[k % 3]
                    kv = c // 2
                    pOs = [PF[4], PF[5]]
                    K.op("act", lambda e: e.activation(out=pt[:, :, 0:nb], in_=S[:, :, 0:nb], func=AF.Exp, scale=scale),
                         SB, [B_pt])

                    def pv(e):
                        for u in range(2):
                            ins = e.matmul(pOs[u][0][:, 0:nb], VA[:, kt, kv, :], pt[:, u, 0:nb],
                                           start=(ki == 0), stop=(ki == nk - 1))
                        return ins
                    K.op("pe", pv, [VAB[kt], B_pt], [pOs[0][1], pOs[1][1]])
                    if ki == 0:
                        ensure_q(ii + 1)
                    emit_S(k + 2)
                    if ki == nk - 1:
                        rs_t, B_rs = rs[pcx % 2]
                        os_t, B_os = ost[pcx % 2]
                        O2 = PFall[:, 2048:3072].rearrange("p (u n) -> p u n", n=512)
                        K.op("act", lambda e: e.activation(out=rs_t[64:128, :, 0:nb], in_=O2[64:128, :, 0:nb], func=AF.Ln),
                             [pOs[0][1], pOs[1][1]], [B_rs])
                        K.op("act", lambda e: e.activation(out=rs_t[64:128, :, 0:nb], in_=rs_t[64:128, :, 0:nb], func=AF.Exp,
                                                           scale=-1.0), [B_rs], [B_rs])
                        K.op("dve", lambda e: e.tensor_tensor(out=os_t[0:64, :, 0:nb], in0=O2[0:64, :, 0:nb],
                                                              in1=rs_t[64:128, :, 0:nb], op=ALU.mult),
                             [pOs[0][1], pOs[1][1], B_rs], [B_os])
                        for u in range(2):
                            hglob = 2 * c + u
                            K.dma("sp", agT_d[hglob * 64:(hglob + 1) * 64, t0:t0 + nb], os_t[0:64, u, 0:nb], B_agT, src=B_os)
                K.barrier()

        w8 = ExitStack()
        wbr = []
        for nm, wsrc in (("wfo", w_fo), ("wmo", w_mo), ("wgo", w_go)):
            t, B = sb(w8, [128, 4, D], BF16, nm)
            K.dma("pool", t[:], wview(wsrc[l]), B)
            wbr.append((t, B))
        wo, B_wo = sb(w8, [128, 8, D], BF16, "wo")
        K.dma("pool", wo[:], wview(w_o[l]), B_wo)
        attention_gqa()

        def residual_ln(tt, halves, gate_t, B_gate, lg, B_lg, lb, B_lb, stats, tmps, idx):
            for hf, (pp, B_pp) in enumerate(halves):
                tm, B_t = tmps[(2 * idx + hf) % len(tmps)]
                K.op("dve", lambda e: e.tensor_tensor(
                    out=tm[:], in0=pp, in1=gate_t[:, hf * 512:(hf + 1) * 512], op=ALU.mult), [B_pp, B_gate], [B_t])
                yield
                K.op("dve", lambda e: e.scalar_tensor_tensor(
                    out=X[:, tt, hf * 512:(hf + 1) * 512], in0=X[:, tt, hf * 512:(hf + 1) * 512], scalar=ALPHA,
                    in1=tm[:], op0=ALU.mult, op1=ALU.add), [B_t, XB[tt]], [XB[tt]])
                yield
            yield from ln_stats(stats[idx % len(stats)], tt)
            rn, B_rn = stats[idx % len(stats)][2]
            K.op("act", lambda e: e.activation(out=X[:, tt, :], in_=X[:, tt, :], func=AF.Identity,
                                               bias=rn[:, 1:2], scale=rn[:, 0:1]), [XB[tt], B_rn], [XB[tt]])
            yield
            K.op("dve", lambda e: e.tensor_tensor(out=X[:, tt, :], in0=X[:, tt, :], in1=lg[:], op=ALU.mult),
                 [XB[tt], B_lg], [XB[tt]])
            yield
            K.op("pool", lambda e: e.tensor_tensor(out=X[:, tt, :], in0=X[:, tt, :], in1=lb[:], op=ALU.add),
                 [XB[tt], B_lb], [XB[tt]])
            yield

        with ExitStack() as ps:
            lg, B_lg = bcast_row(ps, ln1_g[l:l + 1, :], D)
            lb, B_lb = bcast_row(ps, ln1_b[l:l + 1, :], D)
            g1 = []
            for r in range(2):
                g1.append(bcast_row(ps, mod_d[l, r:r + 1, 2048:3072], D, reads=[B_mod], pfx="g1"))
            brin = [sb(ps, [128, 4, 512], BF16, "brin") for _ in range(3)]
            gin = [sb(ps, [128, 8, 512], BF16, "gin") for _ in range(2)]
            mg, B_mg = sb(ps, [128, 8, 512], F32, "mg")
            mtmp = [sb(ps, [128, 512], F32, "mtmp") for _ in range(2)]
            mbf, B_mbf = sb(ps, [128, 8, 512], BF16, "mbf")
            stats = [[sb(ps, [128, 12], F32, "st"), sb(ps, [128, 2], F32, "mv"), sb(ps, [128, 2], F32, "rn")]
                     for _ in range(2)]
            rtmp = [sb(ps, [128, 512], F32, "rtmp") for _ in range(4)]
            srcs = [(fmT_d, B_fmT), (amT_d, B_amT), (agT_d, B_agT)]
            ic = 0
            igin = 0
            itile = 0
            for bi in blocks_q:
                t0, nb = TB[bi]
                for br in range(3):
                    bt, B_bt = brin[br]
                    K.dma("sp", bt[:, :, 0:nb], srcs[br][0].rearrange("(c p) t -> p c t", p=128)[:, :, t0:t0 + nb],
                          B_bt, [srcs[br][1]])
                    gt, B_gt = gin[igin % 2]
                    igin += 1
                    K.dma("sp", gt[:, :, 0:nb],
                          gT_d[br * 1024:(br + 1) * 1024, :].rearrange("(c p) t -> p c t", p=128)[:, :, t0:t0 + nb],
                          B_gt, [B_gT])
                    wt, B_w = wbr[br]
                    for oc in range(8):
                        pp, B_pp = PF[ic % 2]
                        ic += 1

                        def mmb(e, pp=pp, wt=wt, bt=bt, oc=oc, nb=nb):
                            for kc in range(4):
                                ins = e.matmul(pp[:, 0:nb], wt[:, kc, oc * 128:(oc + 1) * 128], bt[:, kc, 0:nb],
                                               start=(kc == 0), stop=(kc == 3))
                            return ins
                        K.op("pe", mmb, [B_bt, B_w], [B_pp])
                        if br == 0:
                            K.op("dve", lambda e, pp=pp, gt=gt, oc=oc, nb=nb: e.tensor_tensor(
                                out=mg[:, oc, 0:nb], in0=pp[:, 0:nb], in1=gt[:, oc, 0:nb], op=ALU.mult),
                                [B_pp, B_gt], [B_mg])
                        else:
                            tm, B_t = mtmp[ic % 2]
                            K.op("dve", lambda e, tm=tm, pp=pp, gt=gt, oc=oc, nb=nb: e.tensor_tensor(
                                out=tm[:, 0:nb], in0=pp[:, 0:nb], in1=gt[:, oc, 0:nb], op=ALU.mult),
                                [B_pp, B_gt], [B_t])
                            if br == 1:
                                K.op("pool", lambda e, tm=tm, oc=oc, nb=nb: e.tensor_tensor(
                                    out=mg[:, oc, 0:nb], in0=mg[:, oc, 0:nb], in1=tm[:, 0:nb], op=ALU.add),
                                    [B_t, B_mg], [B_mg])
                            else:
                                K.op("pool", lambda e, tm=tm, oc=oc, nb=nb: e.tensor_tensor(
                                    out=mbf[:, oc, 0:nb], in0=mg[:, oc, 0:nb], in1=tm[:, 0:nb], op=ALU.add),
                                    [B_t, B_mg], [B_mbf])
                def p8_tile(itile, ti, tt):
                    halves = []
                    for hf in range(2):
                        pp, B_pp = PF[2 + hf + 2 * (itile % 2)]

                        def mmo(e, pp=pp, hf=hf):
                            for kc in range(8):
                                ins = e.matmul(pp[:, :], mbf[:, kc, ti * 128:(ti + 1) * 128], wo[:, kc, hf * 512:(hf + 1) * 512],
                                               start=(kc == 0), stop=(kc == 7))
                            return ins
                        K.op("pe", mmo, [B_mbf, B_wo], [B_pp])
                        yield
                        halves.append((pp[:, :], B_pp))
                    gate_t, B_gate = g1[1 if tt < 2 else 0]
                    yield from residual_ln(tt, halves, gate_t, B_gate, lg, B_lg, lb, B_lb, stats, rtmp, itile)
                gl = []
                for ti in range(nb // 128):
                    gl.append(p8_tile(itile, ti, t0 // 128 + ti))
                    itile += 1
                run_streams(gl, 2)
            K.barrier()
        w8.close()

        w11 = ExitStack()
        w2s, B_w2 = sb(w11, [128, 32, D], BF16, "w2s")
        w1_es = ExitStack()
        wps = [sb(w1_es, [128, 8, 512], BF16, "w1p") for _ in range(2)]
        wv1 = wview(w1[l])
        wv2 = wview(w2[l])
        for pj in range(2):
            K.dma("pool", wps[pj][0][:], wv1[:, :, pj * 512:(pj + 1) * 512], wps[pj][1])
        for q4i in range(4):
            K.dma("pool", w2s[:, q4i * 8:(q4i + 1) * 8, :], wv2[:, q4i * 8:(q4i + 1) * 8, :], B_w2)
        with ExitStack() as hs:
            hT, _ = sb(hs, [128, 8, T], BF16, "h2T")
            HB = [Buf("h2T%d" % i) for i in range(5)]
            ln_mod(l, 2, hT, HB, last)
            with ExitStack() as ps:
                rl = [sb(ps, [128, 512], F32, "rl") for _ in range(2)]
                ast = [sb(ps, [128, 512], BF16, "ast") for _ in range(3)]
                ip = 0
                for pj in range(8):
                    wt, B_w = wps[pj % 2]
                    if pj >= 2:
                        K.dma("pool", wt[:], wv1[:, :, pj * 512:(pj + 1) * 512], B_w)
                    for bi in blocks_q:
                        t0, nb = TB[bi]
                        for c4 in range(4):
                            ch = pj * 4 + c4
                            pp, B_pp = PF[ip % 4]
                            r_t, B_r = rl[ip % 2]
                            a_t, B_a = ast[ip % 3]
                            ip += 1

                            def mm1(e, wt=wt, pp=pp, c4=c4, t0=t0, nb=nb):
                                for kc in range(8):
                                    ins = e.matmul(pp[:, 0:nb], wt[:, kc, c4 * 128:(c4 + 1) * 128], hT[:, kc, t0:t0 + nb],
                                                   start=(kc == 0), stop=(kc == 7))
                                return ins
                            K.op("pe", mm1, [HB[bi], B_w], [B_pp])
                            K.op("act", lambda e, r_t=r_t, pp=pp, nb=nb: e.activation(out=r_t[:, 0:nb], in_=pp[:, 0:nb],
                                                                                     func=AF.Relu), [B_pp], [B_r])
                            K.op("dve", lambda e, r_t=r_t, a_t=a_t, nb=nb: e.tensor_tensor(
                                out=a_t[:, 0:nb], in0=r_t[:, 0:nb], in1=r_t[:, 0:nb], op=ALU.mult), [B_r], [B_a])
                            K.dma("sp", aT_d[ch * 128:(ch + 1) * 128, t0:t0 + nb], a_t[:, 0:nb], B_aT, src=B_a)
                K.barrier()

        w1_es.close()
        with ExitStack() as ps:
            lg, B_lg = bcast_row(ps, ln2_g[l:l + 1, :], D)
            lb, B_lb = bcast_row(ps, ln2_b[l:l + 1, :], D)
            g2 = []
            for r in range(2):
                g2.append(bcast_row(ps, mod_d[l, r:r + 1, 5120:6144], D, reads=[B_mod], pfx="g2"))
            ain = [sb(ps, [128, 32, 256], BF16, "ain") for _ in range(2)]
            stats = [[sb(ps, [128, 12], F32, "st"), sb(ps, [128, 2], F32, "mv"), sb(ps, [128, 2], F32, "rn")]
                     for _ in range(2)]
            rtmp = [sb(ps, [128, 512], F32, "rtmp") for _ in range(4)]
            aloaded = {}
            t0s = list(range(TC if last else 0, T, 256))

            def ensure_a(ia):
                if ia >= len(t0s) or ia in aloaded:
                    return
                at, B_at = ain[ia % 2]
                K.dma("sp", at[:], aT_d.rearrange("(c p) t -> p c t", p=128)[:, :, t0s[ia]:t0s[ia] + 256], B_at, [B_aT])
                aloaded[ia] = True

            def p11_tile(itile, ia, ti):
                ensure_a(ia)
                at, B_at = ain[ia % 2]
                tt = t0s[ia] // 128 + ti
                halves = []
                for hf in range(2):
                    pp, B_pp = PF[hf + 2 * (itile % 2)]

                    def mm2(e, pp=pp, hf=hf):
                        for kc in range(32):
                            ins = e.matmul(pp[:, :], at[:, kc, ti * 128:(ti + 1) * 128], w2s[:, kc, hf * 512:(hf + 1) * 512],
                                           start=(kc == 0), stop=(kc == 31))
                        return ins
                    K.op("pe", mm2, [B_at, B_w2], [B_pp])
                    yield
                    halves.append((pp[:, :], B_pp))
                if ti == 1:
                    ensure_a(ia + 1)
                gate_t, B_gate = g2[1 if tt < 2 else 0]
                yield from residual_ln(tt, halves, gate_t, B_gate, lg, B_lg, lb, B_lb, stats, rtmp, itile)
            gl = []
            itile = 0
            for ia in range(len(t0s)):
                for ti in range(2):
                    gl.append(p11_tile(itile, ia, ti))
                    itile += 1
            run_streams(gl, 2)
            K.barrier()
        w11.close()

    for tt in range(2, NTT):
        K.dma("sp", y_d[(tt - 2) * 128:(tt - 1) * 128, :], X[:, tt, :], B_y, src=XB[tt])
    K._wait("sp", B_y.w)
    K.barrier()


def _consts():
    c = {}
    c["ident"] = np.eye(128, dtype=np.float32)
    k = np.arange(128)
    ang = 2.0 * np.pi * ((k[:, None] * k[None, :]) % 128) / 128.0
    c["cs128"] = (np.concatenate([np.cos(ang), np.sin(ang)], axis=1) / np.sqrt(128.0)).astype(ml_dtypes.bfloat16)
    for S, suf in ((TX, ""), (TC, "256")):
        t = np.arange(S)
        a = 2.0 * np.pi * ((t[:, None] * t[None, :]) % S) / S
        cc_ = (np.cos(a) / np.sqrt(S)).astype(ml_dtypes.bfloat16)
        ss_ = (-np.sin(a) / np.sqrt(S)).astype(ml_dtypes.bfloat16)
        if S == TX:
            cc_ = np.ascontiguousarray(cc_.reshape(16, 128, 4, 512).transpose(2, 1, 0, 3))
            ss_ = np.ascontiguousarray(ss_.reshape(16, 128, 4, 512).transpose(2, 1, 0, 3))
        c["dftc" + suf] = cc_
        c["dfts" + suf] = ss_
    rows = (np.arange(TX) // 64).astype(np.float32)
    cols = (np.arange(TX) % 64).astype(np.float32)

    def angles(d_rot):
        n = d_rot // 4
        freqs = (np.float32(10000.0) ** (-np.arange(n, dtype=np.float32) / np.float32(n))).astype(np.float32)
        return np.concatenate([rows[:, None] * freqs, cols[:, None] * freqs], axis=-1).astype(np.float32)
    am = angles(32)
    ropeM = np.zeros((128, 2, T), np.float32)
    ropeM[:, 0, :TC] = 1.0
    j = np.arange(128) % 32 % 16
    ropeM[:, 0, TC:] = np.cos(am).astype(np.float32)[:, j].T
    ropeM[:, 1, TC:] = np.sin(am).astype(np.float32)[:, j].T
    c["ropeM"] = ropeM
    ag = angles(64)
    c["ropeG"] = np.stack([np.cos(ag), np.sin(ag)], axis=1).astype(np.float32)
    return c


_CACHE = {}


def _prep(inputs):
    f = lambda a: np.ascontiguousarray(np.asarray(a, dtype=np.float32))
    w_in = f(inputs["w_in"])
    tm_cols = np.concatenate([np.arange(0, 256), np.arange(288, 416), np.arange(416, 544),
                              np.arange(1056, 1312), np.arange(1312, 1824)])
    fm_cols = np.concatenate([np.arange(256, 288), np.arange(544, 1056), np.arange(1824, 4896)])
    sh = {
        "w_ada": f(inputs["w_ada"]), "b_ada": f(inputs["b_ada"]),
        "b_adaT": np.ascontiguousarray(f(inputs["b_ada"]).reshape(NL, 48, 128).transpose(0, 2, 1)),
        "w_tm": np.ascontiguousarray(w_in[:, :, tm_cols]), "w_fm": np.ascontiguousarray(w_in[:, :, fm_cols]),
        "b_gateT": np.ascontiguousarray(f(inputs["b_gate"]).reshape(NL, 24, 128).transpose(0, 2, 1)),
        "g_mq": f(inputs["mla_q_g"]), "g_mkv": f(inputs["mla_kv_g"]),
        "g_gq": np.ascontiguousarray(np.tile(f(inputs["gqa_q_g"]), (1, 8))),
        "g_gk": np.ascontiguousarray(np.tile(f(inputs["gqa_k_g"]), (1, 2))),
        "w_uq": f(inputs["w_uq"]), "w_uk": f(inputs["w_uk"]), "w_uv": f(inputs["w_uv"]),
        "w_fo": f(inputs["w_fo"]), "w_mo": f(inputs["w_mo"]), "w_go": f(inputs["w_go"]), "w_o": f(inputs["w_o"]),
        "ln1_g": f(inputs["ln1_g"]), "ln1_b": f(inputs["ln1_b"]), "ln2_g": f(inputs["ln2_g"]), "ln2_b": f(inputs["ln2_b"]),
        "w1": f(inputs["w1"]), "w2": f(inputs["w2"]),
    }
    sh.update(_consts())
    x = f(inputs["x"])
    ctx = f(inputs["ctx"])
    c = f(inputs["c"])
    c_ctx = f(inputs["c_ctx"])
    maps = []
    for b in range(8):
        m = dict(sh)
        m["xin"] = np.ascontiguousarray(np.concatenate([ctx[b], x[b]], axis=0))
        cc = np.stack([c[b], c_ctx], axis=0)
        m["ccT"] = np.ascontiguousarray(cc.reshape(2, 8, 128).transpose(2, 1, 0).reshape(128, 16))
        maps.append(m)
    return maps


def kernel(**inputs):
    maps = _prep(inputs)
    if "nc" not in _CACHE:
        _CACHE["nc"] = build(NL, False)
    res = run_bass_kernel_spmd(_CACHE["nc"], maps, core_ids=list(range(8)))
    return np.stack([np.asarray(r["y"], dtype=np.float32) for r in res.results], axis=0)
```
